# Optimizing a Trainium2 kernel written in Bass

```python
import math
import jax, jax.numpy as jnp
from jax import lax
import numpy as np

D_MODEL = 1024
BATCH = 8
SEQ = 2048
DEPTH = 2
DEC_BATCH = 128
DEC_SEQ = 4
PAST_LEN = 16384
PAGE_SIZE = 128

N_META = 16
N_A = (DEPTH + 1) // 2
N_B = DEPTH // 2
NORM_EPS = 1e-5
RW_HEAD = 64
RW_HEADS = D_MODEL // RW_HEAD
DECAY_LORA = 64
AAA_LORA = 64
GATE_LORA = 160
RW_LN_EPS = 64e-5
M_EXPAND = 2
M_D_INNER = M_EXPAND * D_MODEL
M_HEADDIM = 64
M_HEADS = M_D_INNER // M_HEADDIM
M_GROUPS = 4
M_HPG = M_HEADS // M_GROUPS
M_STATE = 128
M_CONV = 4
M_CONV_DIM = M_D_INNER + 2 * M_GROUPS * M_STATE
M_IN_DIM = 2 * M_D_INNER + 2 * M_GROUPS * M_STATE + M_HEADS
M_CHUNK = 128
P_HEADS = 8
P_NKEYS = 128
P_EXPERTS = P_NKEYS ** 2
P_QDIM = 256
P_TOPK = 16
P_BLOCK = 128

kernel_name = 'hybrid_rwkv7_mamba2_peer_step'

F32 = jnp.float32


def rms_norm(x, g):
    xf = x.astype(F32)
    y = xf * lax.rsqrt(jnp.mean(xf * xf, axis=-1, keepdims=True) + NORM_EPS)
    return (y * g.astype(F32)).astype(x.dtype)


def rwkv7_time_mix(xn, shift0, wkv0, mix, w_rkv, w0, w1, w2, a0, a1, a2, g1, g2,
                   k_k, k_a, r_k, ln_w, ln_b, w_o):
    bsz, L, D = xn.shape
    H, N = RW_HEADS, RW_HEAD
    prev = jnp.concatenate([shift0[:, None, :].astype(xn.dtype), xn[:, :-1]], axis=1)
    dx = prev - xn
    xr, xw, xk, xv, xa, xg = [xn + dx * mix[j] for j in range(6)]
    r = xr @ w_rkv[0]
    k = xk @ w_rkv[1]
    v = xv @ w_rkv[2]
    w_log = -jax.nn.softplus(-(w0 + jnp.tanh(xw @ w1) @ w2)) - 0.5
    decay = jnp.exp(-jnp.exp(w_log.astype(F32)))
    a = jax.nn.sigmoid(a0 + (xa @ a1) @ a2)
    g = jax.nn.sigmoid(xg @ g1) @ g2
    kk = (k * k_k).reshape(bsz, L, H, N).astype(F32)
    kk = kk / jnp.maximum(jnp.linalg.norm(kk, axis=-1, keepdims=True), 1e-12)
    k = k * (1 + (a - 1) * k_a)

    def heads(t):
        return t.reshape(bsz, L, H, N).astype(F32)

    r_h, k_h, v_h, a_h, w_h = heads(r), heads(k), heads(v), heads(a), heads(decay)

    def step(S, inp):
        r_t, w_t, k_t, v_t, kk_t, a_t = inp
        s_kk = jnp.einsum('bhvk,bhk->bhv', S, -kk_t)
        S = (S * w_t[:, :, None, :]
             + s_kk[..., None] * (kk_t * a_t)[:, :, None, :]
             + v_t[..., None] * k_t[:, :, None, :])
        return S, jnp.einsum('bhvk,bhk->bhv', S, r_t)

    seq = tuple(jnp.swapaxes(t, 0, 1) for t in (r_h, w_h, k_h, v_h, kk, a_h))
    S_fin, o = lax.scan(step, wkv0.astype(F32), seq)
    o = jnp.swapaxes(o, 0, 1)
    mu = jnp.mean(o, axis=-1, keepdims=True)
    var = jnp.mean(jnp.square(o - mu), axis=-1, keepdims=True)
    o = ((o - mu) * lax.rsqrt(var + RW_LN_EPS)).reshape(bsz, L, D) * ln_w + ln_b
    bonus = jnp.sum(r_h * k_h * r_k, axis=-1, keepdims=True) * v_h
    o = (o + bonus.reshape(bsz, L, D)).astype(xn.dtype)
    out = (o * g) @ w_o
    return out, xn[:, -1], S_fin.astype(xn.dtype)


def segsum(x):
    T = x.shape[-1]
    xe = jnp.broadcast_to(x[..., :, None], x.shape + (T,))
    low = jnp.tril(jnp.ones((T, T), dtype=bool), -1)
    cs = jnp.cumsum(jnp.where(low, xe, 0.0), axis=-2)
    return jnp.where(jnp.tril(jnp.ones((T, T), dtype=bool)), cs, -jnp.inf)


def ssd(x, dt, A, Bm, Cm, h0, lead_pad, chunk):
    b, L = x.shape[:2]
    padt = lambda t: jnp.pad(t, [(0, 0), (lead_pad, 0)] + [(0, 0)] * (t.ndim - 2))
    x, dt, Bm, Cm = padt(x), padt(dt), padt(Bm), padt(Cm)
    Lp = L + lead_pad
    nc = Lp // chunk
    x = x.reshape(b, nc, chunk, M_GROUPS, M_HPG, M_HEADDIM)
    dt = dt.reshape(b, nc, chunk, M_GROUPS, M_HPG)
    Bm = Bm.reshape(b, nc, chunk, M_GROUPS, M_STATE)
    Cm = Cm.reshape(b, nc, chunk, M_GROUPS, M_STATE)
    xdt = x * dt[..., None]
    adt = jnp.moveaxis(dt * A, 2, -1)
    a_cs = jnp.cumsum(adt, axis=-1)
    Lmat = jnp.exp(segsum(adt))
    cb = jnp.einsum('bclgn,bcsgn->bcgls', Cm, Bm)
    y_diag = jnp.einsum('bcgls,bcgrls,bcsgrp->bclgrp', cb, Lmat, xdt)
    decay_st = jnp.exp(a_cs[..., -1:] - a_cs)
    st = jnp.einsum('bclgn,bcgrl,bclgrp->bcgrpn', Bm, decay_st, xdt)
    chunk_decay = jnp.exp(a_cs[..., -1])

    def step(h, inp):
        s_c, d_c = inp
        return h * d_c[..., None, None] + s_c, h

    h_fin, h_in = lax.scan(step, h0, (jnp.moveaxis(st, 1, 0), jnp.moveaxis(chunk_decay, 1, 0)))
    h_in = jnp.moveaxis(h_in, 0, 1)
    y_off = jnp.einsum('bclgn,bcgrpn,bcgrl->bclgrp', Cm, h_in, jnp.exp(a_cs))
    y = (y_diag + y_off).reshape(b, Lp, M_GROUPS, M_HPG, M_HEADDIM)[:, lead_pad:]
    return y, h_fin


def mamba2_mix(xn, conv0, ssm0, in_proj, conv_w, conv_b, dt_bias, a_log, d_skip, norm_w, out_proj,
               lead_pad, chunk):
    bsz, L, _ = xn.shape
    zxbcdt = xn @ in_proj
    z = zxbcdt[..., :M_D_INNER]
    xbc = zxbcdt[..., M_D_INNER:M_D_INNER + M_CONV_DIM]
    dt_raw = zxbcdt[..., M_D_INNER + M_CONV_DIM:]
    full = jnp.concatenate([conv0.astype(xn.dtype), xbc], axis=1)
    conv = conv_b + sum(full[:, j:j + L] * conv_w[j] for j in range(M_CONV))
    new_conv = full[:, L:]
    xbc = jax.nn.silu(conv).astype(F32)
    xs = xbc[..., :M_D_INNER].reshape(bsz, L, M_GROUPS, M_HPG, M_HEADDIM)
    Bm = xbc[..., M_D_INNER:M_D_INNER + M_GROUPS * M_STATE].reshape(bsz, L, M_GROUPS, M_STATE)
    Cm = xbc[..., M_D_INNER + M_GROUPS * M_STATE:].reshape(bsz, L, M_GROUPS, M_STATE)
    dt = jax.nn.softplus((dt_raw + dt_bias).astype(F32)).reshape(bsz, L, M_GROUPS, M_HPG)
    A = -jnp.exp(a_log.astype(F32)).reshape(M_GROUPS, M_HPG)
    h0 = ssm0.astype(F32).reshape(bsz, M_GROUPS, M_HPG, M_HEADDIM, M_STATE)
    y, h_fin = ssd(xs, dt, A, Bm, Cm, h0, lead_pad, chunk)
    y = y + d_skip.astype(F32).reshape(M_GROUPS, M_HPG)[..., None] * xs
    y = y.reshape(bsz, L, M_D_INNER)
    yg = (y * jax.nn.silu(z.astype(F32))).reshape(bsz, L, M_GROUPS, M_D_INNER // M_GROUPS)
    yg = yg * lax.rsqrt(jnp.mean(yg * yg, axis=-1, keepdims=True) + NORM_EPS)
    yg = (yg.reshape(bsz, L, M_D_INNER) * norm_w).astype(xn.dtype)
    out = yg @ out_proj
    ssm_new = h_fin.reshape(bsz, M_HEADS, M_HEADDIM, M_STATE).astype(xn.dtype)
    return out, new_conv, ssm_new


def peer_ffn(xn, w_q, sub_keys, u_tab, v_tab):
    bsz, L, D = xn.shape
    T = bsz * L
    xt = xn.reshape(T, D)
    q = (xt @ w_q).reshape(T, P_HEADS, 2, P_QDIM // 2)
    s = jnp.einsum('thzd,zhkd->thzk', q, sub_keys).astype(F32)
    sv, si = lax.top_k(s, P_TOPK)
    cand = sv[:, :, 0, :, None] + sv[:, :, 1, None, :]
    cand_idx = si[:, :, 0, :, None] * P_NKEYS + si[:, :, 1, None, :]
    top_s, pos = lax.top_k(cand.reshape(T, P_HEADS, P_TOPK * P_TOPK), P_TOPK)
    eidx = jnp.take_along_axis(cand_idx.reshape(T, P_HEADS, P_TOPK * P_TOPK), pos, axis=-1)
    gate = jax.nn.softmax(top_s, axis=-1).astype(xn.dtype)
    eidx = eidx.reshape(T, P_HEADS * P_TOPK)
    gate = gate.reshape(T, P_HEADS * P_TOPK)
    pad = (-T) % P_BLOCK
    nb = (T + pad) // P_BLOCK
    xb = jnp.pad(xt, ((0, pad), (0, 0))).reshape(nb, P_BLOCK, D)
    ib = jnp.pad(eidx, ((0, pad), (0, 0))).reshape(nb, P_BLOCK, P_HEADS * P_TOPK)
    gb = jnp.pad(gate, ((0, pad), (0, 0))).reshape(nb, P_BLOCK, P_HEADS * P_TOPK)

    def block(args):
        x_blk, i_blk, g_blk = args
        u = u_tab[i_blk]
        v = v_tab[i_blk]
        act = jax.nn.gelu(jnp.einsum('td,ted->te', x_blk, u), approximate=False)
        return jnp.einsum('te,ted->td', g_blk * act, v)

    out = lax.map(block, (xb, ib, gb)).reshape(nb * P_BLOCK, D)[:T]
    return out.reshape(bsz, L, D)


def setup_inputs(seed: int = 0) -> dict:
    key = jax.random.key(seed)
    ks = iter(jax.random.split(key, 64))
    D = D_MODEL

    def nrm(shape, scale=1.0):
        return jax.random.normal(next(ks), shape, F32) * scale

    def unif(shape, lo, hi):
        return jax.random.uniform(next(ks), shape, F32, lo, hi)

    dt0 = jnp.exp(unif((N_B, M_HEADS), math.log(1e-3), math.log(1e-1)))
    return {
        'x_prompt': nrm((BATCH, SEQ, D)),
        'x_sample': nrm((DEC_BATCH, DEC_SEQ, D)),
        'state_rwkv_shift': nrm((N_A, DEC_BATCH, D)),
        'state_rwkv_wkv': nrm((N_A, DEC_BATCH, RW_HEADS, RW_HEAD, RW_HEAD), 0.3),
        'state_mamba_conv': nrm((N_B, DEC_BATCH, M_CONV - 1, M_CONV_DIM)),
        'state_mamba_ssm': nrm((N_B, DEC_BATCH, M_HEADS, M_HEADDIM, M_STATE), 0.1),
        'meta_tokens': nrm((N_META, D)),
        'norm_mix': 1.0 + nrm((DEPTH, D), 0.05),
        'norm_ffn': 1.0 + nrm((DEPTH, D), 0.05),
        'norm_final': 1.0 + nrm((D,), 0.05),
        'rwkv_mix': unif((N_A, 6, D), 0.0, 1.0),
        'rwkv_w_rkv': nrm((N_A, 3, D, D), D ** -0.5),
        'rwkv_w0': unif((N_A, D), -5.0, 1.0),
        'rwkv_w1': nrm((N_A, D, DECAY_LORA), D ** -0.5),
        'rwkv_w2': nrm((N_A, DECAY_LORA, D), 0.1 * DECAY_LORA ** -0.5),
        'rwkv_a0': nrm((N_A, D), 0.1),
        'rwkv_a1': nrm((N_A, D, AAA_LORA), D ** -0.5),
        'rwkv_a2': nrm((N_A, AAA_LORA, D), 0.1 * AAA_LORA ** -0.5),
        'rwkv_g1': nrm((N_A, D, GATE_LORA), D ** -0.5),
        'rwkv_g2': nrm((N_A, GATE_LORA, D), GATE_LORA ** -0.5),
        'rwkv_k_k': 0.85 + nrm((N_A, D), 0.05),
        'rwkv_k_a': 1.0 + nrm((N_A, D), 0.05),
        'rwkv_r_k': nrm((N_A, RW_HEADS, RW_HEAD), 0.1),
        'rwkv_ln_w': 1.0 + nrm((N_A, D), 0.05),
        'rwkv_ln_b': nrm((N_A, D), 0.02),
        'rwkv_w_o': nrm((N_A, D, D), D ** -0.5),
        'mamba_in_proj': nrm((N_B, D, M_IN_DIM), D ** -0.5),
        'mamba_conv_w': nrm((N_B, M_CONV, M_CONV_DIM), M_CONV ** -0.5),
        'mamba_conv_b': nrm((N_B, M_CONV_DIM), 0.02),
        'mamba_dt_bias': dt0 + jnp.log(-jnp.expm1(-dt0)),
        'mamba_a_log': jnp.log(unif((N_B, M_HEADS), 1.0, 16.0)),
        'mamba_d': 1.0 + nrm((N_B, M_HEADS), 0.05),
        'mamba_norm_w': 1.0 + nrm((N_B, M_D_INNER), 0.05),
        'mamba_out_proj': nrm((N_B, M_D_INNER, D), M_D_INNER ** -0.5),
        'peer_w_q': nrm((DEPTH, D, P_HEADS * P_QDIM), D ** -0.5),
        'peer_sub_keys': nrm((DEPTH, 2, P_HEADS, P_NKEYS, P_QDIM // 2), (P_QDIM // 2) ** -0.5),
        'peer_u': nrm((DEPTH, P_EXPERTS, D), D ** -0.5),
        'peer_v': nrm((DEPTH, P_EXPERTS, D), (P_HEADS * P_TOPK) ** -0.5),
    }


def reference(x_prompt, x_sample, state_rwkv_shift, state_rwkv_wkv, state_mamba_conv, state_mamba_ssm,
              meta_tokens, norm_mix, norm_ffn, norm_final,
              rwkv_mix, rwkv_w_rkv, rwkv_w0, rwkv_w1, rwkv_w2, rwkv_a0, rwkv_a1, rwkv_a2,
              rwkv_g1, rwkv_g2, rwkv_k_k, rwkv_k_a, rwkv_r_k, rwkv_ln_w, rwkv_ln_b, rwkv_w_o,
              mamba_in_proj, mamba_conv_w, mamba_conv_b, mamba_dt_bias, mamba_a_log, mamba_d,
              mamba_norm_w, mamba_out_proj,
              peer_w_q, peer_sub_keys, peer_u, peer_v):

    def trunk(h, shift0, wkv0, conv0, ssm0, lead_pad, chunk):
        shifts, wkvs, convs, ssms = [], [], [], []
        for i in range(DEPTH):
            xn = rms_norm(h, norm_mix[i])
            j = i // 2
            if i % 2 == 0:
                out, s_new, wkv_new = rwkv7_time_mix(
                    xn, shift0[j], wkv0[j], rwkv_mix[j], rwkv_w_rkv[j], rwkv_w0[j], rwkv_w1[j],
                    rwkv_w2[j], rwkv_a0[j], rwkv_a1[j], rwkv_a2[j], rwkv_g1[j], rwkv_g2[j],
                    rwkv_k_k[j], rwkv_k_a[j], rwkv_r_k[j], rwkv_ln_w[j], rwkv_ln_b[j], rwkv_w_o[j])
                shifts.append(s_new)
                wkvs.append(wkv_new)
            else:
                out, c_new, ssm_new = mamba2_mix(
                    xn, conv0[j], ssm0[j], mamba_in_proj[j], mamba_conv_w[j], mamba_conv_b[j],
                    mamba_dt_bias[j], mamba_a_log[j], mamba_d[j], mamba_norm_w[j], mamba_out_proj[j],
                    lead_pad, chunk)
                convs.append(c_new)
                ssms.append(ssm_new)
            h = h + out
            h = h + peer_ffn(rms_norm(h, norm_ffn[i]), peer_w_q[i], peer_sub_keys[i], peer_u[i], peer_v[i])
        return (rms_norm(h, norm_final), jnp.stack(shifts), jnp.stack(wkvs),
                jnp.stack(convs), jnp.stack(ssms))

    bp = x_prompt.shape[0]
    dtp = x_prompt.dtype
    meta = jnp.broadcast_to(meta_tokens.astype(dtp)[None], (bp, N_META, D_MODEL))
    hp = jnp.concatenate([meta, x_prompt], axis=1)
    z_shift = jnp.zeros((N_A, bp, D_MODEL), dtp)
    z_wkv = jnp.zeros((N_A, bp, RW_HEADS, RW_HEAD, RW_HEAD), dtp)
    z_conv = jnp.zeros((N_B, bp, M_CONV - 1, M_CONV_DIM), dtp)
    z_ssm = jnp.zeros((N_B, bp, M_HEADS, M_HEADDIM, M_STATE), dtp)
    yp, p_shift, p_wkv, p_conv, p_ssm = trunk(hp, z_shift, z_wkv, z_conv, z_ssm,
                                              M_CHUNK - N_META, M_CHUNK)
    ys, s_shift, s_wkv, s_conv, s_ssm = trunk(x_sample, state_rwkv_shift, state_rwkv_wkv,
                                              state_mamba_conv, state_mamba_ssm, 0, x_sample.shape[1])
    return (yp[:, N_META:], ys, p_shift, p_wkv, p_conv, p_ssm, s_shift, s_wkv, s_conv, s_ssm)
```

```python
from contextlib import ExitStack
import os
import numpy as np
import concourse.bass as bass
import concourse.mybir as mybir
from concourse.bass_utils import run_bass_kernel_spmd

F32 = mybir.dt.float32
BF16 = mybir.dt.bfloat16
U32 = mybir.dt.uint32
I32 = mybir.dt.int32
AF = mybir.ActivationFunctionType
ALU = mybir.AluOpType
AX = mybir.AxisListType

D = 1024
NCORES = 8
NEG = -1.0e30

ENGS = ['pe', 'dve', 'act', 'pool', 'sp']
EPOCH = 16384
NEPOCH = {'pe': 7, 'dve': 7, 'act': 4, 'pool': 4, 'sp': 1}
NDSEM = 72
SELFSYNC = os.environ.get("K_SELFSYNC", "1") == "1"


class Buf:
    __slots__ = ('name', 'lw', 'rc', 'rd', 'excl')

    def __init__(self, name):
        self.name = name
        self.excl = False
        self.lw = None
        self.rc = {}
        self.rd = {}


class Sched:
    def __init__(self, nc, es):
        self.nc = nc
        self.q = {e: [] for e in ENGS}
        self.cnt = {e: 0 for e in ENGS}
        self.known = {e: {f: 0 for f in ENGS} for e in ENGS}
        self.sems = {e: [es.enter_context(nc.semaphore(f"s_{e}_{i}")) for i in range(NEPOCH[e])] for e in ENGS}
        self.dsems = [es.enter_context(nc.semaphore(f"sd_{i}")) for i in range(NDSEM)]
        self.dcount = [0] * NDSEM
        self.dknown = {e: [0] * NDSEM for e in ENGS}
        self.dnext = 0
        self.n = 0
        self.limit = int(os.environ.get('K_LIMIT', '0'))
        self.phase = ''
        self.marks = []

    def mark(self, name):
        self.marks.append((self.n, name))

    def op(self, eng, fn, reads=(), writes=(), dma=False):
        self.n += 1
        if self.limit and self.n > self.limit:
            return None
        ex = [b for b in reads if b.excl]
        if ex:
            reads = [b for b in reads if not b.excl]
            writes = list(writes) + [b for b in ex if b not in writes]
        deps = []
        for b in list(reads) + list(writes):
            if b.lw is not None:
                deps.append(b.lw)
        for b in writes:
            for e2, seq in b.rc.items():
                deps.append(('c', e2, seq))
            for si, tgt in b.rd.items():
                deps.append(('d', si, tgt))
        waits = []
        for d in deps:
            if d[0] == 'c':
                _, e2, seq = d
                if e2 == eng and not dma and (eng == 'pe' or not SELFSYNC):
                    continue
                if self.known[eng][e2] >= seq:
                    continue
                self.known[eng][e2] = seq
                waits.append((self.sems[e2][(seq - 1) // EPOCH], (seq - 1) % EPOCH + 1))
            else:
                _, si, tgt = d
                if self.dknown[eng][si] >= tgt:
                    continue
                self.dknown[eng][si] = tgt
                waits.append((self.dsems[si], tgt))
        if dma:
            si = self.dnext
            self.dnext = (self.dnext + 1) % NDSEM
            prev = self.dcount[si]
            if prev > 0 and self.dknown[eng][si] < prev:
                self.dknown[eng][si] = prev
                waits.append((self.dsems[si], prev))
            self.dcount[si] = prev + 16
            tok = ('d', si, prev + 16)
            inc = (self.dsems[si], 16)
        else:
            self.cnt[eng] += 1
            seq = self.cnt[eng]
            assert seq <= EPOCH * NEPOCH[eng]
            tok = ('c', eng, seq)
            inc = (self.sems[eng][(seq - 1) // EPOCH], 1)
        for b in reads:
            if tok[0] == 'c':
                b.rc[tok[1]] = tok[2]
            else:
                b.rd[tok[1]] = tok[2]
        for b in writes:
            b.lw = tok
            b.rc = {}
            b.rd = {}
        self.q[eng].append((fn, waits, inc))
        return tok

    def barrier(self):
        for eng in ENGS:
            waits = []
            for si in range(NDSEM):
                if self.dcount[si] > self.dknown[eng][si]:
                    waits.append((self.dsems[si], self.dcount[si]))
                    self.dknown[eng][si] = self.dcount[si]
            for e in ENGS:
                seq = self.cnt[e]
                if seq > self.known[eng][e]:
                    waits.append((self.sems[e][(seq - 1) // EPOCH], (seq - 1) % EPOCH + 1))
                    self.known[eng][e] = seq
            if waits:
                self.q[eng].append((None, waits, None))

    def finish(self):
        waits = []
        for si in range(NDSEM):
            if self.dcount[si] > 0:
                waits.append((self.dsems[si], self.dcount[si]))
        for e in ENGS:
            if e != 'sp' and self.cnt[e] > 0:
                seq = self.cnt[e]
                waits.append((self.sems[e][(seq - 1) // EPOCH], (seq - 1) % EPOCH + 1))
        self.q['sp'].append((None, waits, None))

    def emit(self, block):
        def run(eng, name):
            for fn, waits, inc in self.q[name]:
                for sem, v in waits:
                    eng.wait_ge(sem, v)
                if fn is not None:
                    ins = fn(eng)
                    ins.then_inc(inc[0], inc[1])

        @block.tensor
        def _(e):
            run(e, 'pe')

        @block.vector
        def _(e):
            run(e, 'dve')

        @block.scalar
        def _(e):
            run(e, 'act')

        @block.gpsimd
        def _(e):
            run(e, 'pool')

        @block.sync
        def _(e):
            run(e, 'sp')


class _Scope:
    def __init__(self, cx):
        self.cx = cx

    def __enter__(self):
        self.old = self.cx.es
        self.es = ExitStack()
        self.cx.es = self.es
        return self

    def __exit__(self, *a):
        self.cx.S.barrier()
        self.es.close()
        self.cx.es = self.old
        return False


class T:
    def __init__(self, h, name):
        self.h = h
        self.b = Buf(name)

    def __getitem__(self, k):
        return self.h[k]


class Ctx:
    def __init__(self, nc, es):
        self.nc = nc
        self.es = es
        self.S = Sched(nc, es)
        self.dq = 0
        self.uid = 0

    def sb(self, name, shape, dt=F32):
        self.uid += 1
        name = f"{name}_{self.uid}"
        return T(self.es.enter_context(self.nc.sbuf_tensor(name, list(shape), dt)), name)

    def ps(self, name, shape, dt=F32):
        self.uid += 1
        name = f"{name}_{self.uid}"
        t = T(self.es.enter_context(self.nc.psum_tensor(name, list(shape), dt)), name)
        t.b.excl = True
        return t

    def scope(self):
        return _Scope(self)

    def dma(self, out, in_, reads=(), writes=(), eng=None, **kw):
        if eng is None:
            eng = ('sp', 'act')[self.dq % 2]
            self.dq += 1
        self.S.op(eng, lambda e: e.dma_start(out=out, in_=in_, **kw), reads=[r.b if isinstance(r, T) else r for r in reads],
                  writes=[w.b if isinstance(w, T) else w for w in writes], dma=True)

    def op(self, eng, fn, reads=(), writes=()):
        self.S.op(eng, fn, reads=[r.b if isinstance(r, T) else r for r in reads],
                  writes=[w.b if isinstance(w, T) else w for w in writes])

    def mm(self, out, lhsT, rhs, start, stop, reads, writes):
        self.op('pe', lambda e: e.matmul(out, lhsT, rhs, start=start, stop=stop), reads, writes)

    def tr(self, out, in_, ident, reads, writes):
        self.op('pe', lambda e: e.transpose(out, in_, ident), reads, writes)

    def act(self, out, in_, func, reads, writes, bias=None, scale=None, accum_out=None, eng='act'):
        kw = {}
        if bias is not None:
            kw['bias'] = bias
        if scale is not None:
            kw['scale'] = scale
        if accum_out is not None:
            kw['accum_out'] = accum_out
        self.op('act', lambda e: e.activation(out=out, in_=in_, func=func, **kw), reads, writes)

    def tt(self, out, in0, in1, op, reads, writes, eng='dve'):
        self.op(eng, lambda e: e.tensor_tensor(out=out, in0=in0, in1=in1, op=op), reads, writes)

    def ts(self, out, in0, s1, op0, reads, writes, s2=None, op1=None, eng='dve', accum_out=None):
        kw = {}
        if op1 is not None:
            kw['op1'] = op1
        if accum_out is not None:
            kw['accum_out'] = accum_out
        self.op(eng, lambda e: e.tensor_scalar(out=out, in0=in0, scalar1=s1, scalar2=s2, op0=op0, **kw), reads, writes)

    def stt(self, out, in0, scalar, in1, op0, op1, reads, writes, accum_out=None):
        kw = {}
        if accum_out is not None:
            kw['accum_out'] = accum_out
        self.op('dve', lambda e: e.scalar_tensor_tensor(out=out, in0=in0, scalar=scalar, in1=in1, op0=op0, op1=op1, **kw),
                reads, writes)

    def copy(self, out, in_, reads, writes, eng='dve'):
        if eng == 'act':
            self.op('act', lambda e: e.activation(out=out, in_=in_, func=AF.Copy), reads, writes)
        else:
            self.op(eng, lambda e: e.tensor_copy(out=out, in_=in_), reads, writes)

    def red(self, out, in_, op, reads, writes, axis=AX.X):
        self.op('dve', lambda e: e.tensor_reduce(out=out, in_=in_, axis=axis, op=op), reads, writes)

    def memset(self, ap, val, writes, eng='dve'):
        self.op(eng, lambda e: e.memset(ap, val), [], writes)


def make_consts(cx):
    nc = cx.nc
    c = {}
    it = cx.sb("c_iota", [128, 255], I32)
    cx.op('pool', lambda e: e.iota(it[:, 0:128], pattern=[[1, 128]], base=0, channel_multiplier=-1), [], [it])
    c['ident'] = cx.sb("c_ident", [128, 128], F32)
    c['identb'] = cx.sb("c_identb", [128, 128], BF16)
    c['triu'] = cx.sb("c_triu", [128, 128], F32)
    c['triu_s'] = cx.sb("c_trius", [128, 128], F32)
    c['tril_s'] = cx.sb("c_trils", [128, 128], F32)
    c['ones'] = cx.sb("c_ones", [128, 128], F32)
    cx.op('dve', lambda e: e.tensor_single_scalar(out=c['ident'][:], in_=it[:, 0:128], scalar=0, op=ALU.is_equal), [it], [c['ident']])
    cx.op('dve', lambda e: e.tensor_single_scalar(out=c['identb'][:], in_=it[:, 0:128], scalar=0, op=ALU.is_equal), [it], [c['identb']])
    cx.op('dve', lambda e: e.tensor_single_scalar(out=c['triu'][:], in_=it[:, 0:128], scalar=0, op=ALU.is_ge), [it], [c['triu']])
    cx.op('dve', lambda e: e.tensor_single_scalar(out=c['triu_s'][:], in_=it[:, 0:128], scalar=0, op=ALU.is_gt), [it], [c['triu_s']])
    cx.op('dve', lambda e: e.tensor_single_scalar(out=c['tril_s'][:], in_=it[:, 0:128], scalar=0, op=ALU.is_lt), [it], [c['tril_s']])
    cx.memset(c['ones'][:], 1.0, [c['ones']])
    it2 = cx.sb("c_iota2", [128, 255], I32)
    cx.op('pool', lambda e: e.iota(it2[:], pattern=[[1, 255]], base=-127, channel_multiplier=0), [], [it2])
    c['w1'] = cx.sb("c_w1", [128, 255], F32)
    cx.op('dve', lambda e: e.tensor_single_scalar(out=c['w1'][:], in_=it2[:], scalar=0, op=ALU.is_equal), [it2], [c['w1']])
    c['w1b'] = cx.sb("c_w1b", [128, 255], BF16)
    cx.op('dve', lambda e: e.tensor_single_scalar(out=c['w1b'][:], in_=it2[:], scalar=0, op=ALU.is_equal), [it2], [c['w1b']])
    c['iota16'] = cx.sb("c_iota16", [128, 16], F32)
    cx.op('dve', lambda e: e.tensor_single_scalar(out=c['iota16'][:], in_=it2[:, 127:143], scalar=0, op=ALU.add), [it2], [c['iota16']])
    c['eps'] = cx.sb("c_eps", [128, 1], F32)
    cx.memset(c['eps'][:], 1e-5, [c['eps']])
    return c


def rms_norm(cx, C, ht, gbc, xn, junk, ss, eps=1e-5, n=1024):
    cx.act(junk[0:C, 0:n], ht[0:C, 0:n], AF.Square, [ht], [junk, ss], accum_out=ss[0:C, 0:1])
    cx.ts(ss[0:C, 1:2], ss[0:C, 0:1], 1.0 / n, ALU.mult, [ss], [ss], s2=eps, op1=ALU.add)
    cx.act(ss[0:C, 1:2], ss[0:C, 1:2], AF.Sqrt, [ss], [ss])
    cx.op('dve', lambda e: e.reciprocal(out=ss[0:C, 1:2], in_=ss[0:C, 1:2]), [ss], [ss])
    cx.stt(xn[0:C, 0:n], ht[0:C, 0:n], ss[0:C, 1:2], gbc[0:C, 0:n], ALU.mult, ALU.mult, [ht, ss, gbc], [xn])


def peer_pass(cx, cst, W, layer, tiles, P, gfin=None, dbgp=None):
    nc = cx.nc
    l = layer
    wq = cx.sb(f"pw_wq{l}", [128, 8, 2048], F32)
    keysT = cx.sb(f"pw_keysT{l}", [128, 16, 128], F32)
    gbc = cx.sb(f"pw_g{l}", [128, 1024], F32)
    wqb = [Buf(f"wq{kc}") for kc in range(8)]
    for kc in range(8):
        cx.dma(wq[:, kc, :], W['peer_w_q'][l, kc * 128:(kc + 1) * 128, :], [], [wqb[kc]])
    cx.dma(gbc[:], W['norm_ffn'][l].partition_broadcast(128), [], [gbc])
    if gfin:
        gfb = cx.sb(f"pw_gf{l}", [128, 1024], F32)
        cx.dma(gfb[:], W['norm_final'].partition_broadcast(128), [], [gfb])
    pA, pB = P['a'], P['b']
    with cx.scope():
        kraw = cx.sb(f"pw_kraw{l}", [128, 16, 128], F32)
        cx.dma(kraw[:], W['peer_sub_keys'][l].rearrange("z h k d -> k (z h) d"), [], [kraw])
        for zh in range(16):
            z, h = zh // 8, zh % 8
            qb = h * 2 + z
            pp = (pA, pB)[zh % 2]
            cx.tr(pp[:, 0:128], kraw[:, zh, :], cst['ident'][:], [kraw, cst['ident']], [pp])
            cx.copy(keysT[:, qb, :], pp[:, 0:128], [pp], [keysT], eng=('dve', 'act')[zh % 2])

    ht2 = [cx.sb(f"pe_ht{l}_{i}", [128, 1024], F32) for i in range(2)]
    xnf = cx.sb(f"pe_xnf{l}", [128, 1024], F32)
    ssf = cx.sb(f"pe_ssf{l}", [128, 2], F32)
    xn = cx.sb(f"pe_xn{l}", [128, 1024], F32)
    xnb2 = [cx.sb(f"pe_xnb{l}_{i}", [128, 1024], BF16) for i in range(2)]
    junk = cx.sb(f"pe_junk{l}", [128, 1024], F32)
    ss = cx.sb(f"pe_ss{l}", [128, 2], F32)
    xnT = cx.sb(f"pe_xnT{l}", [128, 8, 128], F32)
    qT = cx.sb(f"pe_qT{l}", [128, 16, 128], F32)
    sc = cx.sb(f"pe_sc{l}", [128, 16, 128], F32)
    sc2 = cx.sb(f"pe_sc2{l}", [128, 16, 128], F32)
    cand = cx.sb(f"pe_cand{l}", [128, 8, 256], F32)
    oh = alias(sc, sc[:].rearrange("p a b -> p (a b)").rearrange("p (h x) -> p h x", h=8), f"pe_oh{l}")
    sv = cx.sb(f"pe_sv{l}", [128, 16, 16], F32)
    si = cx.sb(f"pe_si{l}", [128, 16, 16], U32)
    sif = cx.sb(f"pe_sif{l}", [128, 16, 16], F32)
    tv = cx.sb(f"pe_tv{l}", [128, 8, 16], F32)
    pos = cx.sb(f"pe_pos{l}", [128, 8, 16], U32)
    pij = cx.sb(f"pe_pij{l}", [128, 2, 128], U32)
    pijf = cx.sb(f"pe_pijf{l}", [128, 2, 128], F32)
    e01 = cx.sb(f"pe_e01{l}", [128, 2, 128], F32)
    eidx = cx.sb(f"pe_eidx{l}", [128, 128], F32)
    gate = cx.sb(f"pe_gate{l}", [128, 128], F32)
    zz = cx.sb(f"pe_zz{l}", [128, 16], F32)
    mx8 = cx.sb(f"pe_mx8{l}", [128, 8], F32)
    eidxT2 = [cx.sb(f"pe_eidxT{l}_{i}", [128, 128], U32) for i in range(2)]
    gateT2 = [cx.sb(f"pe_gateT{l}_{i}", [128, 128], F32) for i in range(2)]
    aT = cx.sb(f"pe_aT{l}", [128, 128], F32)
    coefT = cx.sb(f"pe_coefT{l}", [128, 128], F32)
    NBV, NBL = 8, 6
    uvb = [cx.sb(f"pe_uv{l}_{i}", [128, 2048], BF16) for i in range(NBV)]
    lb = [cx.sb(f"pe_lb{l}_{i}", [128, 128], BF16) for i in range(NBL)]
    acol = [cx.sb(f"pe_ac{l}_{i}", [128, 2], F32) for i in range(4)]
    junk2 = [cx.sb(f"pe_jk{l}_{i}", [128, 1024], BF16) for i in range(2)]
    xbc = [P['x0'], P['x1']]
    Y = P['y']
    ident, identb = cst['ident'], cst['identb']
    uvtab = W['tab_uv']
    eoff = l * 16384 * 2048

    def prep(ti, tl):
        C = tl['C']
        ht, xnb, eidxT, gateT = ht2[ti % 2], xnb2[ti % 2], eidxT2[ti % 2], gateT2[ti % 2]
        cx.dma(ht[0:C, :], tl['src'], [tl['hb']], [ht])
        rms_norm(cx, C, ht, gbc, xn, junk, ss)
        cx.copy(xnb[0:C, :], xn[0:C, :], [xn], [xnb], eng='pool')
        for kc in range(8):
            pp = (pA, pB)[kc % 2]
            cx.tr(pp[:, 0:C], xn[0:C, kc * 128:(kc + 1) * 128], ident[0:C, 0:C], [xn, ident], [pp])
            cx.copy(xnT[:, kc, 0:C], pp[:, 0:C], [pp], [xnT], eng=('dve', 'act')[kc % 2])
        for qb in range(16):
            pp = (pA, pB)[qb % 2]
            for kc in range(8):
                cx.mm(pp[:, 0:C], wq[:, kc, qb * 128:(qb + 1) * 128], xnT[:, kc, 0:C], kc == 0, kc == 7, [wqb[kc], xnT], [pp])
            cx.copy(qT[:, qb, 0:C], pp[:, 0:C], [pp], [qT], eng=('dve', 'act')[qb % 2])
            yield
        for qb in range(16):
            pp = (pA, pB)[qb % 2]
            cx.mm(pp[0:C, 0:128], qT[:, qb, 0:C], keysT[:, qb, :], True, True, [qT, keysT], [pp])
            cx.copy(sc[0:C, qb, :], pp[0:C, 0:128], [pp], [sc], eng='act')
            if qb % 4 == 3:
                yield
        for qb in range(16):
            cx.op('dve', lambda e, qb=qb: e.max(out=sv[0:C, qb, 0:8], in_=sc[0:C, qb, :]), [sc], [sv])
            cx.op('dve', lambda e, qb=qb: e.match_replace(out=sc2[0:C, qb, :], in_to_replace=sv[0:C, qb, 0:8],
                                                         in_values=sc[0:C, qb, :], imm_value=NEG), [sc, sv], [sc2])
            cx.op('dve', lambda e, qb=qb: e.max(out=sv[0:C, qb, 8:16], in_=sc2[0:C, qb, :]), [sc2], [sv])
            cx.op('dve', lambda e, qb=qb: e.max_index(out=si[0:C, qb, 0:8], in_max=sv[0:C, qb, 0:8], in_values=sc[0:C, qb, :]), [sc, sv], [si])
            cx.op('dve', lambda e, qb=qb: e.max_index(out=si[0:C, qb, 8:16], in_max=sv[0:C, qb, 8:16], in_values=sc2[0:C, qb, :]), [sc2, sv], [si])
            yield
        cx.copy(sif[0:C], si[0:C], [si], [sif])
        svv = sv[0:C].rearrange("p (h z) m -> p h z m", z=2)
        sifv = sif[0:C].rearrange("p (h z) m -> p h z m", z=2)
        candv = cand[0:C].rearrange("p h (i j) -> p h i j", j=16)
        cx.tt(candv, svv[:, :, 0, :].unsqueeze(3).to_broadcast([C, 8, 16, 16]),
              svv[:, :, 1, :].unsqueeze(2).to_broadcast([C, 8, 16, 16]), ALU.add, [sv], [cand])
        c2 = sc2[0:C].rearrange("p a b -> p (a b)").rearrange("p (h x) -> p h x", h=8)
        for h in range(8):
            cx.op('dve', lambda e, h=h: e.max(out=tv[0:C, h, 0:8], in_=cand[0:C, h, :]), [cand], [tv])
            cx.op('dve', lambda e, h=h: e.match_replace(out=c2[:, h, :], in_to_replace=tv[0:C, h, 0:8],
                                                       in_values=cand[0:C, h, :], imm_value=NEG), [cand, tv], [sc2])
            cx.op('dve', lambda e, h=h: e.max(out=tv[0:C, h, 8:16], in_=c2[:, h, :]), [sc2], [tv])
            cx.op('dve', lambda e, h=h: e.max_index(out=pos[0:C, h, 0:8], in_max=tv[0:C, h, 0:8], in_values=cand[0:C, h, :]), [cand, tv], [pos])
            cx.op('dve', lambda e, h=h: e.max_index(out=pos[0:C, h, 8:16], in_max=tv[0:C, h, 8:16], in_values=c2[:, h, :]), [sc2, tv], [pos])
            yield
        posf = pos[0:C].rearrange("p h m -> p (h m)")
        cx.op('dve', lambda e: e.tensor_single_scalar(out=pij[0:C, 0, :], in_=posf, scalar=4, op=ALU.logical_shift_right), [pos], [pij])
        cx.op('dve', lambda e: e.tensor_single_scalar(out=pij[0:C, 1, :], in_=posf, scalar=15, op=ALU.bitwise_and), [pos], [pij])
        cx.copy(pijf[0:C], pij[0:C], [pij], [pijf])
        ohv = oh[0:C].rearrange("p h (m i) -> p h m i", i=16)
        io = cst['iota16']
        for z in range(2):
            pv = pijf[0:C, z, :].rearrange("p (h m) -> p h m", h=8)
            cx.tt(ohv, pv.unsqueeze(3).to_broadcast([C, 8, 16, 16]),
                  io[0:C, :].unsqueeze(1).unsqueeze(1).to_broadcast([C, 8, 16, 16]), ALU.is_equal, [pijf, io], [oh])
            cx.tt(ohv, ohv, sifv[:, :, z, :].unsqueeze(2).to_broadcast([C, 8, 16, 16]), ALU.mult, [oh, sif], [oh])
            cx.red(e01[0:C, z, :], oh[0:C].rearrange("p h (m i) -> p (h m) i", i=16), ALU.add, [oh], [e01])
            yield
        cx.stt(eidx[0:C, :], e01[0:C, 0, :], 128.0, e01[0:C, 1, :], ALU.mult, ALU.add, [e01], [eidx])
        cx.copy(mx8[0:C, :], tv[0:C, :, 0], [tv], [mx8], eng='act')
        cx.tt(tv[0:C], tv[0:C], mx8[0:C, :].unsqueeze(2).to_broadcast([C, 8, 16]), ALU.subtract, [tv, mx8], [tv])
        gv = gate[0:C].rearrange("p (h m) -> p h m", h=8)
        cx.act(gv, tv[0:C], AF.Exp, [tv], [gate])
        cx.red(zz[0:C, 0:8], gv, ALU.add, [gate], [zz])
        cx.op('dve', lambda e: e.reciprocal(out=zz[0:C, 8:16], in_=zz[0:C, 0:8]), [zz], [zz])
        cx.tt(gv, gv, zz[0:C, 8:16].unsqueeze(2).to_broadcast([C, 8, 16]), ALU.mult, [gate, zz], [gate])
        cx.tr(pA[:, 0:C], eidx[0:C, :], ident[0:C, 0:C], [eidx, ident], [pA])
        cx.copy(eidxT[:, 0:C], pA[:, 0:C], [pA], [eidxT])
        cx.tr(pB[:, 0:C], gate[0:C, :], ident[0:C, 0:C], [gate, ident], [pB])
        cx.copy(gateT[:, 0:C], pB[:, 0:C], [pB], [gateT], eng='act')

    def gath(ti, tl):
        C = tl['C']
        ht, xnb, eidxT, gateT = ht2[ti % 2], xnb2[ti % 2], eidxT2[ti % 2], gateT2[ti % 2]
        w1 = cst['w1b']
        for step in range(C + 4):
            j = step
            if j < C:
                uv = uvb[j % NBV]
                xb = xbc[j % 2]
                jk = junk2[j % 2]
                cx.S.op('pool', lambda e, uv=uv, j=j: e.indirect_dma_start(
                    out=uv[:], out_offset=None, in_=uvtab,
                    in_offset=bass.IndirectOffsetOnAxis(ap=eidxT[:, j:j + 1], axis=0), element_offset=eoff),
                    reads=[eidxT.b], writes=[uv.b], dma=True)
                for hf in range(2):
                    cx.mm(xb[:, hf * 512:(hf + 1) * 512], identb[0:C, j:j + 1].to_broadcast([C, 128]),
                          xnb[0:C, hf * 512:(hf + 1) * 512], True, True, [identb, xnb], [xb])
                ac = acol[j % 4]
                cx.stt(jk[:], uv[:, 0:1024], 1.0, xb[:], ALU.mult, ALU.mult, [uv, xb], [jk, ac], accum_out=ac[:, 0:1])
                cx.act(ac[:, 1:2], ac[:, 0:1], AF.Gelu, [ac], [ac])
            j1 = step - 2
            if 0 <= j1 < C:
                L = lb[j1 % NBL]
                ac = acol[j1 % 4]
                cx.ts(L[:, 0:C], w1[:, 127 - j1:127 - j1 + C], ac[:, 1:2], ALU.mult, [w1, ac, gateT], [L], s2=gateT[:, j1:j1 + 1], op1=ALU.mult, eng='pool')
            j2 = step - 4
            if j2 >= 0:
                L = lb[j2 % NBL]
                vv = uvb[j2 % NBV]
                for hf in range(2):
                    cx.mm(Y[0:C, hf * 512:(hf + 1) * 512], L[:, 0:C], vv[:, 1024 + hf * 512:1024 + (hf + 1) * 512], j2 == 0, j2 == C - 1, [L, vv], [Y])
            yield
        cx.tt(ht[0:C, :], Y[0:C, :], ht[0:C, :], ALU.add, [Y, ht], [ht])
        cx.dma(tl['dst'], ht[0:C, :], [ht], [tl['hb']])
        if gfin:
            rms_norm(cx, C, ht, gfb, xnf, junk2[0], ssf)
            for (oap, r0, r1) in tl['fin']:
                cx.dma(oap, xnf[r0:r1, :], [xnf], [])

    for _ in prep(0, tiles[0]):
        pass
    for ti, tl in enumerate(tiles):
        nxt = prep(ti + 1, tiles[ti + 1]) if ti + 1 < len(tiles) else None
        k = 0
        for _ in gath(ti, tl):
            k += 1
            if nxt is not None and k % 2 == 0:
                try:
                    next(nxt)
                except StopIteration:
                    nxt = None
        if nxt is not None:
            for _ in nxt:
                pass


def load_cast(cx, dst, src_ap, stg, i, P, ncols):
    st = stg[i % len(stg)]
    cx.dma(st[0:P, 0:ncols], src_ap, [], [st])
    eng = ('dve', 'pool', 'act')[i % 3]
    return st, eng


def load_w_kc(cx, dstT, src2d, stg, cnt, ncols):
    K = src2d.shape[0]
    for kc in range((K + 127) // 128):
        P = min(128, K - kc * 128)
        st, eng = load_cast(cx, None, src2d[kc * 128:kc * 128 + P, :], stg, cnt[0], P, ncols)
        cx.copy(dstT[0:P, kc, 0:ncols], st[0:P, 0:ncols], [st], [dstT], eng=eng)
        cnt[0] += 1


def bcast_vec(cx, name, src1d, n=1024):
    t = cx.sb(name, [128, n], F32)
    cx.dma(t[:], src1d.partition_broadcast(128), [], [t])
    return t


def alias(parent, ap, name):
    t = T(ap, name)
    t.b = parent.b
    return t


def rms_norm_ps(cx, C, ht, gps, gT, xn, junk, ss, eps=1e-5, n=1024):
    cx.act(junk[0:C, 0:n], ht[0:C, 0:n], AF.Square, [ht], [junk, ss], accum_out=ss[0:C, 0:1])
    cx.ts(ss[0:C, 1:2], ss[0:C, 0:1], 1.0 / n, ALU.mult, [ss], [ss], s2=eps, op1=ALU.add)
    cx.act(ss[0:C, 1:2], ss[0:C, 1:2], AF.Sqrt, [ss], [ss])
    cx.op('dve', lambda e: e.reciprocal(out=ss[0:C, 1:2], in_=ss[0:C, 1:2]), [ss], [ss])
    cx.stt(xn[0:C, 0:n], ht[0:C, 0:n], ss[0:C, 1:2], gps, ALU.mult, ALU.mult, [ht, ss, gT], [xn])


def rwkv_pass(cx, cst, W, tiles, smp, O):
    ident, identb, ones = cst['ident'], cst['identb'], cst['ones']
    triu, triu_s, tril_s = cst['triu'], cst['triu_s'], cst['tril_s']
    NH = 16
    with cx.scope():
        PA = cx.ps("rw_PA", [128, 1024])
        PB = cx.ps("rw_PB", [128, 1024])
        PC = cx.ps("rw_PC", [128, 1024])
        PD = cx.ps("rw_PD", [128, 1024])
        Wo = cx.sb("rw_Wo", [128, 8, 1024], BF16)
        VT = [cx.sb(f"rw_VT{i}", [128, 1024]) for i in range(3)]
        vmap = {'g': (0, 0), 'kk': (0, 32), 'ka': (0, 64), 'rk': (1, 0), 'lnw': (1, 32), 'lnb': (1, 64), 'w0': (2, 0), 'a0': (2, 32)}
        for nm, src in (('g', W['norm_mix'][0:1, :]), ('kk', W['rwkv_k_k'][0:1, :]), ('ka', W['rwkv_k_a'][0:1, :]),
                        ('rk', W['rwkv_r_k'].rearrange("o h k -> o (h k)")), ('lnw', W['rwkv_ln_w'][0:1, :]), ('lnb', W['rwkv_ln_b'][0:1, :]),
                        ('w0', W['rwkv_w0'][0:1, :]), ('a0', W['rwkv_a0'][0:1, :])):
            ti_, p_ = vmap[nm]
            cx.dma(VT[ti_][p_:p_ + 1, :], src, [], [VT[ti_]])

        def vrow(nm, hf):
            ti_, p_ = vmap[nm]
            return ones[p_:p_ + 1, :], VT[ti_][p_:p_ + 1, hf * 512:(hf + 1) * 512], VT[ti_]

        def bcast(nm, C, PP):
            for hf in range(2):
                l, r, t = vrow(nm, hf)
                cx.mm(PP[0:C, hf * 512:(hf + 1) * 512], l[:, 0:C], r, True, True, [ones, t], [PP])

        ht = cx.sb("rw_ht", [128, 1024])
        xn = cx.sb("rw_xn", [128, 1024]); t1 = xn
        ss = cx.sb("rw_ss", [128, 2])
        r_ = cx.sb("rw_r", [128, 1024]); junk = r_
        k_ = cx.sb("rw_k", [128, 1024]); o_ = k_
        v_ = cx.sb("rw_v", [128, 1024])
        vb = cx.sb("rw_vb", [128, 1024], BF16)
        lw = cx.sb("rw_lw", [128, 1024])
        a_ = cx.sb("rw_a", [128, 1024])
        g_ = cx.sb("rw_g_", [128, 1024], BF16)
        kk = cx.sb("rw_kk_", [128, 1024])
        kp = cx.sb("rw_kp", [128, 1024])
        b_ = cx.sb("rw_b", [128, 1024])
        bon = cx.sb("rw_bon", [128, 1024])
        st16 = cx.sb("rw_st16", [128, 64])
        ogb = cx.sb("rw_ogb", [128, 1024], BF16)
        ogT = cx.sb("rw_ogT", [128, 8, 128], BF16)
        prevT = cx.sb("rw_prevT", [128, 8, 16])
        cx.memset(prevT[:], 0.0, [prevT])

        def back(tl, osrc, oT):
            C = tl['C']
            o3 = osrc[0:C, :].rearrange("p (h k) -> p h k", k=64)
            cx.red(st16[0:C, 0:16], o3, ALU.add, [oT], [st16])
            cx.act(t1[0:C, :], osrc[0:C, :], AF.Square, [oT], [t1])
            cx.red(st16[0:C, 16:32], t1[0:C, :].rearrange("p (h k) -> p h k", k=64), ALU.add, [t1], [st16])
            cx.ts(st16[0:C, 0:32], st16[0:C, 0:32], 1.0 / 64, ALU.mult, [st16], [st16])
            cx.tt(st16[0:C, 48:64], st16[0:C, 0:16], st16[0:C, 0:16], ALU.mult, [st16], [st16])
            cx.tt(st16[0:C, 16:32], st16[0:C, 16:32], st16[0:C, 48:64], ALU.subtract, [st16], [st16])
            cx.ts(st16[0:C, 16:32], st16[0:C, 16:32], 64e-5, ALU.add, [st16], [st16])
            cx.act(st16[0:C, 16:32], st16[0:C, 16:32], AF.Sqrt, [st16], [st16])
            cx.op('dve', lambda e: e.reciprocal(out=st16[0:C, 16:32], in_=st16[0:C, 16:32]), [st16], [st16])
            t3 = t1[0:C, :].rearrange("p (h k) -> p h k", k=64)
            cx.tt(t3, o3, st16[0:C, 0:16].unsqueeze(2).to_broadcast([C, 16, 64]), ALU.subtract, [oT, st16], [t1])
            cx.tt(t3, t3, st16[0:C, 16:32].unsqueeze(2).to_broadcast([C, 16, 64]), ALU.mult, [t1, st16], [t1])
            bcast('lnw', C, PC)
            cx.tt(t1[0:C, :], t1[0:C, :], PC[0:C, :], ALU.mult, [t1, PC], [t1])
            bcast('lnb', C, PD)
            cx.tt(t1[0:C, :], t1[0:C, :], PD[0:C, :], ALU.add, [t1, PD], [t1])
            cx.tt(t1[0:C, :], t1[0:C, :], bon[0:C, :], ALU.add, [t1, bon], [t1])
            cx.tt(ogb[0:C, :], t1[0:C, :], g_[0:C, :], ALU.mult, [t1, g_], [ogb])
            PBb = PB[:].bitcast(BF16)
            for kc in range(8):
                cx.tr(PBb[:, kc * 128:kc * 128 + C], ogb[0:C, kc * 128:(kc + 1) * 128], identb[0:C, 0:C], [ogb, identb], [PB])
            cx.copy(ogT[:, :, 0:C], PBb[:, 0:1024].rearrange("p (a b) -> p a b", b=128)[:, :, 0:C], [PB], [ogT])
            for hf in range(2):
                for kc in range(8):
                    cx.mm(PA[0:C, hf * 512:(hf + 1) * 512], ogT[:, kc, 0:C], Wo[:, kc, hf * 512:(hf + 1) * 512], kc == 0, kc == 7, [ogT, Wo], [PA])
            cx.tt(ht[0:C, :], PA[0:C, :], ht[0:C, :], ALU.add, [PA, ht], [ht])
            cx.dma(tl['dst'], ht[0:C, :], [ht], [tl['hb']])

        with cx.scope():
            Wr = cx.sb("rw_Wr", [128, 8, 1024], BF16)
            Wk = cx.sb("rw_Wk", [128, 8, 1024], BF16)
            Wv = cx.sb("rw_Wv", [128, 8, 1024], BF16)
            Wl = cx.sb("rw_Wl", [128, 8, 288], BF16)
            W2 = cx.sb("rw_W2", [128, 4, 1024], BF16)
            mixv = cx.sb("rw_mixv", [128, 8, 6], F32)
            with cx.scope():
                stg = [cx.sb(f"rw_stg{i}", [128, 1024], F32) for i in range(2)]
                cnt = [0]
                load_w_kc(cx, Wr, W['rwkv_w_rkv'][0, 0], stg, cnt, 1024)
                load_w_kc(cx, Wk, W['rwkv_w_rkv'][0, 1], stg, cnt, 1024)
                load_w_kc(cx, Wv, W['rwkv_w_rkv'][0, 2], stg, cnt, 1024)
                load_w_kc(cx, Wo, W['rwkv_w_o'][0], stg, cnt, 1024)
                for (nm, c0, n) in (('rwkv_w1', 0, 64), ('rwkv_a1', 64, 64), ('rwkv_g1', 128, 160)):
                    for kc in range(8):
                        st = stg[cnt[0] % 2]
                        cx.dma(st[:, 0:n], W[nm][0, kc * 128:(kc + 1) * 128, :], [], [st])
                        cx.copy(Wl[:, kc, c0:c0 + n], st[:, 0:n], [st], [Wl], eng=('dve', 'pool')[cnt[0] % 2])
                        cnt[0] += 1
                for (nm, slot, r0, nr) in (('rwkv_w2', 0, 0, 64), ('rwkv_a2', 1, 0, 64), ('rwkv_g2', 2, 0, 128), ('rwkv_g2', 3, 128, 32)):
                    st = stg[cnt[0] % 2]
                    cx.dma(st[0:nr, :], W[nm][0, r0:r0 + nr, :], [], [st])
                    cx.copy(W2[0:nr, slot, :], st[0:nr, :], [st], [W2], eng=('dve', 'pool')[cnt[0] % 2])
                    cnt[0] += 1
                cx.memset(stg[0][0:32, :], 0.0, [stg[0]])
                cx.dma(stg[0][0:6, :], W['rwkv_mix'][0], [], [stg[0]])
                for kc in range(8):
                    cx.mm(PA[:, kc * 32:(kc + 1) * 32], stg[0][0:32, kc * 128:(kc + 1) * 128], ident[0:32, 0:32], True, True, [stg[0], ident], [PA])
                cx.copy(mixv[:], PA[:, 0:256].rearrange("p (a b) -> p a b", b=32)[:, :, 0:6], [PA], [mixv])

            xnT = cx.sb("rw_xnT", [128, 8, 128])
            dxT = cx.sb("rw_dxT", [128, 8, 128])
            tmpT = cx.sb("rw_tmpT", [128, 8, 128])
            mT = [alias(ogT, ogT[:], "rw_mT0"), cx.sb("rw_mT1", [128, 8, 128], BF16)]
            hT = cx.sb("rw_hT", [128, 2, 128], BF16)

            def front(tl, sh, after_norm=None):
                C = tl['C']
                if isinstance(tl['src'], list):
                    for (ap, r0, r1) in tl['src']:
                        cx.dma(ht[r0:r1, :], ap, [tl['hb']], [ht])
                else:
                    cx.dma(ht[0:C, :], tl['src'], [tl['hb']], [ht])
                bcast('g', C, PD)
                rms_norm_ps(cx, C, ht, PD[0:C, :], PD, xn, junk, ss)
                if after_norm is not None:
                    after_norm()
                for kc in range(8):
                    pp = (PA, PB)[kc % 2]
                    cx.tr(pp[:, 0:C], xn[0:C, kc * 128:(kc + 1) * 128], ident[0:C, 0:C], [xn, ident], [pp])
                    cx.copy(xnT[:, kc, 0:C], pp[:, 0:C], [pp], [xnT], eng=('dve', 'act')[kc % 2])
                cx.tt(dxT[:, :, 0:sh], prevT[:, :, 0:sh], xnT[:, :, 0:sh], ALU.subtract, [prevT, xnT], [dxT])
                if C > sh:
                    cx.tt(dxT[:, :, sh:C], xnT[:, :, 0:C - sh], xnT[:, :, sh:C], ALU.subtract, [xnT], [dxT])
                cx.copy(prevT[:, :, 0:sh], xnT[:, :, C - sh:C], [xnT], [prevT], eng='pool')

                def mix(j, dst):
                    cx.tt(tmpT[:, :, 0:C], dxT[:, :, 0:C], mixv[:, :, j:j + 1].to_broadcast([128, 8, C]), ALU.mult, [dxT, mixv], [tmpT])
                    cx.tt(dst[:, :, 0:C], tmpT[:, :, 0:C], xnT[:, :, 0:C], ALU.add, [tmpT, xnT], [dst])

                def proj(src, Wm, PP):
                    for hf in range(2):
                        for kc in range(8):
                            cx.mm(PP[0:C, hf * 512:(hf + 1) * 512], src[:, kc, 0:C], Wm[:, kc, hf * 512:(hf + 1) * 512], kc == 0, kc == 7, [src, Wm], [PP])

                mix(0, mT[0]); proj(mT[0], Wr, PA)
                cx.copy(r_[0:C, :], PA[0:C, :], [PA], [r_], eng='act')
                mix(2, mT[1]); proj(mT[1], Wk, PB)
                cx.copy(k_[0:C, :], PB[0:C, :], [PB], [k_], eng='act')
                mix(3, mT[0]); proj(mT[0], Wv, PA)
                cx.copy(v_[0:C, :], PA[0:C, :], [PA], [v_], eng='act')
                cx.copy(vb[0:C, :], PA[0:C, :], [PA], [vb], eng='dve')
                mix(1, mT[1])
                for kc in range(8):
                    cx.mm(PC[0:64, 0:C], Wl[:, kc, 0:64], mT[1][:, kc, 0:C], kc == 0, kc == 7, [Wl, mT[1]], [PC])
                cx.act(hT[0:64, 0, 0:C], PC[0:64, 0:C], AF.Tanh, [PC], [hT])
                for hf in range(2):
                    l, r, t = vrow('w0', hf)
                    cx.mm(PB[0:C, hf * 512:(hf + 1) * 512], l[:, 0:C], r, True, False, [ones, t], [PB])
                    cx.mm(PB[0:C, hf * 512:(hf + 1) * 512], hT[0:64, 0, 0:C], W2[0:64, 0, hf * 512:(hf + 1) * 512], False, True, [hT, W2], [PB])
                cx.act(lw[0:C, :], PB[0:C, :], AF.Sigmoid, [PB], [lw])
                cx.ts(lw[0:C, :], lw[0:C, :], -0.6065306597126334, ALU.mult, [lw], [lw], eng='pool')
                mix(4, mT[0])
                for kc in range(8):
                    cx.mm(PC[0:64, 0:C], Wl[:, kc, 64:128], mT[0][:, kc, 0:C], kc == 0, kc == 7, [Wl, mT[0]], [PC])
                cx.copy(hT[0:64, 0, 0:C], PC[0:64, 0:C], [PC], [hT], eng='act')
                for hf in range(2):
                    l, r, t = vrow('a0', hf)
                    cx.mm(PA[0:C, hf * 512:(hf + 1) * 512], l[:, 0:C], r, True, False, [ones, t], [PA])
                    cx.mm(PA[0:C, hf * 512:(hf + 1) * 512], hT[0:64, 0, 0:C], W2[0:64, 1, hf * 512:(hf + 1) * 512], False, True, [hT, W2], [PA])
                cx.act(a_[0:C, :], PA[0:C, :], AF.Sigmoid, [PA], [a_])
                mix(5, mT[1])
                for (c0, n, slot) in ((128, 128, 0), (256, 32, 1)):
                    for kc in range(8):
                        cx.mm(PC[0:n, 0:C], Wl[:, kc, c0:c0 + n], mT[1][:, kc, 0:C], kc == 0, kc == 7, [Wl, mT[1]], [PC])
                    cx.act(hT[0:n, slot, 0:C], PC[0:n, 0:C], AF.Sigmoid, [PC], [hT])
                for hf in range(2):
                    cx.mm(PB[0:C, hf * 512:(hf + 1) * 512], hT[0:128, 0, 0:C], W2[0:128, 2, hf * 512:(hf + 1) * 512], True, False, [hT, W2], [PB])
                    cx.mm(PB[0:C, hf * 512:(hf + 1) * 512], hT[0:32, 1, 0:C], W2[0:32, 3, hf * 512:(hf + 1) * 512], False, True, [hT, W2], [PB])
                cx.copy(g_[0:C, :], PB[0:C, :], [PB], [g_], eng='act')
                bcast('kk', C, PC)
                cx.tt(kk[0:C, :], k_[0:C, :], PC[0:C, :], ALU.mult, [k_, PC], [kk])
                cx.tt(t1[0:C, :], kk[0:C, :], kk[0:C, :], ALU.mult, [kk], [t1], eng='pool')
                cx.red(st16[0:C, 0:16], t1[0:C, :].rearrange("p (h k) -> p h k", k=64), ALU.add, [t1], [st16])
                cx.act(st16[0:C, 0:16], st16[0:C, 0:16], AF.Sqrt, [st16], [st16])
                cx.ts(st16[0:C, 0:16], st16[0:C, 0:16], 1e-12, ALU.max, [st16], [st16])
                cx.op('dve', lambda e: e.reciprocal(out=st16[0:C, 16:32], in_=st16[0:C, 0:16]), [st16], [st16])
                kk3 = kk[0:C, :].rearrange("p (h k) -> p h k", k=64)
                cx.tt(kk3, kk3, st16[0:C, 16:32].unsqueeze(2).to_broadcast([C, 16, 64]), ALU.mult, [kk, st16], [kk])
                bcast('ka', C, PD)
                cx.stt(t1[0:C, :], a_[0:C, :], -1.0, PD[0:C, :], ALU.add, ALU.mult, [a_, PD], [t1])
                cx.stt(kp[0:C, :], t1[0:C, :], 1.0, k_[0:C, :], ALU.add, ALU.mult, [t1, k_], [kp])
                cx.tt(b_[0:C, :], kk[0:C, :], a_[0:C, :], ALU.mult, [kk, a_], [b_], eng='pool')
                bcast('rk', C, PC)
                cx.tt(t1[0:C, :], r_[0:C, :], kp[0:C, :], ALU.mult, [r_, kp], [t1])
                cx.tt(t1[0:C, :], t1[0:C, :], PC[0:C, :], ALU.mult, [t1, PC], [t1])
                cx.red(st16[0:C, 32:48], t1[0:C, :].rearrange("p (h k) -> p h k", k=64), ALU.add, [t1], [st16])
                cx.tt(bon[0:C, :].rearrange("p (h k) -> p h k", k=64), v_[0:C, :].rearrange("p (h k) -> p h k", k=64),
                      st16[0:C, 32:48].unsqueeze(2).to_broadcast([C, 16, 64]), ALU.mult, [v_, st16], [bon])

            with cx.scope():
                Hs = cx.sb("rw_H", [64, 16, 64])
                Hb = cx.sb("rw_Hb", [64, 16, 64], BF16)
                G = cx.sb("rw_G", [64, 32])
                FT_B = cx.sb("rw_FTB", [64, 16, 128], BF16)
                FT_K = cx.sb("rw_FTK", [64, 16, 128], BF16)
                FT_AR = cx.sb("rw_FTAR", [64, 16, 256], BF16)
                tokb = [alias(ogb, ogb[:], "rw_tokb0"), cx.sb("rw_tokb1", [128, 1024], BF16)]
                xnTb = xnT[:].rearrange("p a b -> p (a b)").bitcast(BF16)
                Kh = alias(xnT, xnTb[:, 0:1024], "rw_Kh")
                Bh = alias(xnT, xnTb[:, 1024:2048], "rw_Bh")
                e1 = alias(tmpT, tmpT[:].rearrange("p a b -> p (a b)"), "rw_e1")
                sets = []
                for i in range(1):
                    sets.append(dict(PTb=cx.sb(f"rw_PTb{i}", [128, 4, 128], BF16), ArbT=cx.sb(f"rw_ArbT{i}", [128, 4, 128], BF16),
                                     AakT=cx.sb(f"rw_AakT{i}", [128, 4, 128], BF16), ArkT=cx.sb(f"rw_ArkT{i}", [128, 4, 128], BF16)))
                Mx = [cx.sb(f"rw_M{i}", [128, 4, 128]) for i in range(2)]
                MTx = [cx.sb(f"rw_MT{i}", [128, 4, 128]) for i in range(2)]
                PTf = cx.sb("rw_PTf", [128, 4, 128])
                Xs = cx.sb("rw_Xs", [128, 256], BF16)
                Us = cx.sb("rw_Us", [128, 256], BF16)
                HT_o = alias(a_, a_[:, 0:512].rearrange("p (a b) -> p a b", b=64), "rw_HTo")
                cx.S.mark('weights_loaded')
                cx.memset(Hs[:], 0.0, [Hs])
                cx.memset(Hb[:], 0.0, [Hb], eng='pool')
                for ti, tl in enumerate(tiles):
                    C = tl['C']
                    cx.S.mark(f'tile{ti}_start')
                    if ti == len(tiles) - 1:
                        front(tl, 1, lambda C=C: cx.dma(O['p_shift'], xn[C - 1:C, :], [xn], []))
                    else:
                        front(tl, 1)
                    for hf in range(2):
                        cx.mm(PA[0:C, hf * 512:(hf + 1) * 512], triu[0:C, 0:C], lw[0:C, hf * 512:(hf + 1) * 512], True, True, [triu, lw], [PA])
                        cx.mm(PB[0:C, hf * 512:(hf + 1) * 512], ones[0:C, 0:C], lw[0:C, hf * 512:(hf + 1) * 512], True, True, [ones, lw], [PB])
                    for h in range(16):
                        cx.mm(PC[0:64, h * 2:h * 2 + 2], lw[0:C, h * 64:(h + 1) * 64], ones[0:C, 0:2], True, True, [lw, ones], [PC])
                    cx.act(G[:, 0:32], PC[0:64, 0:32], AF.Exp, [PC], [G])
                    PCb = PC[:].bitcast(BF16)
                    PDb = PD[:].bitcast(BF16)

                    def to_fm(src, dst, off, pp, ppT):
                        for h in range(16):
                            cx.tr(pp[0:64, h * 128:h * 128 + C], src[0:C, h * 64:(h + 1) * 64], identb[0:C, 0:C], [src, identb], [ppT])
                        cx.copy(dst[:, :, off:off + C], pp[0:64, 0:2048].rearrange("p (a b) -> p a b", b=128)[:, :, 0:C], [ppT], [dst], eng='act')

                    cx.act(e1[0:C, :], PA[0:C, :], AF.Exp, [PA], [e1])
                    cx.tt(tokb[0][0:C, :], r_[0:C, :], e1[0:C, :], ALU.mult, [r_, e1], [tokb[0]])
                    to_fm(tokb[0], FT_AR, C, PCb, PC)
                    cx.act(e1[0:C, :], PA[0:C, :], AF.Exp, [PA], [e1], scale=-1.0)
                    cx.tt(tokb[1][0:C, :], kp[0:C, :], e1[0:C, :], ALU.mult, [kp, e1], [tokb[1]])
                    to_fm(tokb[1], FT_K, 0, PDb, PD)
                    cx.tt(tokb[0][0:C, :], b_[0:C, :], e1[0:C, :], ALU.mult, [b_, e1], [tokb[0]])
                    to_fm(tokb[0], FT_B, 0, PCb, PC)
                    cx.tt(t1[0:C, :], PA[0:C, :], lw[0:C, :], ALU.subtract, [PA, lw], [t1])
                    cx.act(e1[0:C, :], t1[0:C, :], AF.Exp, [t1], [e1])
                    cx.stt(tokb[1][0:C, :], kk[0:C, :], -1.0, e1[0:C, :], ALU.mult, ALU.mult, [kk, e1], [tokb[1]])
                    to_fm(tokb[1], FT_AR, 0, PDb, PD)
                    cx.copy(t1[0:C, :], PA[0:C, :], [PA], [t1], eng='act')
                    cx.tt(t1[0:C, :], PB[0:C, :], t1[0:C, :], ALU.subtract, [PB, t1], [t1])
                    cx.act(e1[0:C, :], t1[0:C, :], AF.Exp, [t1], [e1])
                    cx.tt(Kh[0:C, :], kp[0:C, :], e1[0:C, :], ALU.mult, [kp, e1], [Kh])
                    cx.tt(Bh[0:C, :], b_[0:C, :], e1[0:C, :], ALU.mult, [b_, e1], [Bh], eng='pool')
                    cx.S.mark(f'tile{ti}_prep_done')
                    nl = max(1, int(np.ceil(np.log2(C))))
                    for hg in range(4):
                        st_ = sets[0]
                        PTb, ArbT, AakT, ArkT = st_['PTb'], st_['ArbT'], st_['AakT'], st_['ArkT']
                        for (FT_l, outs) in ((FT_B, ('MT', ArbT)), (FT_K, (AakT, ArkT))):
                            for q in range(4):
                                h = hg * 4 + q
                                pp = (PC, PD)[q // 2]
                                c0 = (q % 2) * 2 * C
                                cx.mm(pp[0:C, c0:c0 + 2 * C], FT_l[0:64, h, 0:C], FT_AR[0:64, h, 0:2 * C], True, True, [FT_l, FT_AR], [pp])
                            for q2 in range(2):
                                pp = (PC, PD)[q2]
                                v4 = pp[0:C, 0:4 * C].rearrange("p (q z c) -> p q z c", q=2, z=2)
                                o0 = MTx[0] if outs[0] == 'MT' else outs[0]
                                cx.tt(o0[0:C, 2 * q2:2 * q2 + 2, 0:C], v4[:, :, 0, :], triu_s[0:C, 0:C].unsqueeze(1).to_broadcast([C, 2, C]), ALU.mult, [pp, triu_s], [o0])
                                cx.tt(outs[1][0:C, 2 * q2:2 * q2 + 2, 0:C], v4[:, :, 1, :], triu[0:C, 0:C].unsqueeze(1).to_broadcast([C, 2, C]), ALU.mult, [pp, triu], [outs[1]])
                        for q in range(4):
                            h = hg * 4 + q
                            cx.mm(PC[0:C, q * C:(q + 1) * C], FT_AR[0:64, h, 0:C], FT_B[0:64, h, 0:C], True, True, [FT_AR, FT_B], [PC])
                        cx.tt(Mx[0][0:C, :, 0:C], PC[0:C, 0:4 * C].rearrange("p (q c) -> p q c", q=4), tril_s[0:C, 0:C].unsqueeze(1).to_broadcast([C, 4, C]), ALU.mult, [PC, tril_s], [Mx[0]])
                        cx.tt(PTf[0:C, :, 0:C], MTx[0][0:C, :, 0:C], ident[0:C, 0:C].unsqueeze(1).to_broadcast([C, 4, C]), ALU.add, [MTx[0], ident], [PTf])
                        cur = 0
                        for lv in range(1, nl):
                            nx = 1 - cur
                            last = (lv == nl - 1)
                            for q in range(4):
                                cx.mm(PC[0:C, q * C:(q + 1) * C], MTx[cur][0:C, q, 0:C], Mx[cur][0:C, q, 0:C], True, True, [MTx[cur], Mx[cur]], [PC])
                            cx.copy(Mx[nx][0:C, :, 0:C], PC[0:C, 0:4 * C].rearrange("p (q c) -> p q c", q=4), [PC], [Mx[nx]], eng='act')
                            if not last:
                                for q in range(4):
                                    cx.mm(PD[0:C, q * C:(q + 1) * C], Mx[cur][0:C, q, 0:C], MTx[cur][0:C, q, 0:C], True, True, [MTx[cur], Mx[cur]], [PD])
                                cx.copy(MTx[nx][0:C, :, 0:C], PD[0:C, 0:4 * C].rearrange("p (q c) -> p q c", q=4), [PD], [MTx[nx]], eng='dve')
                            for q in range(4):
                                cx.mm(PC[0:C, 512 + q * C:512 + (q + 1) * C], Mx[nx][0:C, q, 0:C], PTf[0:C, q, 0:C], True, True, [Mx[nx], PTf], [PC])
                            cx.tt(PTf[0:C, :, 0:C], PTf[0:C, :, 0:C], PC[0:C, 512:512 + 4 * C].rearrange("p (q c) -> p q c", q=4), ALU.add, [PTf, PC], [PTf])
                            cur = nx
                        cx.copy(PTb[0:C, :, 0:C], PTf[0:C, :, 0:C], [PTf], [PTb], eng='pool')
                        cx.S.mark(f'tile{ti}_hg{hg}_inv_done')
                        c4 = hg * 256
                        pX = PA if hg % 2 == 0 else PB
                        for q in range(4):
                            h = hg * 4 + q
                            cx.mm(pX[0:C, q * 64:(q + 1) * 64], FT_AR[0:64, h, 0:C], Hb[0:64, h, :], True, False, [FT_AR, Hb], [pX])
                            cx.mm(pX[0:C, q * 64:(q + 1) * 64], AakT[0:C, q, 0:C], vb[0:C, h * 64:(h + 1) * 64], False, True, [AakT, vb], [pX])
                        cx.copy(Xs[0:C, :], pX[0:C, 0:256], [pX], [Xs], eng='act')
                        for q in range(4):
                            cx.mm(pX[0:C, 256 + q * 64:256 + (q + 1) * 64], PTb[0:C, q, 0:C], Xs[0:C, q * 64:(q + 1) * 64], True, True, [PTb, Xs], [pX])
                        cx.copy(Us[0:C, :], pX[0:C, 256:512], [pX], [Us], eng='act')
                        for q in range(4):
                            h = hg * 4 + q
                            oc = 512 + q * 64
                            cx.mm(pX[0:C, oc:oc + 64], FT_AR[0:64, h, C:2 * C], Hb[0:64, h, :], True, False, [FT_AR, Hb], [pX])
                            cx.mm(pX[0:C, oc:oc + 64], ArkT[0:C, q, 0:C], vb[0:C, h * 64:(h + 1) * 64], False, False, [ArkT, vb], [pX])
                            cx.mm(pX[0:C, oc:oc + 64], ArbT[0:C, q, 0:C], Us[0:C, q * 64:(q + 1) * 64], False, True, [ArbT, Us], [pX])
                        cx.copy(o_[0:C, c4:c4 + 256], pX[0:C, 512:768], [pX], [o_], eng='act')
                        for q in range(4):
                            h = hg * 4 + q
                            oc = 768 + q * 64
                            cx.mm(pX[0:64, oc:oc + 64], Kh[0:C, h * 64:(h + 1) * 64], vb[0:C, h * 64:(h + 1) * 64], True, False, [Kh, vb], [pX])
                            cx.mm(pX[0:64, oc:oc + 64], Bh[0:C, h * 64:(h + 1) * 64], Us[0:C, q * 64:(q + 1) * 64], False, True, [Bh, Us], [pX])
                        H4 = Hs[0:64, hg * 4:hg * 4 + 4, :]
                        cx.tt(H4, H4, G[:, 0:32].rearrange("p (a b) -> p a b", b=2)[:, hg * 4:hg * 4 + 4, 0:1].to_broadcast([64, 4, 64]), ALU.mult, [Hs, G], [Hs])
                        cx.tt(H4, H4, pX[0:64, 768:1024].rearrange("p (a b) -> p a b", b=64), ALU.add, [Hs, pX], [Hs])
                    cx.copy(Hb[:], Hs[:], [Hs], [Hb], eng='pool')
                    cx.S.mark(f'tile{ti}_chunk_done')
                    back(tl, o_, o_)
                cx.S.mark('prompt_tiles_done')
                for fb in range(8):
                    pp = (PA, PB)[fb % 2]
                    cx.tr(pp[:, 0:64], Hs[0:64, 2 * fb:2 * fb + 2, :].rearrange("p a b -> p (a b)"), ident[0:64, 0:64], [Hs, ident], [pp])
                    cx.copy(HT_o[:, fb, :], pp[:, 0:64], [pp], [HT_o], eng=('dve', 'act')[fb % 2])
                cx.dma(O['p_wkv'].rearrange("(fb hp) v k -> (hp v) fb k", hp=2), HT_o[:, :, :], [HT_o], [])

            C = 64
            cx.S.mark('prompt_done')
            sh0 = alias(a_, a_[0:16, :], "rw_sh0")
            cx.dma(sh0[:, :], smp['shift0'], [], [sh0])
            for kc in range(8):
                pp = (PA, PB)[kc % 2]
                cx.tr(pp[:, 0:16], sh0[:, kc * 128:(kc + 1) * 128], ident[0:16, 0:16], [sh0, ident], [pp])
                cx.copy(prevT[:, kc, 0:16], pp[:, 0:16], [pp], [prevT])
            front(smp, 16, lambda: cx.dma(O['s_shift'], xn[48:64, :], [xn], []))
            cx.act(lw[0:C, :], lw[0:C, :], AF.Exp, [lw], [lw])
            SD = smp['scr']
            sdb = Buf("rw_sd")
            for qi, src in enumerate((r_, lw, kp, v_, kk, b_)):
                for t in range(4):
                    cx.dma(SD[qi, :, :, t, :], src[16 * t:16 * t + 16, :].rearrange("p (hh f) -> p hh f", f=128), [src], [sdb])

        with cx.scope():
            cx.S.mark('sample_front_done')
            Sst = cx.sb("rw_S", [128, 8192])
            tmpS = cx.sb("rw_tmpS", [128, 4096])
            ops_ = cx.sb("rw_ops", [128, 6, 4, 128])
            skk = cx.sb("rw_skk", [128, 128])
            osm = cx.sb("rw_osm", [128, 4, 128])
            cx.dma(Sst[:], smp['wkv0'].rearrange("b (hh hl) v k -> (b hh) (hl v k)", hl=2), [], [Sst])
            for qi in range(6):
                cx.dma(ops_[:, qi, :, :], SD[qi].rearrange("b hh t f -> (b hh) t f"), [sdb], [ops_])
            T3 = tmpS[:].rearrange("p (v k) -> p v k", v=64)

            def bk(qi, t, hl):
                return ops_[:, qi, t, hl * 64:(hl + 1) * 64].unsqueeze(1).to_broadcast([128, 64, 64])

            def bv(ap2):
                return ap2.unsqueeze(2).to_broadcast([128, 64, 64])

            for t in range(4):
                for hl in range(2):
                    S3 = Sst[:, hl * 4096:(hl + 1) * 4096].rearrange("p (v k) -> p v k", v=64)
                    hs = slice(hl * 64, hl * 64 + 64)
                    cx.tt(T3, S3, bk(4, t, hl), ALU.mult, [Sst, ops_], [tmpS])
                    cx.red(skk[:, hs], T3, ALU.add, [tmpS], [skk])
                    cx.tt(S3, S3, bk(1, t, hl), ALU.mult, [Sst, ops_], [Sst])
                    cx.tt(T3, bv(skk[:, hs]), bk(5, t, hl), ALU.mult, [skk, ops_], [tmpS])
                    cx.tt(S3, S3, T3, ALU.subtract, [Sst, tmpS], [Sst])
                    cx.tt(T3, bv(ops_[:, 3, t, hs]), bk(2, t, hl), ALU.mult, [ops_], [tmpS])
                    cx.tt(S3, S3, T3, ALU.add, [Sst, tmpS], [Sst])
                    cx.tt(T3, S3, bk(0, t, hl), ALU.mult, [Sst, ops_], [tmpS])
                    cx.red(osm[:, t, hs], T3, ALU.add, [tmpS], [osm])
            cx.S.mark('sample_rec_done')
            cx.dma(O['s_wkv'].rearrange("b (hh hl) v k -> (b hh) (hl v k)", hl=2), Sst[:], [Sst], [])
            cx.dma(SD[6].rearrange("b hh t f -> (b hh) t f"), osm[:], [osm], [sdb])
            for t in range(4):
                cx.dma(o_[16 * t:16 * t + 16, :].rearrange("p (hh f) -> p hh f", f=128), SD[6, :, :, t, :], [sdb], [o_])
            back(smp, o_, o_)


def mamba_pass(cx, cst, W, tiles, smp, O):
    ident, identb, ones, triu = cst['ident'], cst['identb'], cst['ones'], cst['triu']
    ZO, XO, BO, CO, DO = 0, 2048, 4096, 4608, 5120
    with cx.scope():
        PA = cx.ps("mb_PA", [128, 1024])
        PB = cx.ps("mb_PB", [128, 1024])
        PC = cx.ps("mb_PC", [128, 1024])
        PD = cx.ps("mb_PD", [128, 1024])
        Wout = cx.sb("mb_Wout", [128, 16, 1024], BF16)
        VT = cx.sb("mb_VT", [128, 1024])
        VS = cx.sb("mb_VS", [1, 128])
        vec = cx.sb("mb_vec", [128, 128])
        cx.dma(VT[0:1, :], W['norm_mix'][1:2, :], [], [VT])
        cx.dma(VT[32:33, :], W['mamba_norm_w'][0:1, 0:1024], [], [VT])
        cx.dma(VT[64:65, :], W['mamba_norm_w'][0:1, 1024:2048], [], [VT])
        cx.memset(VS[:], 0.0, [VS])
        cx.dma(VS[0:1, 0:32], W['mamba_dt_bias'][0:1, :], [], [VS])
        cx.dma(VS[0:1, 32:64], W['mamba_a_log'][0:1, :], [], [VS])
        cx.dma(VS[0:1, 64:96], W['mamba_d'][0:1, :], [], [VS])
        cx.mm(PA[:, 0:128], ones[0:1, :], VS[0:1, :], True, True, [ones, VS], [PA])
        cx.copy(vec[:], PA[:, 0:128], [PA], [vec])
        cx.act(vec[:, 32:64], vec[:, 32:64], AF.Exp, [vec], [vec])
        cx.ts(vec[:, 32:64], vec[:, 32:64], -1.0, ALU.mult, [vec], [vec])

        def bcast(p_, C, PP, n=1024):
            for hf in range(n // 512):
                cx.mm(PP[0:C, hf * 512:(hf + 1) * 512], ones[p_:p_ + 1, 0:C], VT[p_:p_ + 1, hf * 512:(hf + 1) * 512], True, True, [ones, VT], [PP])

        ht = cx.sb("mb_ht", [128, 1024])
        ss = cx.sb("mb_ss", [128, 8])
        y_ = cx.sb("mb_y", [128, 2048])
        xn = alias(y_, y_[:, 1024:2048], "mb_xn")
        yjunk = alias(y_, y_[:, 0:1024], "mb_yjunk")
        xs = cx.sb("mb_xs", [128, 2048], BF16)
        tmp = cx.sb("mb_tmp", [128, 512])
        ygb = alias(xs, xs[:, :], "mb_ygb")
        ygT = cx.sb("mb_ygT", [128, 16, 128], BF16)
        xnb = cx.sb("mb_xnb", [128, 1024], BF16)
        xnT = cx.sb("mb_xnT", [128, 8, 128], BF16)

        def norm_in(tl):
            C = tl['C']
            if isinstance(tl['src'], list):
                for (ap, r0, r1) in tl['src']:
                    cx.dma(ht[r0:r1, :], ap, [tl['hb']], [ht])
            else:
                cx.dma(ht[0:C, :], tl['src'], [tl['hb']], [ht])
            bcast(0, C, PD)
            rms_norm_ps(cx, C, ht, PD[0:C, :], PD, xn, yjunk, ss)
            cx.copy(xnb[0:C, :], xn[0:C, :], [xn], [xnb], eng='pool')
            PAb = PA[:].bitcast(BF16)
            for kc in range(8):
                cx.tr(PAb[:, kc * 128:kc * 128 + C], xnb[0:C, kc * 128:(kc + 1) * 128], identb[0:C, 0:C], [xnb, identb], [PA])
            cx.copy(xnT[:, :, 0:C], PAb[:, 0:1024].rearrange("p (a b) -> p a b", b=128)[:, :, 0:C], [PA], [xnT])

        def zgate(C, Win, j):
            pp = (PA, PB)[j % 2]
            for kc in range(8):
                cx.mm(pp[0:C, 0:512], xnT[:, kc, 0:C], Win[:, kc, ZO + j * 512:ZO + (j + 1) * 512], kc == 0, kc == 7, [xnT, Win], [pp])
            cx.act(tmp[0:C, :], pp[0:C, 0:512], AF.Silu, [pp], [tmp])

        def back(tl, Win, zsrc=None, zb=None):
            C = tl['C']
            for j in range(4):
                if zsrc is None:
                    zgate(C, Win, j)
                else:
                    cx.dma(tmp[0:C, :], zsrc[:, j * 512:(j + 1) * 512], [zb], [tmp])
                cx.tt(y_[0:C, j * 512:(j + 1) * 512], y_[0:C, j * 512:(j + 1) * 512], tmp[0:C, :], ALU.mult, [y_, tmp], [y_])
                cx.act(tmp[0:C, :], y_[0:C, j * 512:(j + 1) * 512], AF.Square, [y_], [tmp, ss], accum_out=ss[0:C, j:j + 1])
            cx.ts(ss[0:C, 4:8], ss[0:C, 0:4], 1.0 / 512, ALU.mult, [ss], [ss], s2=1e-5, op1=ALU.add)
            cx.act(ss[0:C, 4:8], ss[0:C, 4:8], AF.Sqrt, [ss], [ss])
            cx.op('dve', lambda e: e.reciprocal(out=ss[0:C, 4:8], in_=ss[0:C, 4:8]), [ss], [ss])
            for hf in range(2):
                bcast(32 + 32 * hf, C, PC)
                for j2 in range(2):
                    j = hf * 2 + j2
                    cx.stt(ygb[0:C, j * 512:(j + 1) * 512], y_[0:C, j * 512:(j + 1) * 512], ss[0:C, 4 + j:5 + j], PC[0:C, j2 * 512:(j2 + 1) * 512],
                           ALU.mult, ALU.mult, [y_, ss, PC], [ygb])
            PBb = PB[:].bitcast(BF16)
            for kc in range(16):
                cx.tr(PBb[:, kc * 128:kc * 128 + C], ygb[0:C, kc * 128:(kc + 1) * 128], identb[0:C, 0:C], [ygb, identb], [PB])
            cx.copy(ygT[:, :, 0:C], PBb[:, 0:2048].rearrange("p (a b) -> p a b", b=128)[:, :, 0:C], [PB], [ygT])
            for hf in range(2):
                for kc in range(16):
                    cx.mm(PA[0:C, hf * 512:(hf + 1) * 512], ygT[:, kc, 0:C], Wout[:, kc, hf * 512:(hf + 1) * 512], kc == 0, kc == 15, [ygT, Wout], [PA])
            cx.tt(ht[0:C, :], PA[0:C, :], ht[0:C, :], ALU.add, [PA, ht], [ht])
            cx.dma(tl['dst'], ht[0:C, :], [ht], [tl['hb']])

        with cx.scope():
            Win = cx.sb("mb_Win", [128, 8, 5152], BF16)
            cwv = cx.sb("mb_cwv", [128, 24, 8])
            with cx.scope():
                stg = [cx.sb(f"mb_stg{i}", [128, 1288], F32) for i in range(2)]
                cnt = 0
                for kc in range(8):
                    for cc in range(4):
                        st = stg[cnt % 2]
                        cx.dma(st[:, :], W['mamba_in_proj'][0, kc * 128:(kc + 1) * 128, cc * 1288:(cc + 1) * 1288], [], [st])
                        cx.copy(Win[:, kc, cc * 1288:(cc + 1) * 1288], st[:, :], [st], [Win], eng=('dve', 'pool', 'act')[cnt % 3])
                        cnt += 1
                for kc in range(16):
                    st = stg[cnt % 2]
                    cx.dma(st[:, 0:1024], W['mamba_out_proj'][0, kc * 128:(kc + 1) * 128, :], [], [st])
                    cx.copy(Wout[:, kc, :], st[:, 0:1024], [st], [Wout], eng=('dve', 'pool', 'act')[cnt % 3])
                    cnt += 1
                for c3 in range(3):
                    st = stg[cnt % 2]
                    cnt += 1
                    cx.memset(st[0:32, 0:1024], 0.0, [st])
                    cx.dma(st[0:4, 0:1024], W['mamba_conv_w'][0, :, c3 * 1024:(c3 + 1) * 1024], [], [st])
                    cx.dma(st[4:5, 0:1024], W['mamba_conv_b'][0:1, c3 * 1024:(c3 + 1) * 1024], [], [st])
                    for c8 in range(8):
                        cx.mm(PA[:, c8 * 32:(c8 + 1) * 32], st[0:32, c8 * 128:(c8 + 1) * 128], ident[0:32, 0:32], True, True, [st, ident], [PA])
                    cx.copy(cwv[:, c3 * 8:(c3 + 1) * 8, :], PA[:, 0:256].rearrange("p (a b) -> p a b", b=32)[:, :, 0:8], [PA], [cwv])

            cst3 = cx.sb("mb_cst3", [128, 24, 3])
            full = [cx.sb(f"mb_full{i}", [128, 132]) for i in range(2)]
            acc = [cx.sb(f"mb_acc{i}", [128, 128]) for i in range(2)]
            xbcT = cx.sb("mb_xbcT", [128, 24, 128], BF16)
            dtt = cx.sb("mb_dtt", [128, 160])
            ncv = cx.sb("mb_ncv", [64, 1024])

            def conv_part(tl, sh, Cst):
                C = tl['C']
                for ct in range(24):
                    pp = (PA, PB)[ct % 2]
                    fu = full[ct % 2]
                    ac = acc[ct % 2]
                    for kc in range(8):
                        cx.mm(pp[:, 0:C], Win[:, kc, XO + ct * 128:XO + (ct + 1) * 128], xnT[:, kc, 0:C], kc == 0, kc == 7, [Win, xnT], [pp])
                    if sh == 1:
                        cx.copy(fu[:, 0:3], cst3[:, ct, :], [cst3], [fu], eng='pool')
                        cx.copy(fu[:, 3:3 + C], pp[:, 0:C], [pp], [fu], eng='act')
                        cx.copy(cst3[:, ct, :], fu[:, C:C + 3], [fu], [cst3], eng='pool')
                        for j in range(4):
                            if j == 0:
                                cx.ts(ac[:, 0:C], fu[:, 0:C], cwv[:, ct, 0:1], ALU.mult, [fu, cwv], [ac], s2=cwv[:, ct, 4:5], op1=ALU.add)
                            else:
                                cx.stt(ac[:, 0:C], fu[:, j:j + C], cwv[:, ct, j:j + 1], ac[:, 0:C], ALU.mult, ALU.add, [fu, cwv, ac], [ac])
                    else:
                        cx.copy(fu[:, 0:48], Cst[:, ct, :, :].rearrange("p a b -> p (a b)"), [Cst], [fu], eng='pool')
                        cx.copy(fu[:, 48:48 + C], pp[:, 0:C], [pp], [fu], eng='act')
                        for j in range(4):
                            if j == 0:
                                cx.ts(ac[:, 0:C], fu[:, 0:C], cwv[:, ct, 0:1], ALU.mult, [fu, cwv], [ac], s2=cwv[:, ct, 4:5], op1=ALU.add)
                            else:
                                cx.stt(ac[:, 0:C], fu[:, 16 * j:16 * j + C], cwv[:, ct, j:j + 1], ac[:, 0:C], ALU.mult, ALU.add, [fu, cwv, ac], [ac])
                    cx.act(xbcT[:, ct, 0:C], ac[:, 0:C], AF.Silu, [ac], [xbcT])

            def dt_part(tl):
                C = tl['C']
                for kc in range(8):
                    cx.mm(PC[0:C, 0:32], xnT[:, kc, 0:C], Win[:, kc, DO:DO + 32], kc == 0, kc == 7, [xnT, Win], [PC])
                cx.tt(dtt[0:C, 0:32], PC[0:C, 0:32], vec[0:C, 0:32], ALU.add, [PC, vec], [dtt])
                cx.act(dtt[0:C, 0:32], dtt[0:C, 0:32], AF.Exp, [dtt], [dtt])
                cx.act(dtt[0:C, 0:32], dtt[0:C, 0:32], AF.Ln, [dtt], [dtt], bias=ones[0:C, 0:1])
                cx.tt(dtt[0:C, 32:64], dtt[0:C, 0:32], vec[0:C, 32:64], ALU.mult, [dtt, vec], [dtt])

            def x_tok(tl):
                C = tl['C']
                PBb = PB[:].bitcast(BF16)
                for ct in range(16):
                    cx.tr(PBb[0:C, ct * 128:(ct + 1) * 128], xbcT[:, ct, 0:C], identb[:, :], [xbcT, identb], [PB])
                cx.copy(xs[0:C, :], PBb[0:C, 0:2048], [PB], [xs])

            def newconv_rows(tl, r0, nrows, dsts):
                for c3 in range(3):
                    for c2 in range(2):
                        c6 = c3 * 2 + c2
                        pp = (PC, PD)[c6 % 2]
                        for kc in range(8):
                            cx.mm(pp[0:nrows, 0:512], xnT[:, kc, r0:r0 + nrows], Win[:, kc, XO + c6 * 512:XO + (c6 + 1) * 512], kc == 0, kc == 7, [xnT, Win], [pp])
                        cx.copy(ncv[0:nrows, c2 * 512:(c2 + 1) * 512], pp[0:nrows, 0:512], [pp], [ncv], eng=('act', 'dve')[c6 % 2])
                    for (ap, a, b) in dsts:
                        cx.dma(ap[:, c3 * 1024:(c3 + 1) * 1024], ncv[a:b, :], [ncv], [])

            with cx.scope():
                Hs = cx.sb("mb_H", [128, 2048])
                Hb = cx.sb("mb_Hb", [128, 2048], BF16)
                xdt = cx.sb("mb_xdt", [128, 2048], BF16)
                xdtd = cx.sb("mb_xdtd", [128, 2048], BF16)
                Btok = cx.sb("mb_Btok", [128, 512], BF16)
                cbm = cx.sb("mb_cbm", [128, 4, 128], BF16)
                acsT = cx.sb("mb_acsT", [32, 128])
                LT = cx.sb("mb_LT", [128, 512])
                MT = cx.sb("mb_MT", [128, 4, 128], BF16)
                cdb = cx.sb("mb_cdb", [128, 32])
                cx.memset(Hs[:], 0.0, [Hs])
                cx.memset(Hb[:], 0.0, [Hb], eng='pool')
                cx.memset(cst3[:], 0.0, [cst3])
                for ti, tl in enumerate(tiles):
                    C = tl['C']
                    cx.S.mark(f'mb_tile{ti}')
                    norm_in(tl)
                    conv_part(tl, 1, None)
                    if ti == len(tiles) - 1:
                        newconv_rows(tl, C - 4, 4, [(O['p_conv'], 1, 4)])
                    dt_part(tl)
                    x_tok(tl)
                    PBb = PB[:].bitcast(BF16)
                    for g in range(4):
                        cx.tr(PBb[0:C, g * 128:(g + 1) * 128], xbcT[:, 16 + g, 0:C], identb[:, :], [xbcT, identb], [PB])
                    cx.copy(Btok[0:C, :], PBb[0:C, 0:512], [PB], [Btok])
                    cx.mm(PC[0:C, 0:32], triu[0:C, 0:C], dtt[0:C, 32:64], True, True, [triu, dtt], [PC])
                    cx.copy(dtt[0:C, 64:96], PC[0:C, 0:32], [PC], [dtt])
                    cx.mm(PC[:, 32:64], ones[0:C, :], dtt[0:C, 32:64], True, True, [ones, dtt], [PC])
                    cx.act(cdb[:, :], PC[:, 32:64], AF.Exp, [PC], [cdb])
                    cx.tt(dtt[0:C, 128:160], PC[0:C, 32:64], dtt[0:C, 64:96], ALU.subtract, [PC, dtt], [dtt])
                    cx.act(dtt[0:C, 128:160], dtt[0:C, 128:160], AF.Exp, [dtt], [dtt])
                    cx.act(dtt[0:C, 96:128], dtt[0:C, 64:96], AF.Exp, [dtt], [dtt])
                    x3 = xs[0:C, :].rearrange("p (h q) -> p h q", q=64)
                    cx.tt(xdt[0:C, :].rearrange("p (h q) -> p h q", q=64), x3, dtt[0:C, 0:32].unsqueeze(2).to_broadcast([C, 32, 64]), ALU.mult, [xs, dtt], [xdt])
                    cx.tt(xdtd[0:C, :].rearrange("p (h q) -> p h q", q=64), xdt[0:C, :].rearrange("p (h q) -> p h q", q=64),
                          dtt[0:C, 128:160].unsqueeze(2).to_broadcast([C, 32, 64]), ALU.mult, [xdt, dtt], [xdtd], eng='pool')
                    cx.mm(PC[0:32, 64:64 + C], dtt[0:C, 64:96], ident[0:C, 0:C], True, True, [dtt, ident], [PC])
                    cx.copy(acsT[:, 0:C], PC[0:32, 64:64 + C], [PC], [acsT])
                    for g in range(4):
                        cx.mm(PD[0:C, g * 128:g * 128 + C], xbcT[:, 16 + g, 0:C], xbcT[:, 20 + g, 0:C], True, True, [xbcT], [PD])
                    cx.tt(cbm[0:C, :, 0:C], PD[0:C, 0:512].rearrange("p (g c) -> p g c", g=4)[:, :, 0:C], triu[0:C, 0:C].unsqueeze(1).to_broadcast([C, 4, C]), ALU.mult, [PD, triu], [cbm])
                    for g in range(4):
                        pY = (PA, PB)[g % 2]
                        for h4 in range(2):
                            h0 = g * 8 + h4 * 4
                            for q in range(4):
                                h = h0 + q
                                cx.mm(PC[0:C, 512 + q * 128:512 + q * 128 + C], ident[0:32, h:h + 1].to_broadcast([32, C]), acsT[:, 0:C], True, True, [ident, acsT], [PC])
                            L3 = LT[0:C, :].rearrange("p (q c) -> p q c", q=4)[:, :, 0:C]
                            cx.tt(L3, PC[0:C, 512:1024].rearrange("p (q c) -> p q c", q=4)[:, :, 0:C],
                                  dtt[0:C, 64 + h0:64 + h0 + 4].unsqueeze(2).to_broadcast([C, 4, C]), ALU.subtract, [PC, dtt], [LT])
                            cx.ts(L3, L3, 0.0, ALU.min, [LT], [LT])
                            cx.act(L3, L3, AF.Exp, [LT], [LT])
                            cx.tt(MT[0:C, :, 0:C], L3, cbm[0:C, g:g + 1, 0:C].to_broadcast([C, 4, C]), ALU.mult, [LT, cbm], [MT])
                            for q in range(4):
                                h = h0 + q
                                cx.mm(pY[0:C, (h4 * 4 + q) * 64:(h4 * 4 + q + 1) * 64], MT[0:C, q, 0:C], xdt[0:C, h * 64:(h + 1) * 64], True, True, [MT, xdt], [pY])
                        cx.mm(pY[0:C, 512:1024], xbcT[:, 20 + g, 0:C], Hb[:, g * 512:(g + 1) * 512], True, True, [xbcT, Hb], [pY])
                        cx.tt(tmp[0:C, :].rearrange("p (h q) -> p h q", q=64), pY[0:C, 512:1024].rearrange("p (h q) -> p h q", q=64),
                              dtt[0:C, 96 + g * 8:96 + g * 8 + 8].unsqueeze(2).to_broadcast([C, 8, 64]), ALU.mult, [pY, dtt], [tmp])
                        cx.tt(y_[0:C, g * 512:(g + 1) * 512], pY[0:C, 0:512], tmp[0:C, :], ALU.add, [pY, tmp], [y_])
                    for g in range(4):
                        pp = (PC, PD)[g % 2]
                        cx.mm(pp[:, 0:512], Btok[0:C, g * 128:(g + 1) * 128], xdtd[0:C, g * 512:(g + 1) * 512], True, True, [Btok, xdtd], [pp])
                        H3 = Hs[:, g * 512:(g + 1) * 512].rearrange("p (h q) -> p h q", q=64)
                        cx.tt(H3, H3, cdb[:, g * 8:(g + 1) * 8].unsqueeze(2).to_broadcast([128, 8, 64]), ALU.mult, [Hs, cdb], [Hs])
                        cx.tt(Hs[:, g * 512:(g + 1) * 512], Hs[:, g * 512:(g + 1) * 512], pp[:, 0:512], ALU.add, [Hs, pp], [Hs])
                    cx.copy(Hb[:], Hs[:], [Hs], [Hb], eng='pool')
                    for g in range(4):
                        cx.tt(tmp[0:C, :].rearrange("p (h q) -> p h q", q=64), xs[0:C, g * 512:(g + 1) * 512].rearrange("p (h q) -> p h q", q=64),
                              vec[0:C, 64 + g * 8:64 + g * 8 + 8].unsqueeze(2).to_broadcast([C, 8, 64]), ALU.mult, [xs, vec], [tmp])
                        cx.tt(y_[0:C, g * 512:(g + 1) * 512], y_[0:C, g * 512:(g + 1) * 512], tmp[0:C, :], ALU.add, [y_, tmp], [y_])
                    back(tl, Win)
                HTo = alias(y_, y_[:, :], "mb_HTo")
                for rnd in range(8):
                    for q in range(2):
                        c16 = rnd * 2 + q
                        pp = (PA, PB)[q]
                        cx.tr(pp[:, 0:128], Hs[:, c16 * 128:(c16 + 1) * 128], ident[:, :], [Hs, ident], [pp])
                        cx.copy(HTo[:, c16 * 128:(c16 + 1) * 128], pp[:, 0:128], [pp], [HTo], eng=('dve', 'act')[q])
                cx.dma(O['p_ssm'].rearrange("(c hl) p n -> (hl p) c n", hl=2), HTo[:, :].rearrange("q (c n) -> q c n", n=128), [HTo], [])

            cx.S.mark('mb_sample_front')
            C = 64
            Cst = cx.sb("mb_Cst", [128, 24, 3, 16])
            for c3 in range(3):
                for j in range(3):
                    cx.dma(ncv[16 * j:16 * j + 16, :], smp['conv0'][:, j, c3 * 1024:(c3 + 1) * 1024], [], [ncv])
                for c8 in range(8):
                    ct = c3 * 8 + c8
                    pp = (PA, PB)[ct % 2]
                    cx.tr(pp[:, 0:48], ncv[0:48, c8 * 128:(c8 + 1) * 128], ident[0:48, 0:48], [ncv, ident], [pp])
                    cx.copy(Cst[:, ct, :, :].rearrange("p a b -> p (a b)"), pp[:, 0:48], [pp], [Cst], eng=('dve', 'act')[ct % 2])
            norm_in(smp)
            conv_part(smp, 16, Cst)
            newconv_rows(smp, 0, 64, [(smp['conv_out'][:, t - 1, :], 16 * t, 16 * t + 16) for t in range(1, 4)])
            dt_part(smp)
            x_tok(smp)
            SD = smp['scr']
            sdb = Buf("mb_sd")
            bct = alias(ygT, ygT[:].rearrange("p a b -> p (a b)")[:, 0:1024], "mb_bct")
            PBb = PB[:].bitcast(BF16)
            for g in range(8):
                cx.tr(PBb[0:C, g * 128:(g + 1) * 128], xbcT[:, 16 + g, 0:C], identb[:, :], [xbcT, identb], [PB])
            cx.copy(bct[0:C, 0:1024], PBb[0:C, 0:1024], [PB], [bct])
            cx.act(dtt[0:C, 64:96], dtt[0:C, 32:64], AF.Exp, [dtt], [dtt])
            cx.tt(y_[0:C, :].rearrange("p (h q) -> p h q", q=64), xs[0:C, :].rearrange("p (h q) -> p h q", q=64),
                  dtt[0:C, 0:32].unsqueeze(2).to_broadcast([C, 32, 64]), ALU.mult, [xs, dtt], [y_])
            cx.dma(SD['dtx'], y_[0:C, :], [y_], [sdb])
            cx.dma(SD['da'], dtt[0:C, 64:96], [dtt], [sdb])
            for z, nm in enumerate(('b', 'c')):
                cx.copy(tmp[0:C, :], bct[0:C, z * 512:(z + 1) * 512], [bct], [tmp])
                cx.dma(SD[nm], tmp[0:C, :], [tmp], [sdb])
            for j in range(4):
                zgate(C, Win, j)
                cx.dma(SD['z'][:, j * 512:(j + 1) * 512], tmp[0:C, :], [tmp], [sdb])

        cx.S.mark('mb_sample_rec')
        with cx.scope():
            Sst = cx.sb("mb_S", [128, 8192])
            tmpS = cx.sb("mb_tmpS", [128, 8192])
            dtx = cx.sb("mb_dtx", [128, 4, 64])
            dA = cx.sb("mb_dA", [128, 4])
            bcs = cx.sb("mb_bcs", [16, 4, 256])
            cx.memset(bcs[:], 0.0, [bcs])
            BC = cx.sb("mb_BC", [128, 4, 256])
            ysm = cx.sb("mb_ysm", [128, 4, 64])
            E = cx.sb("mb_E", [16, 128])
            ie = cx.sb("mb_ie", [16, 128], I32)
            e2 = cx.sb("mb_e2", [16, 128])
            cx.op('pool', lambda e: e.iota(ie[:], pattern=[[1, 128]], base=0, channel_multiplier=-8), [], [ie])
            cx.op('dve', lambda e: e.tensor_single_scalar(out=E[:], in_=ie[:], scalar=0, op=ALU.is_ge), [ie], [E])
            cx.op('dve', lambda e: e.tensor_single_scalar(out=e2[:], in_=ie[:], scalar=8, op=ALU.is_lt), [ie], [e2])
            cx.tt(E[:], E[:], e2[:], ALU.mult, [E, e2], [E])
            S3 = Sst[:].rearrange("p (q n) -> p q n", n=128)
            T3 = tmpS[:].rearrange("p (q n) -> p q n", n=128)
            for qb in range(4):
                b0 = qb * 4
                cx.dma(Sst[:], smp['ssm0'][b0:b0 + 4].rearrange("b h p n -> (b h) (p n)"), [], [Sst])
                cx.dma(dtx[:], SD['dtx'].rearrange("(t b) (h p) -> b h t p", b=16, p=64)[b0:b0 + 4].rearrange("b h t p -> (b h) t p"), [sdb], [dtx])
                cx.dma(dA[:], SD['da'].rearrange("(t b) h -> b h t", b=16)[b0:b0 + 4].rearrange("b h t -> (b h) t"), [sdb], [dA], allow_slow_non_contiguous=True)
                for z, nm in enumerate(('b', 'c')):
                    cx.dma(bcs[:, :, z * 128:(z + 1) * 128],
                           SD[nm].rearrange("(t b) (g n) -> b g t n", b=16, g=4)[b0:b0 + 4].rearrange("b g t n -> (b g) t n"), [sdb], [bcs])
                for t2 in range(2):
                    cx.mm(PA[:, t2 * 512:(t2 + 1) * 512], E[:, :], bcs[:, 2 * t2:2 * t2 + 2, :].rearrange("p a b -> p (a b)"), True, True, [E, bcs], [PA])
                cx.copy(BC[:].rearrange("p a b -> p (a b)"), PA[:, :], [PA], [BC])
                for t in range(4):
                    cx.ts(Sst[:], Sst[:], dA[:, t:t + 1], ALU.mult, [Sst, dA], [Sst])
                    cx.tt(T3, dtx[:, t, :].unsqueeze(2).to_broadcast([128, 64, 128]), BC[:, t, 0:128].unsqueeze(1).to_broadcast([128, 64, 128]), ALU.mult, [dtx, BC], [tmpS])
                    cx.tt(Sst[:], Sst[:], tmpS[:], ALU.add, [Sst, tmpS], [Sst])
                    cx.tt(T3, S3, BC[:, t, 128:256].unsqueeze(1).to_broadcast([128, 64, 128]), ALU.mult, [Sst, BC], [tmpS])
                    cx.red(ysm[:, t, :], T3, ALU.add, [tmpS], [ysm])
                cx.dma(O['s_ssm'][b0:b0 + 4].rearrange("b h p n -> (b h) (p n)"), Sst[:], [Sst], [])
                cx.dma(SD['y'].rearrange("(t b) (h p) -> b h t p", b=16, p=64)[b0:b0 + 4].rearrange("b h t p -> (b h) t p"), ysm[:], [ysm], [sdb])
            cx.dma(y_[0:64, :], SD['y'], [sdb], [y_])
            C = 64
            for g in range(4):
                cx.tt(tmp[0:C, :].rearrange("p (h q) -> p h q", q=64), xs[0:C, g * 512:(g + 1) * 512].rearrange("p (h q) -> p h q", q=64),
                      vec[0:C, 64 + g * 8:64 + g * 8 + 8].unsqueeze(2).to_broadcast([C, 8, 64]), ALU.mult, [xs, vec], [tmp])
                cx.tt(y_[0:C, g * 512:(g + 1) * 512], y_[0:C, g * 512:(g + 1) * 512], tmp[0:C, :], ALU.add, [y_, tmp], [y_])
            back(smp, None, SD['z'], sdb)


def convert_tables(cx, W):
    with cx.scope():
        stg = [cx.sb(f"cv_stg{i}", [128, 8192], F32) for i in range(2)]
        ob = [cx.sb(f"cv_ob{i}", [128, 8192], BF16) for i in range(3)]
        n = 0
        for l in range(2):
            for z, nm in enumerate(('peer_u', 'peer_v')):
                for c in range(16):
                    st, o = stg[n % 2], ob[n % 3]
                    cx.dma(st[:], W[nm][l, c * 1024:(c + 1) * 1024, :].rearrange("(p r) d -> p (r d)", r=8), [], [st])
                    cx.copy(o[:], st[:], [st], [o], eng=('act', 'dve', 'pool')[n % 3])
                    r0 = (l * 16 + c) * 1024
                    cx.dma(W['tab_uv'][r0:r0 + 1024, z * 1024:(z + 1) * 1024].rearrange("(p r) d -> p r d", r=8),
                           o[:].rearrange("p (r d) -> p r d", r=8), [o], [W['tabbuf']])
                    n += 1


W_SHAPES = {
    'meta_tokens': [16, 1024], 'norm_mix': [2, 1024], 'norm_ffn': [2, 1024], 'norm_final': [1024],
    'rwkv_mix': [1, 6, 1024], 'rwkv_w_rkv': [1, 3, 1024, 1024], 'rwkv_w0': [1, 1024], 'rwkv_w1': [1, 1024, 64],
    'rwkv_w2': [1, 64, 1024], 'rwkv_a0': [1, 1024], 'rwkv_a1': [1, 1024, 64], 'rwkv_a2': [1, 64, 1024],
    'rwkv_g1': [1, 1024, 160], 'rwkv_g2': [1, 160, 1024], 'rwkv_k_k': [1, 1024], 'rwkv_k_a': [1, 1024],
    'rwkv_r_k': [1, 16, 64], 'rwkv_ln_w': [1, 1024], 'rwkv_ln_b': [1, 1024], 'rwkv_w_o': [1, 1024, 1024],
    'mamba_in_proj': [1, 1024, 5152], 'mamba_conv_w': [1, 4, 3072], 'mamba_conv_b': [1, 3072], 'mamba_dt_bias': [1, 32],
    'mamba_a_log': [1, 32], 'mamba_d': [1, 32], 'mamba_norm_w': [1, 2048], 'mamba_out_proj': [1, 2048, 1024],
    'peer_w_q': [2, 1024, 2048], 'peer_sub_keys': [2, 2, 8, 128, 128], 'peer_u': [2, 16384, 1024], 'peer_v': [2, 16384, 1024],
}
IN_SHAPES = {
    'xp': [2048, 1024], 'xs': [16, 4, 1024], 'sh0': [16, 1024], 'wkv0': [16, 16, 64, 64],
    'conv0': [16, 3, 3072], 'ssm0': [16, 32, 64, 128],
}
OUT_SHAPES = {
    'y_p': [2048, 1024], 'y_s': [16, 4, 1024], 'p_shift': [1, 1024], 'p_wkv': [16, 64, 64], 'p_conv': [3, 3072],
    'p_ssm': [32, 64, 128], 's_shift': [16, 1024], 's_wkv': [16, 16, 64, 64], 's_conv': [16, 3, 3072], 's_ssm': [16, 32, 64, 128],
}
NPT = 16


def build_program(npt=NPT, debug=False):
    nc = bass.Bass("TRN2", target_bir_lowering=False)
    W = {n: nc.dram_tensor(n, s, F32, kind="ExternalInput").ap() for n, s in W_SHAPES.items()}
    I = {n: nc.dram_tensor(n, s, F32, kind="ExternalInput").ap() for n, s in IN_SHAPES.items()}
    Od = {n: nc.dram_tensor(n, s, F32, kind="ExternalOutput").ap() for n, s in OUT_SHAPES.items()}
    hscr = nc.dram_tensor("hscr", [npt + 2, 128, 1024], F32, kind="Internal").ap()
    rw_scr = nc.dram_tensor("rw_scr", [7, 16, 8, 4, 128], F32, kind="Internal").ap()
    SD = {'dtx': nc.dram_tensor("sd_dtx", [64, 2048], F32, kind="Internal").ap(), 'da': nc.dram_tensor("sd_da", [64, 32], F32, kind="Internal").ap(),
          'b': nc.dram_tensor("sd_b", [64, 512], F32, kind="Internal").ap(), 'c': nc.dram_tensor("sd_c", [64, 512], F32, kind="Internal").ap(),
          'z': nc.dram_tensor("sd_z", [64, 2048], F32, kind="Internal").ap(), 'y': nc.dram_tensor("sd_y", [64, 2048], F32, kind="Internal").ap()}
    dbg = nc.dram_tensor("dbg_h", [3, npt + 2, 128, 1024], F32, kind="ExternalOutput").ap() if debug else None
    W['tab_uv'] = nc.dram_tensor("tab_uv", [2 * 16384, 2048], BF16, kind="Internal").ap()
    W['tabbuf'] = Buf("tabbuf")
    with ExitStack() as es:
        cx = Ctx(nc, es)
        cst = make_consts(cx)
        convert_tables(cx, W)
        hb = [Buf(f"h{i}") for i in range(npt + 2)]

        def snap(k):
            if debug:
                for i in range(npt + 2):
                    cx.dma(dbg[k, i], hscr[i], [hb[i]], [])

        Cs = [16] + [128] * npt
        first = [dict(C=16, src=W['meta_tokens'][:, :], dst=hscr[0, 0:16, :], hb=hb[0])]
        for i in range(npt):
            first.append(dict(C=128, src=I['xp'][i * 128:(i + 1) * 128, :], dst=hscr[i + 1, :, :], hb=hb[i + 1]))
        smp = dict(C=64, src=[(I['xs'][:, t, :], 16 * t, 16 * t + 16) for t in range(4)], dst=hscr[npt + 1, 0:64, :], hb=hb[npt + 1],
                   shift0=I['sh0'], wkv0=I['wkv0'], scr=rw_scr)
        rwkv_pass(cx, cst, W, first, smp, dict(p_shift=Od['p_shift'], s_shift=Od['s_shift'], p_wkv=Od['p_wkv'], s_wkv=Od['s_wkv']))
        inpl = [dict(C=Cs[i], src=hscr[i, 0:Cs[i], :], dst=hscr[i, 0:Cs[i], :], hb=hb[i]) for i in range(npt + 1)]
        inpl.append(dict(C=64, src=hscr[npt + 1, 0:64, :], dst=hscr[npt + 1, 0:64, :], hb=hb[npt + 1]))

        dbgp = nc.dram_tensor("dbg_p", [10, 128, 128], F32, kind="ExternalOutput").ap() if debug else None

        def peer(layer, gfin):
            with cx.scope():
                P = {'a': cx.ps("pa", [128, 512]), 'b': cx.ps("pb", [128, 512]), 'x0': cx.ps("px0", [128, 1024]),
                     'x1': cx.ps("px1", [128, 1024]), 'y': cx.ps("py", [128, 1024])}
                tl = [dict(t) for t in inpl]
                if gfin:
                    tl[0]['fin'] = []
                    for i in range(npt):
                        tl[i + 1]['fin'] = [(Od['y_p'][i * 128:(i + 1) * 128, :], 0, 128)]
                    tl[npt + 1]['fin'] = [(Od['y_s'][:, t, :], 16 * t, 16 * t + 16) for t in range(4)]
                peer_pass(cx, cst, W, layer, tl, P, gfin=(True if gfin else None), dbgp=(dbgp if layer == 0 else None))

        snap(0)
        peer(0, False)
        snap(1)
        smp2 = dict(C=64, src=hscr[npt + 1, 0:64, :], dst=hscr[npt + 1, 0:64, :], hb=hb[npt + 1],
                    conv0=I['conv0'], ssm0=I['ssm0'], scr=SD, conv_out=Od['s_conv'])
        mamba_pass(cx, cst, W, inpl[:npt + 1], smp2, dict(p_conv=Od['p_conv'], p_ssm=Od['p_ssm'], s_ssm=Od['s_ssm']))
        snap(2)
        peer(1, True)
        cx.S.finish()
        block = es.enter_context(nc.Block())
        cx.S.emit(block)
    return nc


_NC_CACHE = {}


def kernel(**inputs):
    inputs = {k: np.ascontiguousarray(np.asarray(v), dtype=np.float32) for k, v in inputs.items()}
    if 'nc' not in _NC_CACHE:
        _NC_CACHE['nc'] = build_program()
    nc = _NC_CACHE['nc']
    in_maps = []
    for c in range(NCORES):
        m = {n: inputs[n] for n in W_SHAPES}
        m['xp'] = inputs['x_prompt'][c]
        m['xs'] = inputs['x_sample'][16 * c:16 * c + 16]
        m['sh0'] = inputs['state_rwkv_shift'][0, 16 * c:16 * c + 16]
        m['wkv0'] = inputs['state_rwkv_wkv'][0, 16 * c:16 * c + 16]
        m['conv0'] = inputs['state_mamba_conv'][0, 16 * c:16 * c + 16]
        m['ssm0'] = inputs['state_mamba_ssm'][0, 16 * c:16 * c + 16]
        in_maps.append(m)
    res = run_bass_kernel_spmd(nc, in_maps, core_ids=list(range(NCORES)))
    R = res.results
    cat = lambda k: np.concatenate([R[c][k] for c in range(NCORES)], axis=0)
    stk = lambda k: np.stack([R[c][k] for c in range(NCORES)], axis=0)
    y_p = stk('y_p')
    y_s = cat('y_s')
    p_shift = cat('p_shift')[None]
    p_wkv = stk('p_wkv')[None]
    p_conv = stk('p_conv')[None]
    p_ssm = stk('p_ssm')[None]
    s_shift = cat('s_shift')[None]
    s_wkv = cat('s_wkv')[None]
    s_conv = cat('s_conv')[None]
    s_ssm = cat('s_ssm')[None]
    return (y_p, y_s, p_shift, p_wkv, p_conv, p_ssm, s_shift, s_wkv, s_conv, s_ssm)
```

```python
from contextlib import ExitStack
import os
import numpy as np
import concourse.bass as bass
import concourse.mybir as mybir
from concourse.bass_utils import run_bass_kernel_spmd

F32 = mybir.dt.float32
BF16 = mybir.dt.bfloat16
U32 = mybir.dt.uint32
I32 = mybir.dt.int32
AF = mybir.ActivationFunctionType
ALU = mybir.AluOpType
AX = mybir.AxisListType

D = 1024
NCORES = 8
NEG = -1.0e30

ENGS = ['pe', 'dve', 'act', 'pool', 'sp']
EPOCH = 16384
NEPOCH = {'pe': 7, 'dve': 7, 'act': 4, 'pool': 4, 'sp': 1}
NDSEM = 72
SELFSYNC = os.environ.get("K_SELFSYNC", "1") == "1"


class Buf:
    __slots__ = ('name', 'lw', 'rc', 'rd', 'excl')

    def __init__(self, name):
        self.name = name
        self.excl = False
        self.lw = None
        self.rc = {}
        self.rd = {}


class Sched:
    def __init__(self, nc, es):
        self.nc = nc
        self.q = {e: [] for e in ENGS}
        self.cnt = {e: 0 for e in ENGS}
        self.known = {e: {f: 0 for f in ENGS} for e in ENGS}
        self.sems = {e: [es.enter_context(nc.semaphore(f"s_{e}_{i}")) for i in range(NEPOCH[e])] for e in ENGS}
        self.dsems = [es.enter_context(nc.semaphore(f"sd_{i}")) for i in range(NDSEM)]
        self.dcount = [0] * NDSEM
        self.dknown = {e: [0] * NDSEM for e in ENGS}
        self.dnext = 0
        self.n = 0
        self.limit = int(os.environ.get('K_LIMIT', '0'))
        self.phase = ''
        self.marks = []

    def mark(self, name):
        self.marks.append((self.n, name))

    def op(self, eng, fn, reads=(), writes=(), dma=False):
        self.n += 1
        if self.limit and self.n > self.limit:
            return None
        ex = [b for b in reads if b.excl]
        if ex:
            reads = [b for b in reads if not b.excl]
            writes = list(writes) + [b for b in ex if b not in writes]
        deps = []
        for b in list(reads) + list(writes):
            if b.lw is not None:
                deps.append(b.lw)
        for b in writes:
            for e2, seq in b.rc.items():
                deps.append(('c', e2, seq))
            for si, tgt in b.rd.items():
                deps.append(('d', si, tgt))
        waits = []
        for d in deps:
            if d[0] == 'c':
                _, e2, seq = d
                if e2 == eng and not dma and (eng == 'pe' or not SELFSYNC):
                    continue
                if self.known[eng][e2] >= seq:
                    continue
                self.known[eng][e2] = seq
                waits.append((self.sems[e2][(seq - 1) // EPOCH], (seq - 1) % EPOCH + 1))
            else:
                _, si, tgt = d
                if self.dknown[eng][si] >= tgt:
                    continue
                self.dknown[eng][si] = tgt
                waits.append((self.dsems[si], tgt))
        if dma:
            si = self.dnext
            self.dnext = (self.dnext + 1) % NDSEM
            prev = self.dcount[si]
            if prev > 0 and self.dknown[eng][si] < prev:
                self.dknown[eng][si] = prev
                waits.append((self.dsems[si], prev))
            self.dcount[si] = prev + 16
            tok = ('d', si, prev + 16)
            inc = (self.dsems[si], 16)
        else:
            self.cnt[eng] += 1
            seq = self.cnt[eng]
            assert seq <= EPOCH * NEPOCH[eng]
            tok = ('c', eng, seq)
            inc = (self.sems[eng][(seq - 1) // EPOCH], 1)
        for b in reads:
            if tok[0] == 'c':
                b.rc[tok[1]] = tok[2]
            else:
                b.rd[tok[1]] = tok[2]
        for b in writes:
            b.lw = tok
            b.rc = {}
            b.rd = {}
        self.q[eng].append((fn, waits, inc))
        return tok

    def barrier(self):
        for eng in ENGS:
            waits = []
            for si in range(NDSEM):
                if self.dcount[si] > self.dknown[eng][si]:
                    waits.append((self.dsems[si], self.dcount[si]))
                    self.dknown[eng][si] = self.dcount[si]
            for e in ENGS:
                seq = self.cnt[e]
                if seq > self.known[eng][e]:
                    waits.append((self.sems[e][(seq - 1) // EPOCH], (seq - 1) % EPOCH + 1))
                    self.known[eng][e] = seq
            if waits:
                self.q[eng].append((None, waits, None))

    def finish(self):
        waits = []
        for si in range(NDSEM):
            if self.dcount[si] > 0:
                waits.append((self.dsems[si], self.dcount[si]))
        for e in ENGS:
            if e != 'sp' and self.cnt[e] > 0:
                seq = self.cnt[e]
                waits.append((self.sems[e][(seq - 1) // EPOCH], (seq - 1) % EPOCH + 1))
        self.q['sp'].append((None, waits, None))

    def emit(self, block):
        def run(eng, name):
            for fn, waits, inc in self.q[name]:
                for sem, v in waits:
                    eng.wait_ge(sem, v)
                if fn is not None:
                    ins = fn(eng)
                    ins.then_inc(inc[0], inc[1])

        @block.tensor
        def _(e):
            run(e, 'pe')

        @block.vector
        def _(e):
            run(e, 'dve')

        @block.scalar
        def _(e):
            run(e, 'act')

        @block.gpsimd
        def _(e):
            run(e, 'pool')

        @block.sync
        def _(e):
            run(e, 'sp')


class _Scope:
    def __init__(self, cx):
        self.cx = cx

    def __enter__(self):
        self.old = self.cx.es
        self.es = ExitStack()
        self.cx.es = self.es
        return self

    def __exit__(self, *a):
        self.cx.S.barrier()
        self.es.close()
        self.cx.es = self.old
        return False


class T:
    def __init__(self, h, name):
        self.h = h
        self.b = Buf(name)

    def __getitem__(self, k):
        return self.h[k]


class Ctx:
    def __init__(self, nc, es):
        self.nc = nc
        self.es = es
        self.S = Sched(nc, es)
        self.dq = 0
        self.uid = 0

    def sb(self, name, shape, dt=F32):
        self.uid += 1
        name = f"{name}_{self.uid}"
        return T(self.es.enter_context(self.nc.sbuf_tensor(name, list(shape), dt)), name)

    def ps(self, name, shape, dt=F32):
        self.uid += 1
        name = f"{name}_{self.uid}"
        t = T(self.es.enter_context(self.nc.psum_tensor(name, list(shape), dt)), name)
        t.b.excl = True
        return t

    def scope(self):
        return _Scope(self)

    def dma(self, out, in_, reads=(), writes=(), eng=None, **kw):
        if eng is None:
            eng = ('sp', 'act')[self.dq % 2]
            self.dq += 1
        self.S.op(eng, lambda e: e.dma_start(out=out, in_=in_, **kw), reads=[r.b if isinstance(r, T) else r for r in reads],
                  writes=[w.b if isinstance(w, T) else w for w in writes], dma=True)

    def op(self, eng, fn, reads=(), writes=()):
        self.S.op(eng, fn, reads=[r.b if isinstance(r, T) else r for r in reads],
                  writes=[w.b if isinstance(w, T) else w for w in writes])

    def mm(self, out, lhsT, rhs, start, stop, reads, writes):
        self.op('pe', lambda e: e.matmul(out, lhsT, rhs, start=start, stop=stop), reads, writes)

    def tr(self, out, in_, ident, reads, writes):
        self.op('pe', lambda e: e.transpose(out, in_, ident), reads, writes)

    def act(self, out, in_, func, reads, writes, bias=None, scale=None, accum_out=None, eng='act'):
        kw = {}
        if bias is not None:
            kw['bias'] = bias
        if scale is not None:
            kw['scale'] = scale
        if accum_out is not None:
            kw['accum_out'] = accum_out
        self.op('act', lambda e: e.activation(out=out, in_=in_, func=func, **kw), reads, writes)

    def tt(self, out, in0, in1, op, reads, writes, eng='dve'):
        self.op(eng, lambda e: e.tensor_tensor(out=out, in0=in0, in1=in1, op=op), reads, writes)

    def ts(self, out, in0, s1, op0, reads, writes, s2=None, op1=None, eng='dve', accum_out=None):
        kw = {}
        if op1 is not None:
            kw['op1'] = op1
        if accum_out is not None:
            kw['accum_out'] = accum_out
        self.op(eng, lambda e: e.tensor_scalar(out=out, in0=in0, scalar1=s1, scalar2=s2, op0=op0, **kw), reads, writes)

    def stt(self, out, in0, scalar, in1, op0, op1, reads, writes, accum_out=None):
        kw = {}
        if accum_out is not None:
            kw['accum_out'] = accum_out
        self.op('dve', lambda e: e.scalar_tensor_tensor(out=out, in0=in0, scalar=scalar, in1=in1, op0=op0, op1=op1, **kw),
                reads, writes)

    def copy(self, out, in_, reads, writes, eng='dve'):
        if eng == 'act':
            self.op('act', lambda e: e.activation(out=out, in_=in_, func=AF.Copy), reads, writes)
        else:
            self.op(eng, lambda e: e.tensor_copy(out=out, in_=in_), reads, writes)

    def red(self, out, in_, op, reads, writes, axis=AX.X):
        self.op('dve', lambda e: e.tensor_reduce(out=out, in_=in_, axis=axis, op=op), reads, writes)

    def memset(self, ap, val, writes, eng='dve'):
        self.op(eng, lambda e: e.memset(ap, val), [], writes)


def make_consts(cx):
    nc = cx.nc
    c = {}
    it = cx.sb("c_iota", [128, 255], I32)
    cx.op('pool', lambda e: e.iota(it[:, 0:128], pattern=[[1, 128]], base=0, channel_multiplier=-1), [], [it])
    c['ident'] = cx.sb("c_ident", [128, 128], F32)
    c['identb'] = cx.sb("c_identb", [128, 128], BF16)
    c['triu'] = cx.sb("c_triu", [128, 128], F32)
    c['triu_s'] = cx.sb("c_trius", [128, 128], F32)
    c['tril_s'] = cx.sb("c_trils", [128, 128], F32)
    c['ones'] = cx.sb("c_ones", [128, 128], F32)
    cx.op('dve', lambda e: e.tensor_single_scalar(out=c['ident'][:], in_=it[:, 0:128], scalar=0, op=ALU.is_equal), [it], [c['ident']])
    cx.op('dve', lambda e: e.tensor_single_scalar(out=c['identb'][:], in_=it[:, 0:128], scalar=0, op=ALU.is_equal), [it], [c['identb']])
    cx.op('dve', lambda e: e.tensor_single_scalar(out=c['triu'][:], in_=it[:, 0:128], scalar=0, op=ALU.is_ge), [it], [c['triu']])
    cx.op('dve', lambda e: e.tensor_single_scalar(out=c['triu_s'][:], in_=it[:, 0:128], scalar=0, op=ALU.is_gt), [it], [c['triu_s']])
    cx.op('dve', lambda e: e.tensor_single_scalar(out=c['tril_s'][:], in_=it[:, 0:128], scalar=0, op=ALU.is_lt), [it], [c['tril_s']])
    cx.memset(c['ones'][:], 1.0, [c['ones']])
    it2 = cx.sb("c_iota2", [128, 255], I32)
    cx.op('pool', lambda e: e.iota(it2[:], pattern=[[1, 255]], base=-127, channel_multiplier=0), [], [it2])
    c['w1'] = cx.sb("c_w1", [128, 255], F32)
    cx.op('dve', lambda e: e.tensor_single_scalar(out=c['w1'][:], in_=it2[:], scalar=0, op=ALU.is_equal), [it2], [c['w1']])
    c['w1b'] = cx.sb("c_w1b", [128, 255], BF16)
    cx.op('dve', lambda e: e.tensor_single_scalar(out=c['w1b'][:], in_=it2[:], scalar=0, op=ALU.is_equal), [it2], [c['w1b']])
    c['iota16'] = cx.sb("c_iota16", [128, 16], F32)
    cx.op('dve', lambda e: e.tensor_single_scalar(out=c['iota16'][:], in_=it2[:, 127:143], scalar=0, op=ALU.add), [it2], [c['iota16']])
    c['eps'] = cx.sb("c_eps", [128, 1], F32)
    cx.memset(c['eps'][:], 1e-5, [c['eps']])
    return c


def rms_norm(cx, C, ht, gbc, xn, junk, ss, eps=1e-5, n=1024):
    cx.act(junk[0:C, 0:n], ht[0:C, 0:n], AF.Square, [ht], [junk, ss], accum_out=ss[0:C, 0:1])
    cx.ts(ss[0:C, 1:2], ss[0:C, 0:1], 1.0 / n, ALU.mult, [ss], [ss], s2=eps, op1=ALU.add)
    cx.act(ss[0:C, 1:2], ss[0:C, 1:2], AF.Sqrt, [ss], [ss])
    cx.op('dve', lambda e: e.reciprocal(out=ss[0:C, 1:2], in_=ss[0:C, 1:2]), [ss], [ss])
    cx.stt(xn[0:C, 0:n], ht[0:C, 0:n], ss[0:C, 1:2], gbc[0:C, 0:n], ALU.mult, ALU.mult, [ht, ss, gbc], [xn])


def peer_pass(cx, cst, W, layer, tiles, P, gfin=None, dbgp=None):
    nc = cx.nc
    l = layer
    wq = cx.sb(f"pw_wq{l}", [128, 8, 2048], F32)
    keysT = cx.sb(f"pw_keysT{l}", [128, 16, 128], F32)
    gbc = cx.sb(f"pw_g{l}", [128, 1024], F32)
    wqb = [Buf(f"wq{kc}") for kc in range(8)]
    for kc in range(8):
        cx.dma(wq[:, kc, :], W['peer_w_q'][l, kc * 128:(kc + 1) * 128, :], [], [wqb[kc]])
    cx.dma(gbc[:], W['norm_ffn'][l].partition_broadcast(128), [], [gbc])
    if gfin:
        gfb = cx.sb(f"pw_gf{l}", [128, 1024], F32)
        cx.dma(gfb[:], W['norm_final'].partition_broadcast(128), [], [gfb])
    pA, pB = P['a'], P['b']
    with cx.scope():
        kraw = cx.sb(f"pw_kraw{l}", [128, 16, 128], F32)
        cx.dma(kraw[:], W['peer_sub_keys'][l].rearrange("z h k d -> k (z h) d"), [], [kraw])
        for zh in range(16):
            z, h = zh // 8, zh % 8
            qb = h * 2 + z
            pp = (pA, pB)[zh % 2]
            cx.tr(pp[:, 0:128], kraw[:, zh, :], cst['ident'][:], [kraw, cst['ident']], [pp])
            cx.copy(keysT[:, qb, :], pp[:, 0:128], [pp], [keysT], eng=('dve', 'act')[zh % 2])

    ht2 = [cx.sb(f"pe_ht{l}_{i}", [128, 1024], F32) for i in range(2)]
    xnf = cx.sb(f"pe_xnf{l}", [128, 1024], F32)
    ssf = cx.sb(f"pe_ssf{l}", [128, 2], F32)
    xn = cx.sb(f"pe_xn{l}", [128, 1024], F32)
    xnb2 = [cx.sb(f"pe_xnb{l}_{i}", [128, 1024], BF16) for i in range(2)]
    junk = cx.sb(f"pe_junk{l}", [128, 1024], F32)
    ss = cx.sb(f"pe_ss{l}", [128, 2], F32)
    xnT = cx.sb(f"pe_xnT{l}", [128, 8, 128], F32)
    qT = cx.sb(f"pe_qT{l}", [128, 16, 128], F32)
    sc = cx.sb(f"pe_sc{l}", [128, 16, 128], F32)
    sc2 = cx.sb(f"pe_sc2{l}", [128, 16, 128], F32)
    cand = cx.sb(f"pe_cand{l}", [128, 8, 256], F32)
    oh = alias(sc, sc[:].rearrange("p a b -> p (a b)").rearrange("p (h x) -> p h x", h=8), f"pe_oh{l}")
    sv = cx.sb(f"pe_sv{l}", [128, 16, 16], F32)
    si = cx.sb(f"pe_si{l}", [128, 16, 16], U32)
    sif = cx.sb(f"pe_sif{l}", [128, 16, 16], F32)
    tv = cx.sb(f"pe_tv{l}", [128, 8, 16], F32)
    pos = cx.sb(f"pe_pos{l}", [128, 8, 16], U32)
    pij = cx.sb(f"pe_pij{l}", [128, 2, 128], U32)
    pijf = cx.sb(f"pe_pijf{l}", [128, 2, 128], F32)
    e01 = cx.sb(f"pe_e01{l}", [128, 2, 128], F32)
    eidx = cx.sb(f"pe_eidx{l}", [128, 128], F32)
    gate = cx.sb(f"pe_gate{l}", [128, 128], F32)
    zz = cx.sb(f"pe_zz{l}", [128, 16], F32)
    mx8 = cx.sb(f"pe_mx8{l}", [128, 8], F32)
    eidxT2 = [cx.sb(f"pe_eidxT{l}_{i}", [128, 128], U32) for i in range(2)]
    gateT2 = [cx.sb(f"pe_gateT{l}_{i}", [128, 128], F32) for i in range(2)]
    aT = cx.sb(f"pe_aT{l}", [128, 128], F32)
    coefT = cx.sb(f"pe_coefT{l}", [128, 128], F32)
    NBV, NBL = 7, 6
    uvb = [cx.sb(f"pe_uv{l}_{i}", [128, 2048], BF16) for i in range(NBV)]
    lb = [cx.sb(f"pe_lb{l}_{i}", [128, 128], BF16) for i in range(NBL)]
    acol = [cx.sb(f"pe_ac{l}_{i}", [128, 2], F32) for i in range(4)]
    junk2 = [cx.sb(f"pe_jk{l}_{i}", [128, 1024], BF16) for i in range(2)]
    xbc = [P['x0'], P['x1']]
    Y = P['y']
    ident, identb = cst['ident'], cst['identb']
    uvtab = W['tab_uv']
    eoff = l * 16384 * 2048

    def prep(ti, tl):
        C = tl['C']
        ht, xnb, eidxT, gateT = ht2[ti % 2], xnb2[ti % 2], eidxT2[ti % 2], gateT2[ti % 2]
        cx.dma(ht[0:C, :], tl['src'], [tl['hb']], [ht])
        rms_norm(cx, C, ht, gbc, xn, junk, ss)
        cx.copy(xnb[0:C, :], xn[0:C, :], [xn], [xnb], eng='pool')
        for kc in range(8):
            pp = (pA, pB)[kc % 2]
            cx.tr(pp[:, 0:C], xn[0:C, kc * 128:(kc + 1) * 128], ident[0:C, 0:C], [xn, ident], [pp])
            cx.copy(xnT[:, kc, 0:C], pp[:, 0:C], [pp], [xnT], eng=('dve', 'act')[kc % 2])
        for qb in range(16):
            pp = (pA, pB)[qb % 2]
            for kc in range(8):
                cx.mm(pp[:, 0:C], wq[:, kc, qb * 128:(qb + 1) * 128], xnT[:, kc, 0:C], kc == 0, kc == 7, [wqb[kc], xnT], [pp])
            cx.copy(qT[:, qb, 0:C], pp[:, 0:C], [pp], [qT], eng=('dve', 'act')[qb % 2])
            yield
        for qb in range(16):
            pp = (pA, pB)[qb % 2]
            cx.mm(pp[0:C, 0:128], qT[:, qb, 0:C], keysT[:, qb, :], True, True, [qT, keysT], [pp])
            cx.copy(sc[0:C, qb, :], pp[0:C, 0:128], [pp], [sc], eng='act')
            if qb % 4 == 3:
                yield
        for qb in range(16):
            cx.op('dve', lambda e, qb=qb: e.max(out=sv[0:C, qb, 0:8], in_=sc[0:C, qb, :]), [sc], [sv])
            cx.op('dve', lambda e, qb=qb: e.match_replace(out=sc2[0:C, qb, :], in_to_replace=sv[0:C, qb, 0:8],
                                                         in_values=sc[0:C, qb, :], imm_value=NEG), [sc, sv], [sc2])
            cx.op('dve', lambda e, qb=qb: e.max(out=sv[0:C, qb, 8:16], in_=sc2[0:C, qb, :]), [sc2], [sv])
            cx.op('dve', lambda e, qb=qb: e.max_index(out=si[0:C, qb, 0:8], in_max=sv[0:C, qb, 0:8], in_values=sc[0:C, qb, :]), [sc, sv], [si])
            cx.op('dve', lambda e, qb=qb: e.max_index(out=si[0:C, qb, 8:16], in_max=sv[0:C, qb, 8:16], in_values=sc2[0:C, qb, :]), [sc2, sv], [si])
            yield
        cx.copy(sif[0:C], si[0:C], [si], [sif])
        svv = sv[0:C].rearrange("p (h z) m -> p h z m", z=2)
        sifv = sif[0:C].rearrange("p (h z) m -> p h z m", z=2)
        candv = cand[0:C].rearrange("p h (i j) -> p h i j", j=16)
        cx.tt(candv, svv[:, :, 0, :].unsqueeze(3).to_broadcast([C, 8, 16, 16]),
              svv[:, :, 1, :].unsqueeze(2).to_broadcast([C, 8, 16, 16]), ALU.add, [sv], [cand])
        c2 = sc2[0:C].rearrange("p a b -> p (a b)").rearrange("p (h x) -> p h x", h=8)
        for h in range(8):
            cx.op('dve', lambda e, h=h: e.max(out=tv[0:C, h, 0:8], in_=cand[0:C, h, :]), [cand], [tv])
            cx.op('dve', lambda e, h=h: e.match_replace(out=c2[:, h, :], in_to_replace=tv[0:C, h, 0:8],
                                                       in_values=cand[0:C, h, :], imm_value=NEG), [cand, tv], [sc2])
            cx.op('dve', lambda e, h=h: e.max(out=tv[0:C, h, 8:16], in_=c2[:, h, :]), [sc2], [tv])
            cx.op('dve', lambda e, h=h: e.max_index(out=pos[0:C, h, 0:8], in_max=tv[0:C, h, 0:8], in_values=cand[0:C, h, :]), [cand, tv], [pos])
            cx.op('dve', lambda e, h=h: e.max_index(out=pos[0:C, h, 8:16], in_max=tv[0:C, h, 8:16], in_values=c2[:, h, :]), [sc2, tv], [pos])
            yield
        posf = pos[0:C].rearrange("p h m -> p (h m)")
        cx.op('dve', lambda e: e.tensor_single_scalar(out=pij[0:C, 0, :], in_=posf, scalar=4, op=ALU.logical_shift_right), [pos], [pij])
        cx.op('dve', lambda e: e.tensor_single_scalar(out=pij[0:C, 1, :], in_=posf, scalar=15, op=ALU.bitwise_and), [pos], [pij])
        cx.copy(pijf[0:C], pij[0:C], [pij], [pijf])
        ohv = oh[0:C].rearrange("p h (m i) -> p h m i", i=16)
        io = cst['iota16']
        for z in range(2):
            pv = pijf[0:C, z, :].rearrange("p (h m) -> p h m", h=8)
            cx.tt(ohv, pv.unsqueeze(3).to_broadcast([C, 8, 16, 16]),
                  io[0:C, :].unsqueeze(1).unsqueeze(1).to_broadcast([C, 8, 16, 16]), ALU.is_equal, [pijf, io], [oh])
            cx.tt(ohv, ohv, sifv[:, :, z, :].unsqueeze(2).to_broadcast([C, 8, 16, 16]), ALU.mult, [oh, sif], [oh])
            cx.red(e01[0:C, z, :], oh[0:C].rearrange("p h (m i) -> p (h m) i", i=16), ALU.add, [oh], [e01])
            yield
        cx.stt(eidx[0:C, :], e01[0:C, 0, :], 128.0, e01[0:C, 1, :], ALU.mult, ALU.add, [e01], [eidx])
        cx.copy(mx8[0:C, :], tv[0:C, :, 0], [tv], [mx8], eng='act')
        cx.tt(tv[0:C], tv[0:C], mx8[0:C, :].unsqueeze(2).to_broadcast([C, 8, 16]), ALU.subtract, [tv, mx8], [tv])
        gv = gate[0:C].rearrange("p (h m) -> p h m", h=8)
        cx.act(gv, tv[0:C], AF.Exp, [tv], [gate])
        cx.red(zz[0:C, 0:8], gv, ALU.add, [gate], [zz])
        cx.op('dve', lambda e: e.reciprocal(out=zz[0:C, 8:16], in_=zz[0:C, 0:8]), [zz], [zz])
        cx.tt(gv, gv, zz[0:C, 8:16].unsqueeze(2).to_broadcast([C, 8, 16]), ALU.mult, [gate, zz], [gate])
        cx.tr(pA[:, 0:C], eidx[0:C, :], ident[0:C, 0:C], [eidx, ident], [pA])
        cx.copy(eidxT[:, 0:C], pA[:, 0:C], [pA], [eidxT])
        cx.tr(pB[:, 0:C], gate[0:C, :], ident[0:C, 0:C], [gate, ident], [pB])
        cx.copy(gateT[:, 0:C], pB[:, 0:C], [pB], [gateT], eng='act')

    def gath(ti, tl):
        C = tl['C']
        ht, xnb, eidxT, gateT = ht2[ti % 2], xnb2[ti % 2], eidxT2[ti % 2], gateT2[ti % 2]
        w1 = cst['w1b']
        for step in range(C + 3):
            j = step
            if j < C:
                uv = uvb[j % NBV]
                xb = xbc[j % 2]
                jk = junk2[j % 2]
                cx.S.op('pool', lambda e, uv=uv, j=j: e.indirect_dma_start(
                    out=uv[:], out_offset=None, in_=uvtab,
                    in_offset=bass.IndirectOffsetOnAxis(ap=eidxT[:, j:j + 1], axis=0), element_offset=eoff),
                    reads=[eidxT.b], writes=[uv.b], dma=True)
                for hf in range(2):
                    cx.mm(xb[:, hf * 512:(hf + 1) * 512], identb[0:C, j:j + 1].to_broadcast([C, 128]),
                          xnb[0:C, hf * 512:(hf + 1) * 512], True, True, [identb, xnb], [xb])
                ac = acol[j % 4]
                cx.stt(jk[:], uv[:, 0:1024], 1.0, xb[:], ALU.mult, ALU.mult, [uv, xb], [jk, ac], accum_out=ac[:, 0:1])
                cx.act(ac[:, 1:2], ac[:, 0:1], AF.Gelu, [ac], [ac])
            j1 = step - 1
            if 0 <= j1 < C:
                L = lb[j1 % NBL]
                ac = acol[j1 % 4]
                cx.ts(L[:, 0:C], w1[:, 127 - j1:127 - j1 + C], ac[:, 1:2], ALU.mult, [w1, ac, gateT], [L], s2=gateT[:, j1:j1 + 1], op1=ALU.mult)
            j2 = step - 3
            if j2 >= 0:
                L = lb[j2 % NBL]
                vv = uvb[j2 % NBV]
                for hf in range(2):
                    cx.mm(Y[0:C, hf * 512:(hf + 1) * 512], L[:, 0:C], vv[:, 1024 + hf * 512:1024 + (hf + 1) * 512], j2 == 0, j2 == C - 1, [L, vv], [Y])
            yield
        cx.tt(ht[0:C, :], Y[0:C, :], ht[0:C, :], ALU.add, [Y, ht], [ht])
        cx.dma(tl['dst'], ht[0:C, :], [ht], [tl['hb']])
        if gfin:
            rms_norm(cx, C, ht, gfb, xnf, junk2[0], ssf)
            for (oap, r0, r1) in tl['fin']:
                cx.dma(oap, xnf[r0:r1, :], [xnf], [])

    for _ in prep(0, tiles[0]):
        pass
    for ti, tl in enumerate(tiles):
        nxt = prep(ti + 1, tiles[ti + 1]) if ti + 1 < len(tiles) else None
        k = 0
        for _ in gath(ti, tl):
            k += 1
            if nxt is not None and k % 2 == 0:
                try:
                    next(nxt)
                except StopIteration:
                    nxt = None
        if nxt is not None:
            for _ in nxt:
                pass


def load_cast(cx, dst, src_ap, stg, i, P, ncols):
    st = stg[i % len(stg)]
    cx.dma(st[0:P, 0:ncols], src_ap, [], [st])
    eng = ('dve', 'pool', 'act')[i % 3]
    return st, eng


def load_w_kc(cx, dstT, src2d, stg, cnt, ncols):
    K = src2d.shape[0]
    for kc in range((K + 127) // 128):
        P = min(128, K - kc * 128)
        st, eng = load_cast(cx, None, src2d[kc * 128:kc * 128 + P, :], stg, cnt[0], P, ncols)
        cx.copy(dstT[0:P, kc, 0:ncols], st[0:P, 0:ncols], [st], [dstT], eng=eng)
        cnt[0] += 1


def bcast_vec(cx, name, src1d, n=1024):
    t = cx.sb(name, [128, n], F32)
    cx.dma(t[:], src1d.partition_broadcast(128), [], [t])
    return t


def alias(parent, ap, name):
    t = T(ap, name)
    t.b = parent.b
    return t


def rms_norm_ps(cx, C, ht, gps, gT, xn, junk, ss, eps=1e-5, n=1024):
    cx.act(junk[0:C, 0:n], ht[0:C, 0:n], AF.Square, [ht], [junk, ss], accum_out=ss[0:C, 0:1])
    cx.ts(ss[0:C, 1:2], ss[0:C, 0:1], 1.0 / n, ALU.mult, [ss], [ss], s2=eps, op1=ALU.add)
    cx.act(ss[0:C, 1:2], ss[0:C, 1:2], AF.Sqrt, [ss], [ss])
    cx.op('dve', lambda e: e.reciprocal(out=ss[0:C, 1:2], in_=ss[0:C, 1:2]), [ss], [ss])
    cx.stt(xn[0:C, 0:n], ht[0:C, 0:n], ss[0:C, 1:2], gps, ALU.mult, ALU.mult, [ht, ss, gT], [xn])


def rwkv_pass(cx, cst, W, tiles, smp, O):
    ident, identb, ones = cst['ident'], cst['identb'], cst['ones']
    triu, triu_s, tril_s = cst['triu'], cst['triu_s'], cst['tril_s']
    NH = 16
    with cx.scope():
        PA = cx.ps("rw_PA", [128, 1024])
        PB = cx.ps("rw_PB", [128, 1024])
        PC = cx.ps("rw_PC", [128, 1024])
        PD = cx.ps("rw_PD", [128, 1024])
        Wo = cx.sb("rw_Wo", [128, 8, 1024], BF16)
        VT = [cx.sb(f"rw_VT{i}", [128, 1024]) for i in range(3)]
        vmap = {'g': (0, 0), 'kk': (0, 32), 'ka': (0, 64), 'rk': (1, 0), 'lnw': (1, 32), 'lnb': (1, 64), 'w0': (2, 0), 'a0': (2, 32)}
        for nm, src in (('g', W['norm_mix'][0:1, :]), ('kk', W['rwkv_k_k'][0:1, :]), ('ka', W['rwkv_k_a'][0:1, :]),
                        ('rk', W['rwkv_r_k'].rearrange("o h k -> o (h k)")), ('lnw', W['rwkv_ln_w'][0:1, :]), ('lnb', W['rwkv_ln_b'][0:1, :]),
                        ('w0', W['rwkv_w0'][0:1, :]), ('a0', W['rwkv_a0'][0:1, :])):
            ti_, p_ = vmap[nm]
            cx.dma(VT[ti_][p_:p_ + 1, :], src, [], [VT[ti_]])

        def vrow(nm, hf):
            ti_, p_ = vmap[nm]
            return ones[p_:p_ + 1, :], VT[ti_][p_:p_ + 1, hf * 512:(hf + 1) * 512], VT[ti_]

        def bcast(nm, C, PP):
            for hf in range(2):
                l, r, t = vrow(nm, hf)
                cx.mm(PP[0:C, hf * 512:(hf + 1) * 512], l[:, 0:C], r, True, True, [ones, t], [PP])

        ht = cx.sb("rw_ht", [128, 1024])
        xn = cx.sb("rw_xn", [128, 1024]); t1 = xn
        ss = cx.sb("rw_ss", [128, 2])
        r_ = cx.sb("rw_r", [128, 1024]); junk = r_
        k_ = cx.sb("rw_k", [128, 1024]); o_ = k_
        v_ = cx.sb("rw_v", [128, 1024])
        vb = cx.sb("rw_vb", [128, 1024], BF16)
        lw = cx.sb("rw_lw", [128, 1024])
        a_ = cx.sb("rw_a", [128, 1024])
        g_ = cx.sb("rw_g_", [128, 1024], BF16)
        kk = cx.sb("rw_kk_", [128, 1024])
        kp = cx.sb("rw_kp", [128, 1024])
        b_ = cx.sb("rw_b", [128, 1024])
        bon = cx.sb("rw_bon", [128, 1024])
        st16 = cx.sb("rw_st16", [128, 64])
        ogb = cx.sb("rw_ogb", [128, 1024], BF16)
        ogT = cx.sb("rw_ogT", [128, 8, 128], BF16)
        prevT = cx.sb("rw_prevT", [128, 8, 16])
        cx.memset(prevT[:], 0.0, [prevT])

        def back(tl, osrc, oT):
            C = tl['C']
            o3 = osrc[0:C, :].rearrange("p (h k) -> p h k", k=64)
            cx.red(st16[0:C, 0:16], o3, ALU.add, [oT], [st16])
            cx.act(t1[0:C, :], osrc[0:C, :], AF.Square, [oT], [t1])
            cx.red(st16[0:C, 16:32], t1[0:C, :].rearrange("p (h k) -> p h k", k=64), ALU.add, [t1], [st16])
            cx.ts(st16[0:C, 0:32], st16[0:C, 0:32], 1.0 / 64, ALU.mult, [st16], [st16])
            cx.tt(st16[0:C, 48:64], st16[0:C, 0:16], st16[0:C, 0:16], ALU.mult, [st16], [st16])
            cx.tt(st16[0:C, 16:32], st16[0:C, 16:32], st16[0:C, 48:64], ALU.subtract, [st16], [st16])
            cx.ts(st16[0:C, 16:32], st16[0:C, 16:32], 64e-5, ALU.add, [st16], [st16])
            cx.act(st16[0:C, 16:32], st16[0:C, 16:32], AF.Sqrt, [st16], [st16])
            cx.op('dve', lambda e: e.reciprocal(out=st16[0:C, 16:32], in_=st16[0:C, 16:32]), [st16], [st16])
            t3 = t1[0:C, :].rearrange("p (h k) -> p h k", k=64)
            cx.tt(t3, o3, st16[0:C, 0:16].unsqueeze(2).to_broadcast([C, 16, 64]), ALU.subtract, [oT, st16], [t1])
            cx.tt(t3, t3, st16[0:C, 16:32].unsqueeze(2).to_broadcast([C, 16, 64]), ALU.mult, [t1, st16], [t1])
            bcast('lnw', C, PC)
            cx.tt(t1[0:C, :], t1[0:C, :], PC[0:C, :], ALU.mult, [t1, PC], [t1])
            bcast('lnb', C, PD)
            cx.tt(t1[0:C, :], t1[0:C, :], PD[0:C, :], ALU.add, [t1, PD], [t1])
            cx.tt(t1[0:C, :], t1[0:C, :], bon[0:C, :], ALU.add, [t1, bon], [t1])
            cx.tt(ogb[0:C, :], t1[0:C, :], g_[0:C, :], ALU.mult, [t1, g_], [ogb])
            PBb = PB[:].bitcast(BF16)
            for kc in range(8):
                cx.tr(PBb[:, kc * 128:kc * 128 + C], ogb[0:C, kc * 128:(kc + 1) * 128], identb[0:C, 0:C], [ogb, identb], [PB])
            cx.copy(ogT[:, :, 0:C], PBb[:, 0:1024].rearrange("p (a b) -> p a b", b=128)[:, :, 0:C], [PB], [ogT])
            for hf in range(2):
                for kc in range(8):
                    cx.mm(PA[0:C, hf * 512:(hf + 1) * 512], ogT[:, kc, 0:C], Wo[:, kc, hf * 512:(hf + 1) * 512], kc == 0, kc == 7, [ogT, Wo], [PA])
            cx.tt(ht[0:C, :], PA[0:C, :], ht[0:C, :], ALU.add, [PA, ht], [ht])
            cx.dma(tl['dst'], ht[0:C, :], [ht], [tl['hb']])

        with cx.scope():
            Wr = cx.sb("rw_Wr", [128, 8, 1024], BF16)
            Wk = cx.sb("rw_Wk", [128, 8, 1024], BF16)
            Wv = cx.sb("rw_Wv", [128, 8, 1024], BF16)
            Wl = cx.sb("rw_Wl", [128, 8, 288], BF16)
            W2 = cx.sb("rw_W2", [128, 4, 1024], BF16)
            mixv = cx.sb("rw_mixv", [128, 8, 6], F32)
            with cx.scope():
                stg = [cx.sb(f"rw_stg{i}", [128, 1024], F32) for i in range(2)]
                cnt = [0]
                load_w_kc(cx, Wr, W['rwkv_w_rkv'][0, 0], stg, cnt, 1024)
                load_w_kc(cx, Wk, W['rwkv_w_rkv'][0, 1], stg, cnt, 1024)
                load_w_kc(cx, Wv, W['rwkv_w_rkv'][0, 2], stg, cnt, 1024)
                load_w_kc(cx, Wo, W['rwkv_w_o'][0], stg, cnt, 1024)
                for (nm, c0, n) in (('rwkv_w1', 0, 64), ('rwkv_a1', 64, 64), ('rwkv_g1', 128, 160)):
                    for kc in range(8):
                        st = stg[cnt[0] % 2]
                        cx.dma(st[:, 0:n], W[nm][0, kc * 128:(kc + 1) * 128, :], [], [st])
                        cx.copy(Wl[:, kc, c0:c0 + n], st[:, 0:n], [st], [Wl], eng=('dve', 'pool')[cnt[0] % 2])
                        cnt[0] += 1
                for (nm, slot, r0, nr) in (('rwkv_w2', 0, 0, 64), ('rwkv_a2', 1, 0, 64), ('rwkv_g2', 2, 0, 128), ('rwkv_g2', 3, 128, 32)):
                    st = stg[cnt[0] % 2]
                    cx.dma(st[0:nr, :], W[nm][0, r0:r0 + nr, :], [], [st])
                    cx.copy(W2[0:nr, slot, :], st[0:nr, :], [st], [W2], eng=('dve', 'pool')[cnt[0] % 2])
                    cnt[0] += 1
                cx.memset(stg[0][0:32, :], 0.0, [stg[0]])
                cx.dma(stg[0][0:6, :], W['rwkv_mix'][0], [], [stg[0]])
                for kc in range(8):
                    cx.mm(PA[:, kc * 32:(kc + 1) * 32], stg[0][0:32, kc * 128:(kc + 1) * 128], ident[0:32, 0:32], True, True, [stg[0], ident], [PA])
                cx.copy(mixv[:], PA[:, 0:256].rearrange("p (a b) -> p a b", b=32)[:, :, 0:6], [PA], [mixv])

            xnT = cx.sb("rw_xnT", [128, 8, 128])
            dxT = cx.sb("rw_dxT", [128, 8, 128])
            tmpT = cx.sb("rw_tmpT", [128, 8, 128])
            mT = [alias(ogT, ogT[:], "rw_mT0"), cx.sb("rw_mT1", [128, 8, 128], BF16)]
            hT = cx.sb("rw_hT", [128, 2, 128], BF16)

            def front(tl, sh, after_norm=None):
                C = tl['C']
                if isinstance(tl['src'], list):
                    for (ap, r0, r1) in tl['src']:
                        cx.dma(ht[r0:r1, :], ap, [tl['hb']], [ht])
                else:
                    cx.dma(ht[0:C, :], tl['src'], [tl['hb']], [ht])
                bcast('g', C, PD)
                rms_norm_ps(cx, C, ht, PD[0:C, :], PD, xn, junk, ss)
                if after_norm is not None:
                    after_norm()
                for kc in range(8):
                    pp = (PA, PB)[kc % 2]
                    cx.tr(pp[:, 0:C], xn[0:C, kc * 128:(kc + 1) * 128], ident[0:C, 0:C], [xn, ident], [pp])
                    cx.copy(xnT[:, kc, 0:C], pp[:, 0:C], [pp], [xnT], eng=('dve', 'act')[kc % 2])
                cx.tt(dxT[:, :, 0:sh], prevT[:, :, 0:sh], xnT[:, :, 0:sh], ALU.subtract, [prevT, xnT], [dxT])
                if C > sh:
                    cx.tt(dxT[:, :, sh:C], xnT[:, :, 0:C - sh], xnT[:, :, sh:C], ALU.subtract, [xnT], [dxT])
                cx.copy(prevT[:, :, 0:sh], xnT[:, :, C - sh:C], [xnT], [prevT], eng='pool')

                def mix(j, dst):
                    cx.tt(tmpT[:, :, 0:C], dxT[:, :, 0:C], mixv[:, :, j:j + 1].to_broadcast([128, 8, C]), ALU.mult, [dxT, mixv], [tmpT])
                    cx.tt(dst[:, :, 0:C], tmpT[:, :, 0:C], xnT[:, :, 0:C], ALU.add, [tmpT, xnT], [dst])

                def proj(src, Wm, PP):
                    for hf in range(2):
                        for kc in range(8):
                            cx.mm(PP[0:C, hf * 512:(hf + 1) * 512], src[:, kc, 0:C], Wm[:, kc, hf * 512:(hf + 1) * 512], kc == 0, kc == 7, [src, Wm], [PP])

                mix(0, mT[0]); proj(mT[0], Wr, PA)
                cx.copy(r_[0:C, :], PA[0:C, :], [PA], [r_], eng='act')
                mix(2, mT[1]); proj(mT[1], Wk, PB)
                cx.copy(k_[0:C, :], PB[0:C, :], [PB], [k_], eng='act')
                mix(3, mT[0]); proj(mT[0], Wv, PA)
                cx.copy(v_[0:C, :], PA[0:C, :], [PA], [v_], eng='act')
                cx.copy(vb[0:C, :], PA[0:C, :], [PA], [vb], eng='dve')
                mix(1, mT[1])
                for kc in range(8):
                    cx.mm(PC[0:64, 0:C], Wl[:, kc, 0:64], mT[1][:, kc, 0:C], kc == 0, kc == 7, [Wl, mT[1]], [PC])
                cx.act(hT[0:64, 0, 0:C], PC[0:64, 0:C], AF.Tanh, [PC], [hT])
                for hf in range(2):
                    l, r, t = vrow('w0', hf)
                    cx.mm(PB[0:C, hf * 512:(hf + 1) * 512], l[:, 0:C], r, True, False, [ones, t], [PB])
                    cx.mm(PB[0:C, hf * 512:(hf + 1) * 512], hT[0:64, 0, 0:C], W2[0:64, 0, hf * 512:(hf + 1) * 512], False, True, [hT, W2], [PB])
                cx.act(lw[0:C, :], PB[0:C, :], AF.Sigmoid, [PB], [lw])
                cx.ts(lw[0:C, :], lw[0:C, :], -0.6065306597126334, ALU.mult, [lw], [lw], eng='pool')
                mix(4, mT[0])
                for kc in range(8):
                    cx.mm(PC[0:64, 0:C], Wl[:, kc, 64:128], mT[0][:, kc, 0:C], kc == 0, kc == 7, [Wl, mT[0]], [PC])
                cx.copy(hT[0:64, 0, 0:C], PC[0:64, 0:C], [PC], [hT], eng='act')
                for hf in range(2):
                    l, r, t = vrow('a0', hf)
                    cx.mm(PA[0:C, hf * 512:(hf + 1) * 512], l[:, 0:C], r, True, False, [ones, t], [PA])
                    cx.mm(PA[0:C, hf * 512:(hf + 1) * 512], hT[0:64, 0, 0:C], W2[0:64, 1, hf * 512:(hf + 1) * 512], False, True, [hT, W2], [PA])
                cx.act(a_[0:C, :], PA[0:C, :], AF.Sigmoid, [PA], [a_])
                mix(5, mT[1])
                for (c0, n, slot) in ((128, 128, 0), (256, 32, 1)):
                    for kc in range(8):
                        cx.mm(PC[0:n, 0:C], Wl[:, kc, c0:c0 + n], mT[1][:, kc, 0:C], kc == 0, kc == 7, [Wl, mT[1]], [PC])
                    cx.act(hT[0:n, slot, 0:C], PC[0:n, 0:C], AF.Sigmoid, [PC], [hT])
                for hf in range(2):
                    cx.mm(PB[0:C, hf * 512:(hf + 1) * 512], hT[0:128, 0, 0:C], W2[0:128, 2, hf * 512:(hf + 1) * 512], True, False, [hT, W2], [PB])
                    cx.mm(PB[0:C, hf * 512:(hf + 1) * 512], hT[0:32, 1, 0:C], W2[0:32, 3, hf * 512:(hf + 1) * 512], False, True, [hT, W2], [PB])
                cx.copy(g_[0:C, :], PB[0:C, :], [PB], [g_], eng='act')
                bcast('kk', C, PC)
                cx.tt(kk[0:C, :], k_[0:C, :], PC[0:C, :], ALU.mult, [k_, PC], [kk])
                cx.tt(t1[0:C, :], kk[0:C, :], kk[0:C, :], ALU.mult, [kk], [t1], eng='pool')
                cx.red(st16[0:C, 0:16], t1[0:C, :].rearrange("p (h k) -> p h k", k=64), ALU.add, [t1], [st16])
                cx.act(st16[0:C, 0:16], st16[0:C, 0:16], AF.Sqrt, [st16], [st16])
                cx.ts(st16[0:C, 0:16], st16[0:C, 0:16], 1e-12, ALU.max, [st16], [st16])
                cx.op('dve', lambda e: e.reciprocal(out=st16[0:C, 16:32], in_=st16[0:C, 0:16]), [st16], [st16])
                kk3 = kk[0:C, :].rearrange("p (h k) -> p h k", k=64)
                cx.tt(kk3, kk3, st16[0:C, 16:32].unsqueeze(2).to_broadcast([C, 16, 64]), ALU.mult, [kk, st16], [kk])
                bcast('ka', C, PD)
                cx.stt(t1[0:C, :], a_[0:C, :], -1.0, PD[0:C, :], ALU.add, ALU.mult, [a_, PD], [t1])
                cx.stt(kp[0:C, :], t1[0:C, :], 1.0, k_[0:C, :], ALU.add, ALU.mult, [t1, k_], [kp])
                cx.tt(b_[0:C, :], kk[0:C, :], a_[0:C, :], ALU.mult, [kk, a_], [b_], eng='pool')
                bcast('rk', C, PC)
                cx.tt(t1[0:C, :], r_[0:C, :], kp[0:C, :], ALU.mult, [r_, kp], [t1])
                cx.tt(t1[0:C, :], t1[0:C, :], PC[0:C, :], ALU.mult, [t1, PC], [t1])
                cx.red(st16[0:C, 32:48], t1[0:C, :].rearrange("p (h k) -> p h k", k=64), ALU.add, [t1], [st16])
                cx.tt(bon[0:C, :].rearrange("p (h k) -> p h k", k=64), v_[0:C, :].rearrange("p (h k) -> p h k", k=64),
                      st16[0:C, 32:48].unsqueeze(2).to_broadcast([C, 16, 64]), ALU.mult, [v_, st16], [bon])

            with cx.scope():
                Hs = cx.sb("rw_H", [64, 16, 64])
                Hb = cx.sb("rw_Hb", [64, 16, 64], BF16)
                G = cx.sb("rw_G", [64, 32])
                FT_B = cx.sb("rw_FTB", [64, 16, 128], BF16)
                FT_K = cx.sb("rw_FTK", [64, 16, 128], BF16)
                FT_AR = cx.sb("rw_FTAR", [64, 16, 256], BF16)
                tokb = [alias(ogb, ogb[:], "rw_tokb0"), cx.sb("rw_tokb1", [128, 1024], BF16)]
                xnTb = xnT[:].rearrange("p a b -> p (a b)").bitcast(BF16)
                Kh = alias(xnT, xnTb[:, 0:1024], "rw_Kh")
                Bh = alias(xnT, xnTb[:, 1024:2048], "rw_Bh")
                e1 = alias(tmpT, tmpT[:].rearrange("p a b -> p (a b)"), "rw_e1")
                sets = []
                for i in range(1):
                    sets.append(dict(PTb=cx.sb(f"rw_PTb{i}", [128, 4, 128], BF16), ArbT=cx.sb(f"rw_ArbT{i}", [128, 4, 128], BF16),
                                     AakT=cx.sb(f"rw_AakT{i}", [128, 4, 128], BF16), ArkT=cx.sb(f"rw_ArkT{i}", [128, 4, 128], BF16)))
                Mx = [cx.sb(f"rw_M{i}", [128, 4, 128]) for i in range(2)]
                MTx = [cx.sb(f"rw_MT{i}", [128, 4, 128]) for i in range(2)]
                PTf = cx.sb("rw_PTf", [128, 4, 128])
                Xs = cx.sb("rw_Xs", [128, 256], BF16)
                Us = cx.sb("rw_Us", [128, 256], BF16)
                HT_o = alias(a_, a_[:, 0:512].rearrange("p (a b) -> p a b", b=64), "rw_HTo")
                cx.S.mark('weights_loaded')
                cx.memset(Hs[:], 0.0, [Hs])
                cx.memset(Hb[:], 0.0, [Hb], eng='pool')
                for ti, tl in enumerate(tiles):
                    C = tl['C']
                    cx.S.mark(f'tile{ti}_start')
                    if ti == len(tiles) - 1:
                        front(tl, 1, lambda C=C: cx.dma(O['p_shift'], xn[C - 1:C, :], [xn], []))
                    else:
                        front(tl, 1)
                    for hf in range(2):
                        cx.mm(PA[0:C, hf * 512:(hf + 1) * 512], triu[0:C, 0:C], lw[0:C, hf * 512:(hf + 1) * 512], True, True, [triu, lw], [PA])
                        cx.mm(PB[0:C, hf * 512:(hf + 1) * 512], ones[0:C, 0:C], lw[0:C, hf * 512:(hf + 1) * 512], True, True, [ones, lw], [PB])
                    for h in range(16):
                        cx.mm(PC[0:64, h * 2:h * 2 + 2], lw[0:C, h * 64:(h + 1) * 64], ones[0:C, 0:2], True, True, [lw, ones], [PC])
                    cx.act(G[:, 0:32], PC[0:64, 0:32], AF.Exp, [PC], [G])
                    PCb = PC[:].bitcast(BF16)
                    PDb = PD[:].bitcast(BF16)

                    def to_fm(src, dst, off, pp, ppT):
                        for h in range(16):
                            cx.tr(pp[0:64, h * 128:h * 128 + C], src[0:C, h * 64:(h + 1) * 64], identb[0:C, 0:C], [src, identb], [ppT])
                        cx.copy(dst[:, :, off:off + C], pp[0:64, 0:2048].rearrange("p (a b) -> p a b", b=128)[:, :, 0:C], [ppT], [dst], eng='act')

                    cx.act(e1[0:C, :], PA[0:C, :], AF.Exp, [PA], [e1])
                    cx.tt(tokb[0][0:C, :], r_[0:C, :], e1[0:C, :], ALU.mult, [r_, e1], [tokb[0]])
                    to_fm(tokb[0], FT_AR, C, PCb, PC)
                    cx.act(e1[0:C, :], PA[0:C, :], AF.Exp, [PA], [e1], scale=-1.0)
                    cx.tt(tokb[1][0:C, :], kp[0:C, :], e1[0:C, :], ALU.mult, [kp, e1], [tokb[1]])
                    to_fm(tokb[1], FT_K, 0, PDb, PD)
                    cx.tt(tokb[0][0:C, :], b_[0:C, :], e1[0:C, :], ALU.mult, [b_, e1], [tokb[0]])
                    to_fm(tokb[0], FT_B, 0, PCb, PC)
                    cx.tt(t1[0:C, :], PA[0:C, :], lw[0:C, :], ALU.subtract, [PA, lw], [t1])
                    cx.act(e1[0:C, :], t1[0:C, :], AF.Exp, [t1], [e1])
                    cx.stt(tokb[1][0:C, :], kk[0:C, :], -1.0, e1[0:C, :], ALU.mult, ALU.mult, [kk, e1], [tokb[1]])
                    to_fm(tokb[1], FT_AR, 0, PDb, PD)
                    cx.copy(t1[0:C, :], PA[0:C, :], [PA], [t1], eng='act')
                    cx.tt(t1[0:C, :], PB[0:C, :], t1[0:C, :], ALU.subtract, [PB, t1], [t1])
                    cx.act(e1[0:C, :], t1[0:C, :], AF.Exp, [t1], [e1])
                    cx.tt(Kh[0:C, :], kp[0:C, :], e1[0:C, :], ALU.mult, [kp, e1], [Kh])
                    cx.tt(Bh[0:C, :], b_[0:C, :], e1[0:C, :], ALU.mult, [b_, e1], [Bh], eng='pool')
                    cx.S.mark(f'tile{ti}_prep_done')
                    nl = max(1, int(np.ceil(np.log2(C))))
                    for hg in range(4):
                        st_ = sets[0]
                        PTb, ArbT, AakT, ArkT = st_['PTb'], st_['ArbT'], st_['AakT'], st_['ArkT']
                        for (FT_l, outs) in ((FT_B, ('MT', ArbT)), (FT_K, (AakT, ArkT))):
                            for q in range(4):
                                h = hg * 4 + q
                                pp = (PC, PD)[q // 2]
                                c0 = (q % 2) * 2 * C
                                cx.mm(pp[0:C, c0:c0 + 2 * C], FT_l[0:64, h, 0:C], FT_AR[0:64, h, 0:2 * C], True, True, [FT_l, FT_AR], [pp])
                            for q2 in range(2):
                                pp = (PC, PD)[q2]
                                v4 = pp[0:C, 0:4 * C].rearrange("p (q z c) -> p q z c", q=2, z=2)
                                o0 = MTx[0] if outs[0] == 'MT' else outs[0]
                                cx.tt(o0[0:C, 2 * q2:2 * q2 + 2, 0:C], v4[:, :, 0, :], triu_s[0:C, 0:C].unsqueeze(1).to_broadcast([C, 2, C]), ALU.mult, [pp, triu_s], [o0])
                                cx.tt(outs[1][0:C, 2 * q2:2 * q2 + 2, 0:C], v4[:, :, 1, :], triu[0:C, 0:C].unsqueeze(1).to_broadcast([C, 2, C]), ALU.mult, [pp, triu], [outs[1]])
                        for q in range(4):
                            h = hg * 4 + q
                            cx.mm(PC[0:C, q * C:(q + 1) * C], FT_AR[0:64, h, 0:C], FT_B[0:64, h, 0:C], True, True, [FT_AR, FT_B], [PC])
                        cx.tt(Mx[0][0:C, :, 0:C], PC[0:C, 0:4 * C].rearrange("p (q c) -> p q c", q=4), tril_s[0:C, 0:C].unsqueeze(1).to_broadcast([C, 4, C]), ALU.mult, [PC, tril_s], [Mx[0]])
                        cx.tt(PTf[0:C, :, 0:C], MTx[0][0:C, :, 0:C], ident[0:C, 0:C].unsqueeze(1).to_broadcast([C, 4, C]), ALU.add, [MTx[0], ident], [PTf])
                        cur = 0
                        for lv in range(1, nl):
                            nx = 1 - cur
                            last = (lv == nl - 1)
                            for q in range(4):
                                cx.mm(PC[0:C, q * C:(q + 1) * C], MTx[cur][0:C, q, 0:C], Mx[cur][0:C, q, 0:C], True, True, [MTx[cur], Mx[cur]], [PC])
                            cx.copy(Mx[nx][0:C, :, 0:C], PC[0:C, 0:4 * C].rearrange("p (q c) -> p q c", q=4), [PC], [Mx[nx]], eng='act')
                            if not last:
                                for q in range(4):
                                    cx.mm(PD[0:C, q * C:(q + 1) * C], Mx[cur][0:C, q, 0:C], MTx[cur][0:C, q, 0:C], True, True, [MTx[cur], Mx[cur]], [PD])
                                cx.copy(MTx[nx][0:C, :, 0:C], PD[0:C, 0:4 * C].rearrange("p (q c) -> p q c", q=4), [PD], [MTx[nx]], eng='dve')
                            for q in range(4):
                                cx.mm(PC[0:C, 512 + q * C:512 + (q + 1) * C], Mx[nx][0:C, q, 0:C], PTf[0:C, q, 0:C], True, True, [Mx[nx], PTf], [PC])
                            cx.tt(PTf[0:C, :, 0:C], PTf[0:C, :, 0:C], PC[0:C, 512:512 + 4 * C].rearrange("p (q c) -> p q c", q=4), ALU.add, [PTf, PC], [PTf])
                            cur = nx
                        cx.copy(PTb[0:C, :, 0:C], PTf[0:C, :, 0:C], [PTf], [PTb], eng='pool')
                        cx.S.mark(f'tile{ti}_hg{hg}_inv_done')
                        c4 = hg * 256
                        pX = PA if hg % 2 == 0 else PB
                        for q in range(4):
                            h = hg * 4 + q
                            cx.mm(pX[0:C, q * 64:(q + 1) * 64], FT_AR[0:64, h, 0:C], Hb[0:64, h, :], True, False, [FT_AR, Hb], [pX])
                            cx.mm(pX[0:C, q * 64:(q + 1) * 64], AakT[0:C, q, 0:C], vb[0:C, h * 64:(h + 1) * 64], False, True, [AakT, vb], [pX])
                        cx.copy(Xs[0:C, :], pX[0:C, 0:256], [pX], [Xs], eng='act')
                        for q in range(4):
                            cx.mm(pX[0:C, 256 + q * 64:256 + (q + 1) * 64], PTb[0:C, q, 0:C], Xs[0:C, q * 64:(q + 1) * 64], True, True, [PTb, Xs], [pX])
                        cx.copy(Us[0:C, :], pX[0:C, 256:512], [pX], [Us], eng='act')
                        for q in range(4):
                            h = hg * 4 + q
                            oc = 512 + q * 64
                            cx.mm(pX[0:C, oc:oc + 64], FT_AR[0:64, h, C:2 * C], Hb[0:64, h, :], True, False, [FT_AR, Hb], [pX])
                            cx.mm(pX[0:C, oc:oc + 64], ArkT[0:C, q, 0:C], vb[0:C, h * 64:(h + 1) * 64], False, False, [ArkT, vb], [pX])
                            cx.mm(pX[0:C, oc:oc + 64], ArbT[0:C, q, 0:C], Us[0:C, q * 64:(q + 1) * 64], False, True, [ArbT, Us], [pX])
                        cx.copy(o_[0:C, c4:c4 + 256], pX[0:C, 512:768], [pX], [o_], eng='act')
                        for q in range(4):
                            h = hg * 4 + q
                            oc = 768 + q * 64
                            cx.mm(pX[0:64, oc:oc + 64], Kh[0:C, h * 64:(h + 1) * 64], vb[0:C, h * 64:(h + 1) * 64], True, False, [Kh, vb], [pX])
                            cx.mm(pX[0:64, oc:oc + 64], Bh[0:C, h * 64:(h + 1) * 64], Us[0:C, q * 64:(q + 1) * 64], False, True, [Bh, Us], [pX])
                        H4 = Hs[0:64, hg * 4:hg * 4 + 4, :]
                        cx.tt(H4, H4, G[:, 0:32].rearrange("p (a b) -> p a b", b=2)[:, hg * 4:hg * 4 + 4, 0:1].to_broadcast([64, 4, 64]), ALU.mult, [Hs, G], [Hs])
                        cx.tt(H4, H4, pX[0:64, 768:1024].rearrange("p (a b) -> p a b", b=64), ALU.add, [Hs, pX], [Hs])
                    cx.copy(Hb[:], Hs[:], [Hs], [Hb], eng='pool')
                    cx.S.mark(f'tile{ti}_chunk_done')
                    back(tl, o_, o_)
                cx.S.mark('prompt_tiles_done')
                for fb in range(8):
                    pp = (PA, PB)[fb % 2]
                    cx.tr(pp[:, 0:64], Hs[0:64, 2 * fb:2 * fb + 2, :].rearrange("p a b -> p (a b)"), ident[0:64, 0:64], [Hs, ident], [pp])
                    cx.copy(HT_o[:, fb, :], pp[:, 0:64], [pp], [HT_o], eng=('dve', 'act')[fb % 2])
                cx.dma(O['p_wkv'].rearrange("(fb hp) v k -> (hp v) fb k", hp=2), HT_o[:, :, :], [HT_o], [])

            C = 64
            cx.S.mark('prompt_done')
            sh0 = alias(a_, a_[0:16, :], "rw_sh0")
            cx.dma(sh0[:, :], smp['shift0'], [], [sh0])
            for kc in range(8):
                pp = (PA, PB)[kc % 2]
                cx.tr(pp[:, 0:16], sh0[:, kc * 128:(kc + 1) * 128], ident[0:16, 0:16], [sh0, ident], [pp])
                cx.copy(prevT[:, kc, 0:16], pp[:, 0:16], [pp], [prevT])
            front(smp, 16, lambda: cx.dma(O['s_shift'], xn[48:64, :], [xn], []))
            cx.act(lw[0:C, :], lw[0:C, :], AF.Exp, [lw], [lw])
            SD = smp['scr']
            sdb = Buf("rw_sd")
            for qi, src in enumerate((r_, lw, kp, v_, kk, b_)):
                for t in range(4):
                    cx.dma(SD[qi, :, :, t, :], src[16 * t:16 * t + 16, :].rearrange("p (hh f) -> p hh f", f=128), [src], [sdb])

        with cx.scope():
            cx.S.mark('sample_front_done')
            Sst = cx.sb("rw_S", [128, 8192])
            tmpS = cx.sb("rw_tmpS", [128, 4096])
            ops_ = cx.sb("rw_ops", [128, 6, 4, 128])
            skk = cx.sb("rw_skk", [128, 128])
            osm = cx.sb("rw_osm", [128, 4, 128])
            cx.dma(Sst[:], smp['wkv0'].rearrange("b (hh hl) v k -> (b hh) (hl v k)", hl=2), [], [Sst])
            for qi in range(6):
                cx.dma(ops_[:, qi, :, :], SD[qi].rearrange("b hh t f -> (b hh) t f"), [sdb], [ops_])
            T3 = tmpS[:].rearrange("p (v k) -> p v k", v=64)

            def bk(qi, t, hl):
                return ops_[:, qi, t, hl * 64:(hl + 1) * 64].unsqueeze(1).to_broadcast([128, 64, 64])

            def bv(ap2):
                return ap2.unsqueeze(2).to_broadcast([128, 64, 64])

            for t in range(4):
                for hl in range(2):
                    S3 = Sst[:, hl * 4096:(hl + 1) * 4096].rearrange("p (v k) -> p v k", v=64)
                    hs = slice(hl * 64, hl * 64 + 64)
                    cx.tt(T3, S3, bk(4, t, hl), ALU.mult, [Sst, ops_], [tmpS])
                    cx.red(skk[:, hs], T3, ALU.add, [tmpS], [skk])
                    cx.tt(S3, S3, bk(1, t, hl), ALU.mult, [Sst, ops_], [Sst])
                    cx.tt(T3, bv(skk[:, hs]), bk(5, t, hl), ALU.mult, [skk, ops_], [tmpS])
                    cx.tt(S3, S3, T3, ALU.subtract, [Sst, tmpS], [Sst])
                    cx.tt(T3, bv(ops_[:, 3, t, hs]), bk(2, t, hl), ALU.mult, [ops_], [tmpS])
                    cx.tt(S3, S3, T3, ALU.add, [Sst, tmpS], [Sst])
                    cx.tt(T3, S3, bk(0, t, hl), ALU.mult, [Sst, ops_], [tmpS])
                    cx.red(osm[:, t, hs], T3, ALU.add, [tmpS], [osm])
            cx.S.mark('sample_rec_done')
            cx.dma(O['s_wkv'].rearrange("b (hh hl) v k -> (b hh) (hl v k)", hl=2), Sst[:], [Sst], [])
            cx.dma(SD[6].rearrange("b hh t f -> (b hh) t f"), osm[:], [osm], [sdb])
            for t in range(4):
                cx.dma(o_[16 * t:16 * t + 16, :].rearrange("p (hh f) -> p hh f", f=128), SD[6, :, :, t, :], [sdb], [o_])
            back(smp, o_, o_)


def mamba_pass(cx, cst, W, tiles, smp, O):
    ident, identb, ones, triu = cst['ident'], cst['identb'], cst['ones'], cst['triu']
    ZO, XO, BO, CO, DO = 0, 2048, 4096, 4608, 5120
    with cx.scope():
        PA = cx.ps("mb_PA", [128, 1024])
        PB = cx.ps("mb_PB", [128, 1024])
        PC = cx.ps("mb_PC", [128, 1024])
        PD = cx.ps("mb_PD", [128, 1024])
        Wout = cx.sb("mb_Wout", [128, 16, 1024], BF16)
        VT = cx.sb("mb_VT", [128, 1024])
        VS = cx.sb("mb_VS", [1, 128])
        vec = cx.sb("mb_vec", [128, 128])
        cx.dma(VT[0:1, :], W['norm_mix'][1:2, :], [], [VT])
        cx.dma(VT[32:33, :], W['mamba_norm_w'][0:1, 0:1024], [], [VT])
        cx.dma(VT[64:65, :], W['mamba_norm_w'][0:1, 1024:2048], [], [VT])
        cx.memset(VS[:], 0.0, [VS])
        cx.dma(VS[0:1, 0:32], W['mamba_dt_bias'][0:1, :], [], [VS])
        cx.dma(VS[0:1, 32:64], W['mamba_a_log'][0:1, :], [], [VS])
        cx.dma(VS[0:1, 64:96], W['mamba_d'][0:1, :], [], [VS])
        cx.mm(PA[:, 0:128], ones[0:1, :], VS[0:1, :], True, True, [ones, VS], [PA])
        cx.copy(vec[:], PA[:, 0:128], [PA], [vec])
        cx.act(vec[:, 32:64], vec[:, 32:64], AF.Exp, [vec], [vec])
        cx.ts(vec[:, 32:64], vec[:, 32:64], -1.0, ALU.mult, [vec], [vec])

        def bcast(p_, C, PP, n=1024):
            for hf in range(n // 512):
                cx.mm(PP[0:C, hf * 512:(hf + 1) * 512], ones[p_:p_ + 1, 0:C], VT[p_:p_ + 1, hf * 512:(hf + 1) * 512], True, True, [ones, VT], [PP])

        ht = cx.sb("mb_ht", [128, 1024])
        ss = cx.sb("mb_ss", [128, 8])
        y_ = cx.sb("mb_y", [128, 2048])
        xn = alias(y_, y_[:, 1024:2048], "mb_xn")
        yjunk = alias(y_, y_[:, 0:1024], "mb_yjunk")
        xs = cx.sb("mb_xs", [128, 2048], BF16)
        tmp = cx.sb("mb_tmp", [128, 512])
        ygb = alias(xs, xs[:, :], "mb_ygb")
        ygT = cx.sb("mb_ygT", [128, 16, 128], BF16)
        xnb = cx.sb("mb_xnb", [128, 1024], BF16)
        xnT = cx.sb("mb_xnT", [128, 8, 128], BF16)

        def norm_in(tl):
            C = tl['C']
            if isinstance(tl['src'], list):
                for (ap, r0, r1) in tl['src']:
                    cx.dma(ht[r0:r1, :], ap, [tl['hb']], [ht])
            else:
                cx.dma(ht[0:C, :], tl['src'], [tl['hb']], [ht])
            bcast(0, C, PD)
            rms_norm_ps(cx, C, ht, PD[0:C, :], PD, xn, yjunk, ss)
            cx.copy(xnb[0:C, :], xn[0:C, :], [xn], [xnb], eng='pool')
            PAb = PA[:].bitcast(BF16)
            for kc in range(8):
                cx.tr(PAb[:, kc * 128:kc * 128 + C], xnb[0:C, kc * 128:(kc + 1) * 128], identb[0:C, 0:C], [xnb, identb], [PA])
            cx.copy(xnT[:, :, 0:C], PAb[:, 0:1024].rearrange("p (a b) -> p a b", b=128)[:, :, 0:C], [PA], [xnT])

        def zgate(C, Win, j):
            pp = (PA, PB)[j % 2]
            for kc in range(8):
                cx.mm(pp[0:C, 0:512], xnT[:, kc, 0:C], Win[:, kc, ZO + j * 512:ZO + (j + 1) * 512], kc == 0, kc == 7, [xnT, Win], [pp])
            cx.act(tmp[0:C, :], pp[0:C, 0:512], AF.Silu, [pp], [tmp])

        def back(tl, Win, zsrc=None, zb=None):
            C = tl['C']
            for j in range(4):
                if zsrc is None:
                    zgate(C, Win, j)
                else:
                    cx.dma(tmp[0:C, :], zsrc[:, j * 512:(j + 1) * 512], [zb], [tmp])
                cx.tt(y_[0:C, j * 512:(j + 1) * 512], y_[0:C, j * 512:(j + 1) * 512], tmp[0:C, :], ALU.mult, [y_, tmp], [y_])
                cx.act(tmp[0:C, :], y_[0:C, j * 512:(j + 1) * 512], AF.Square, [y_], [tmp, ss], accum_out=ss[0:C, j:j + 1])
            cx.ts(ss[0:C, 4:8], ss[0:C, 0:4], 1.0 / 512, ALU.mult, [ss], [ss], s2=1e-5, op1=ALU.add)
            cx.act(ss[0:C, 4:8], ss[0:C, 4:8], AF.Sqrt, [ss], [ss])
            cx.op('dve', lambda e: e.reciprocal(out=ss[0:C, 4:8], in_=ss[0:C, 4:8]), [ss], [ss])
            for hf in range(2):
                bcast(32 + 32 * hf, C, PC)
                for j2 in range(2):
                    j = hf * 2 + j2
                    cx.stt(ygb[0:C, j * 512:(j + 1) * 512], y_[0:C, j * 512:(j + 1) * 512], ss[0:C, 4 + j:5 + j], PC[0:C, j2 * 512:(j2 + 1) * 512],
                           ALU.mult, ALU.mult, [y_, ss, PC], [ygb])
            PBb = PB[:].bitcast(BF16)
            for kc in range(16):
                cx.tr(PBb[:, kc * 128:kc * 128 + C], ygb[0:C, kc * 128:(kc + 1) * 128], identb[0:C, 0:C], [ygb, identb], [PB])
            cx.copy(ygT[:, :, 0:C], PBb[:, 0:2048].rearrange("p (a b) -> p a b", b=128)[:, :, 0:C], [PB], [ygT])
            for hf in range(2):
                for kc in range(16):
                    cx.mm(PA[0:C, hf * 512:(hf + 1) * 512], ygT[:, kc, 0:C], Wout[:, kc, hf * 512:(hf + 1) * 512], kc == 0, kc == 15, [ygT, Wout], [PA])
            cx.tt(ht[0:C, :], PA[0:C, :], ht[0:C, :], ALU.add, [PA, ht], [ht])
            cx.dma(tl['dst'], ht[0:C, :], [ht], [tl['hb']])

        with cx.scope():
            Win = cx.sb("mb_Win", [128, 8, 5152], BF16)
            cwv = cx.sb("mb_cwv", [128, 24, 8])
            with cx.scope():
                stg = [cx.sb(f"mb_stg{i}", [128, 1288], F32) for i in range(2)]
                cnt = 0
                for kc in range(8):
                    for cc in range(4):
                        st = stg[cnt % 2]
                        cx.dma(st[:, :], W['mamba_in_proj'][0, kc * 128:(kc + 1) * 128, cc * 1288:(cc + 1) * 1288], [], [st])
                        cx.copy(Win[:, kc, cc * 1288:(cc + 1) * 1288], st[:, :], [st], [Win], eng=('dve', 'pool', 'act')[cnt % 3])
                        cnt += 1
                for kc in range(16):
                    st = stg[cnt % 2]
                    cx.dma(st[:, 0:1024], W['mamba_out_proj'][0, kc * 128:(kc + 1) * 128, :], [], [st])
                    cx.copy(Wout[:, kc, :], st[:, 0:1024], [st], [Wout], eng=('dve', 'pool', 'act')[cnt % 3])
                    cnt += 1
                for c3 in range(3):
                    st = stg[cnt % 2]
                    cnt += 1
                    cx.memset(st[0:32, 0:1024], 0.0, [st])
                    cx.dma(st[0:4, 0:1024], W['mamba_conv_w'][0, :, c3 * 1024:(c3 + 1) * 1024], [], [st])
                    cx.dma(st[4:5, 0:1024], W['mamba_conv_b'][0:1, c3 * 1024:(c3 + 1) * 1024], [], [st])
                    for c8 in range(8):
                        cx.mm(PA[:, c8 * 32:(c8 + 1) * 32], st[0:32, c8 * 128:(c8 + 1) * 128], ident[0:32, 0:32], True, True, [st, ident], [PA])
                    cx.copy(cwv[:, c3 * 8:(c3 + 1) * 8, :], PA[:, 0:256].rearrange("p (a b) -> p a b", b=32)[:, :, 0:8], [PA], [cwv])

            cst3 = cx.sb("mb_cst3", [128, 24, 3])
            full = [cx.sb(f"mb_full{i}", [128, 132]) for i in range(2)]
            acc = [cx.sb(f"mb_acc{i}", [128, 128]) for i in range(2)]
            xbcT = cx.sb("mb_xbcT", [128, 24, 128], BF16)
            dtt = cx.sb("mb_dtt", [128, 160])
            ncv = cx.sb("mb_ncv", [64, 1024])

            def conv_part(tl, sh, Cst):
                C = tl['C']
                for ct in range(24):
                    pp = (PA, PB)[ct % 2]
                    fu = full[ct % 2]
                    ac = acc[ct % 2]
                    for kc in range(8):
                        cx.mm(pp[:, 0:C], Win[:, kc, XO + ct * 128:XO + (ct + 1) * 128], xnT[:, kc, 0:C], kc == 0, kc == 7, [Win, xnT], [pp])
                    if sh == 1:
                        cx.copy(fu[:, 0:3], cst3[:, ct, :], [cst3], [fu], eng='pool')
                        cx.copy(fu[:, 3:3 + C], pp[:, 0:C], [pp], [fu], eng='act')
                        cx.copy(cst3[:, ct, :], fu[:, C:C + 3], [fu], [cst3], eng='pool')
                        for j in range(4):
                            if j == 0:
                                cx.ts(ac[:, 0:C], fu[:, 0:C], cwv[:, ct, 0:1], ALU.mult, [fu, cwv], [ac], s2=cwv[:, ct, 4:5], op1=ALU.add)
                            else:
                                cx.stt(ac[:, 0:C], fu[:, j:j + C], cwv[:, ct, j:j + 1], ac[:, 0:C], ALU.mult, ALU.add, [fu, cwv, ac], [ac])
                    else:
                        cx.copy(fu[:, 0:48], Cst[:, ct, :, :].rearrange("p a b -> p (a b)"), [Cst], [fu], eng='pool')
                        cx.copy(fu[:, 48:48 + C], pp[:, 0:C], [pp], [fu], eng='act')
                        for j in range(4):
                            if j == 0:
                                cx.ts(ac[:, 0:C], fu[:, 0:C], cwv[:, ct, 0:1], ALU.mult, [fu, cwv], [ac], s2=cwv[:, ct, 4:5], op1=ALU.add)
                            else:
                                cx.stt(ac[:, 0:C], fu[:, 16 * j:16 * j + C], cwv[:, ct, j:j + 1], ac[:, 0:C], ALU.mult, ALU.add, [fu, cwv, ac], [ac])
                    cx.act(xbcT[:, ct, 0:C], ac[:, 0:C], AF.Silu, [ac], [xbcT])

            def dt_part(tl):
                C = tl['C']
                for kc in range(8):
                    cx.mm(PC[0:C, 0:32], xnT[:, kc, 0:C], Win[:, kc, DO:DO + 32], kc == 0, kc == 7, [xnT, Win], [PC])
                cx.tt(dtt[0:C, 0:32], PC[0:C, 0:32], vec[0:C, 0:32], ALU.add, [PC, vec], [dtt])
                cx.act(dtt[0:C, 0:32], dtt[0:C, 0:32], AF.Exp, [dtt], [dtt])
                cx.act(dtt[0:C, 0:32], dtt[0:C, 0:32], AF.Ln, [dtt], [dtt], bias=ones[0:C, 0:1])
                cx.tt(dtt[0:C, 32:64], dtt[0:C, 0:32], vec[0:C, 32:64], ALU.mult, [dtt, vec], [dtt])

            def x_tok(tl):
                C = tl['C']
                PBb = PB[:].bitcast(BF16)
                for ct in range(16):
                    cx.tr(PBb[0:C, ct * 128:(ct + 1) * 128], xbcT[:, ct, 0:C], identb[:, :], [xbcT, identb], [PB])
                cx.copy(xs[0:C, :], PBb[0:C, 0:2048], [PB], [xs])

            def newconv_rows(tl, r0, nrows, dsts):
                for c3 in range(3):
                    for c2 in range(2):
                        c6 = c3 * 2 + c2
                        pp = (PC, PD)[c6 % 2]
                        for kc in range(8):
                            cx.mm(pp[0:nrows, 0:512], xnT[:, kc, r0:r0 + nrows], Win[:, kc, XO + c6 * 512:XO + (c6 + 1) * 512], kc == 0, kc == 7, [xnT, Win], [pp])
                        cx.copy(ncv[0:nrows, c2 * 512:(c2 + 1) * 512], pp[0:nrows, 0:512], [pp], [ncv], eng=('act', 'dve')[c6 % 2])
                    for (ap, a, b) in dsts:
                        cx.dma(ap[:, c3 * 1024:(c3 + 1) * 1024], ncv[a:b, :], [ncv], [])

            with cx.scope():
                Hs = cx.sb("mb_H", [128, 2048])
                Hb = cx.sb("mb_Hb", [128, 2048], BF16)
                xdt = cx.sb("mb_xdt", [128, 2048], BF16)
                xdtd = cx.sb("mb_xdtd", [128, 2048], BF16)
                Btok = cx.sb("mb_Btok", [128, 512], BF16)
                cbm = cx.sb("mb_cbm", [128, 4, 128], BF16)
                acsT = cx.sb("mb_acsT", [32, 128])
                LT = cx.sb("mb_LT", [128, 512])
                MT = cx.sb("mb_MT", [128, 4, 128], BF16)
                cdb = cx.sb("mb_cdb", [128, 32])
                cx.memset(Hs[:], 0.0, [Hs])
                cx.memset(Hb[:], 0.0, [Hb], eng='pool')
                cx.memset(cst3[:], 0.0, [cst3])
                for ti, tl in enumerate(tiles):
                    C = tl['C']
                    cx.S.mark(f'mb_tile{ti}')
                    norm_in(tl)
                    conv_part(tl, 1, None)
                    if ti == len(tiles) - 1:
                        newconv_rows(tl, C - 4, 4, [(O['p_conv'], 1, 4)])
                    dt_part(tl)
                    x_tok(tl)
                    PBb = PB[:].bitcast(BF16)
                    for g in range(4):
                        cx.tr(PBb[0:C, g * 128:(g + 1) * 128], xbcT[:, 16 + g, 0:C], identb[:, :], [xbcT, identb], [PB])
                    cx.copy(Btok[0:C, :], PBb[0:C, 0:512], [PB], [Btok])
                    cx.mm(PC[0:C, 0:32], triu[0:C, 0:C], dtt[0:C, 32:64], True, True, [triu, dtt], [PC])
                    cx.copy(dtt[0:C, 64:96], PC[0:C, 0:32], [PC], [dtt])
                    cx.mm(PC[:, 32:64], ones[0:C, :], dtt[0:C, 32:64], True, True, [ones, dtt], [PC])
                    cx.act(cdb[:, :], PC[:, 32:64], AF.Exp, [PC], [cdb])
                    cx.tt(dtt[0:C, 128:160], PC[0:C, 32:64], dtt[0:C, 64:96], ALU.subtract, [PC, dtt], [dtt])
                    cx.act(dtt[0:C, 128:160], dtt[0:C, 128:160], AF.Exp, [dtt], [dtt])
                    cx.act(dtt[0:C, 96:128], dtt[0:C, 64:96], AF.Exp, [dtt], [dtt])
                    x3 = xs[0:C, :].rearrange("p (h q) -> p h q", q=64)
                    cx.tt(xdt[0:C, :].rearrange("p (h q) -> p h q", q=64), x3, dtt[0:C, 0:32].unsqueeze(2).to_broadcast([C, 32, 64]), ALU.mult, [xs, dtt], [xdt])
                    cx.tt(xdtd[0:C, :].rearrange("p (h q) -> p h q", q=64), xdt[0:C, :].rearrange("p (h q) -> p h q", q=64),
                          dtt[0:C, 128:160].unsqueeze(2).to_broadcast([C, 32, 64]), ALU.mult, [xdt, dtt], [xdtd], eng='pool')
                    cx.mm(PC[0:32, 64:64 + C], dtt[0:C, 64:96], ident[0:C, 0:C], True, True, [dtt, ident], [PC])
                    cx.copy(acsT[:, 0:C], PC[0:32, 64:64 + C], [PC], [acsT])
                    for g in range(4):
                        cx.mm(PD[0:C, g * 128:g * 128 + C], xbcT[:, 16 + g, 0:C], xbcT[:, 20 + g, 0:C], True, True, [xbcT], [PD])
                    cx.tt(cbm[0:C, :, 0:C], PD[0:C, 0:512].rearrange("p (g c) -> p g c", g=4)[:, :, 0:C], triu[0:C, 0:C].unsqueeze(1).to_broadcast([C, 4, C]), ALU.mult, [PD, triu], [cbm])
                    for g in range(4):
                        pY = (PA, PB)[g % 2]
                        for h4 in range(2):
                            h0 = g * 8 + h4 * 4
                            for q in range(4):
                                h = h0 + q
                                cx.mm(PC[0:C, 512 + q * 128:512 + q * 128 + C], ident[0:32, h:h + 1].to_broadcast([32, C]), acsT[:, 0:C], True, True, [ident, acsT], [PC])
                            L3 = LT[0:C, :].rearrange("p (q c) -> p q c", q=4)[:, :, 0:C]
                            cx.tt(L3, PC[0:C, 512:1024].rearrange("p (q c) -> p q c", q=4)[:, :, 0:C],
                                  dtt[0:C, 64 + h0:64 + h0 + 4].unsqueeze(2).to_broadcast([C, 4, C]), ALU.subtract, [PC, dtt], [LT])
                            cx.ts(L3, L3, 0.0, ALU.min, [LT], [LT])
                            cx.act(L3, L3, AF.Exp, [LT], [LT])
                            cx.tt(MT[0:C, :, 0:C], L3, cbm[0:C, g:g + 1, 0:C].to_broadcast([C, 4, C]), ALU.mult, [LT, cbm], [MT])
                            for q in range(4):
                                h = h0 + q
                                cx.mm(pY[0:C, (h4 * 4 + q) * 64:(h4 * 4 + q + 1) * 64], MT[0:C, q, 0:C], xdt[0:C, h * 64:(h + 1) * 64], True, True, [MT, xdt], [pY])
                        cx.mm(pY[0:C, 512:1024], xbcT[:, 20 + g, 0:C], Hb[:, g * 512:(g + 1) * 512], True, True, [xbcT, Hb], [pY])
                        cx.tt(tmp[0:C, :].rearrange("p (h q) -> p h q", q=64), pY[0:C, 512:1024].rearrange("p (h q) -> p h q", q=64),
                              dtt[0:C, 96 + g * 8:96 + g * 8 + 8].unsqueeze(2).to_broadcast([C, 8, 64]), ALU.mult, [pY, dtt], [tmp])
                        cx.tt(y_[0:C, g * 512:(g + 1) * 512], pY[0:C, 0:512], tmp[0:C, :], ALU.add, [pY, tmp], [y_])
                    for g in range(4):
                        pp = (PC, PD)[g % 2]
                        cx.mm(pp[:, 0:512], Btok[0:C, g * 128:(g + 1) * 128], xdtd[0:C, g * 512:(g + 1) * 512], True, True, [Btok, xdtd], [pp])
                        H3 = Hs[:, g * 512:(g + 1) * 512].rearrange("p (h q) -> p h q", q=64)
                        cx.tt(H3, H3, cdb[:, g * 8:(g + 1) * 8].unsqueeze(2).to_broadcast([128, 8, 64]), ALU.mult, [Hs, cdb], [Hs])
                        cx.tt(Hs[:, g * 512:(g + 1) * 512], Hs[:, g * 512:(g + 1) * 512], pp[:, 0:512], ALU.add, [Hs, pp], [Hs])
                    cx.copy(Hb[:], Hs[:], [Hs], [Hb], eng='pool')
                    for g in range(4):
                        cx.tt(tmp[0:C, :].rearrange("p (h q) -> p h q", q=64), xs[0:C, g * 512:(g + 1) * 512].rearrange("p (h q) -> p h q", q=64),
                              vec[0:C, 64 + g * 8:64 + g * 8 + 8].unsqueeze(2).to_broadcast([C, 8, 64]), ALU.mult, [xs, vec], [tmp])
                        cx.tt(y_[0:C, g * 512:(g + 1) * 512], y_[0:C, g * 512:(g + 1) * 512], tmp[0:C, :], ALU.add, [y_, tmp], [y_])
                    back(tl, Win)
                HTo = alias(y_, y_[:, :], "mb_HTo")
                for rnd in range(8):
                    for q in range(2):
                        c16 = rnd * 2 + q
                        pp = (PA, PB)[q]
                        cx.tr(pp[:, 0:128], Hs[:, c16 * 128:(c16 + 1) * 128], ident[:, :], [Hs, ident], [pp])
                        cx.copy(HTo[:, c16 * 128:(c16 + 1) * 128], pp[:, 0:128], [pp], [HTo], eng=('dve', 'act')[q])
                cx.dma(O['p_ssm'].rearrange("(c hl) p n -> (hl p) c n", hl=2), HTo[:, :].rearrange("q (c n) -> q c n", n=128), [HTo], [])

            cx.S.mark('mb_sample_front')
            C = 64
            Cst = cx.sb("mb_Cst", [128, 24, 3, 16])
            for c3 in range(3):
                for j in range(3):
                    cx.dma(ncv[16 * j:16 * j + 16, :], smp['conv0'][:, j, c3 * 1024:(c3 + 1) * 1024], [], [ncv])
                for c8 in range(8):
                    ct = c3 * 8 + c8
                    pp = (PA, PB)[ct % 2]
                    cx.tr(pp[:, 0:48], ncv[0:48, c8 * 128:(c8 + 1) * 128], ident[0:48, 0:48], [ncv, ident], [pp])
                    cx.copy(Cst[:, ct, :, :].rearrange("p a b -> p (a b)"), pp[:, 0:48], [pp], [Cst], eng=('dve', 'act')[ct % 2])
            norm_in(smp)
            conv_part(smp, 16, Cst)
            newconv_rows(smp, 0, 64, [(smp['conv_out'][:, t - 1, :], 16 * t, 16 * t + 16) for t in range(1, 4)])
            dt_part(smp)
            x_tok(smp)
            SD = smp['scr']
            sdb = Buf("mb_sd")
            bct = alias(ygT, ygT[:].rearrange("p a b -> p (a b)")[:, 0:1024], "mb_bct")
            PBb = PB[:].bitcast(BF16)
            for g in range(8):
                cx.tr(PBb[0:C, g * 128:(g + 1) * 128], xbcT[:, 16 + g, 0:C], identb[:, :], [xbcT, identb], [PB])
            cx.copy(bct[0:C, 0:1024], PBb[0:C, 0:1024], [PB], [bct])
            cx.act(dtt[0:C, 64:96], dtt[0:C, 32:64], AF.Exp, [dtt], [dtt])
            cx.tt(y_[0:C, :].rearrange("p (h q) -> p h q", q=64), xs[0:C, :].rearrange("p (h q) -> p h q", q=64),
                  dtt[0:C, 0:32].unsqueeze(2).to_broadcast([C, 32, 64]), ALU.mult, [xs, dtt], [y_])
            cx.dma(SD['dtx'], y_[0:C, :], [y_], [sdb])
            cx.dma(SD['da'], dtt[0:C, 64:96], [dtt], [sdb])
            for z, nm in enumerate(('b', 'c')):
                cx.copy(tmp[0:C, :], bct[0:C, z * 512:(z + 1) * 512], [bct], [tmp])
                cx.dma(SD[nm], tmp[0:C, :], [tmp], [sdb])
            for j in range(4):
                zgate(C, Win, j)
                cx.dma(SD['z'][:, j * 512:(j + 1) * 512], tmp[0:C, :], [tmp], [sdb])

        cx.S.mark('mb_sample_rec')
        with cx.scope():
            Sst = cx.sb("mb_S", [128, 8192])
            tmpS = cx.sb("mb_tmpS", [128, 8192])
            dtx = cx.sb("mb_dtx", [128, 4, 64])
            dA = cx.sb("mb_dA", [128, 4])
            bcs = cx.sb("mb_bcs", [16, 4, 256])
            cx.memset(bcs[:], 0.0, [bcs])
            BC = cx.sb("mb_BC", [128, 4, 256])
            ysm = cx.sb("mb_ysm", [128, 4, 64])
            E = cx.sb("mb_E", [16, 128])
            ie = cx.sb("mb_ie", [16, 128], I32)
            e2 = cx.sb("mb_e2", [16, 128])
            cx.op('pool', lambda e: e.iota(ie[:], pattern=[[1, 128]], base=0, channel_multiplier=-8), [], [ie])
            cx.op('dve', lambda e: e.tensor_single_scalar(out=E[:], in_=ie[:], scalar=0, op=ALU.is_ge), [ie], [E])
            cx.op('dve', lambda e: e.tensor_single_scalar(out=e2[:], in_=ie[:], scalar=8, op=ALU.is_lt), [ie], [e2])
            cx.tt(E[:], E[:], e2[:], ALU.mult, [E, e2], [E])
            S3 = Sst[:].rearrange("p (q n) -> p q n", n=128)
            T3 = tmpS[:].rearrange("p (q n) -> p q n", n=128)
            for qb in range(4):
                b0 = qb * 4
                cx.dma(Sst[:], smp['ssm0'][b0:b0 + 4].rearrange("b h p n -> (b h) (p n)"), [], [Sst])
                cx.dma(dtx[:], SD['dtx'].rearrange("(t b) (h p) -> b h t p", b=16, p=64)[b0:b0 + 4].rearrange("b h t p -> (b h) t p"), [sdb], [dtx])
                cx.dma(dA[:], SD['da'].rearrange("(t b) h -> b h t", b=16)[b0:b0 + 4].rearrange("b h t -> (b h) t"), [sdb], [dA], allow_slow_non_contiguous=True)
                for z, nm in enumerate(('b', 'c')):
                    cx.dma(bcs[:, :, z * 128:(z + 1) * 128],
                           SD[nm].rearrange("(t b) (g n) -> b g t n", b=16, g=4)[b0:b0 + 4].rearrange("b g t n -> (b g) t n"), [sdb], [bcs])
                for t2 in range(2):
                    cx.mm(PA[:, t2 * 512:(t2 + 1) * 512], E[:, :], bcs[:, 2 * t2:2 * t2 + 2, :].rearrange("p a b -> p (a b)"), True, True, [E, bcs], [PA])
                cx.copy(BC[:].rearrange("p a b -> p (a b)"), PA[:, :], [PA], [BC])
                for t in range(4):
                    cx.ts(Sst[:], Sst[:], dA[:, t:t + 1], ALU.mult, [Sst, dA], [Sst])
                    cx.tt(T3, dtx[:, t, :].unsqueeze(2).to_broadcast([128, 64, 128]), BC[:, t, 0:128].unsqueeze(1).to_broadcast([128, 64, 128]), ALU.mult, [dtx, BC], [tmpS])
                    cx.tt(Sst[:], Sst[:], tmpS[:], ALU.add, [Sst, tmpS], [Sst])
                    cx.tt(T3, S3, BC[:, t, 128:256].unsqueeze(1).to_broadcast([128, 64, 128]), ALU.mult, [Sst, BC], [tmpS])
                    cx.red(ysm[:, t, :], T3, ALU.add, [tmpS], [ysm])
                cx.dma(O['s_ssm'][b0:b0 + 4].rearrange("b h p n -> (b h) (p n)"), Sst[:], [Sst], [])
                cx.dma(SD['y'].rearrange("(t b) (h p) -> b h t p", b=16, p=64)[b0:b0 + 4].rearrange("b h t p -> (b h) t p"), ysm[:], [ysm], [sdb])
            cx.dma(y_[0:64, :], SD['y'], [sdb], [y_])
            C = 64
            for g in range(4):
                cx.tt(tmp[0:C, :].rearrange("p (h q) -> p h q", q=64), xs[0:C, g * 512:(g + 1) * 512].rearrange("p (h q) -> p h q", q=64),
                      vec[0:C, 64 + g * 8:64 + g * 8 + 8].unsqueeze(2).to_broadcast([C, 8, 64]), ALU.mult, [xs, vec], [tmp])
                cx.tt(y_[0:C, g * 512:(g + 1) * 512], y_[0:C, g * 512:(g + 1) * 512], tmp[0:C, :], ALU.add, [y_, tmp], [y_])
            back(smp, None, SD['z'], sdb)


def convert_tables(cx, W):
    with cx.scope():
        stg = [cx.sb(f"cv_stg{i}", [128, 8192], F32) for i in range(4)]
        ob = [cx.sb(f"cv_ob{i}", [128, 8192], BF16) for i in range(4)]
        n = 0
        for l in range(2):
            for z, nm in enumerate(('peer_u', 'peer_v')):
                for c in range(16):
                    st, o = stg[n % 4], ob[n % 4]
                    cx.dma(st[:], W[nm][l, c * 1024:(c + 1) * 1024, :].rearrange("(p r) d -> p (r d)", r=8), [], [st])
                    cx.copy(o[:], st[:], [st], [o], eng=('act', 'dve', 'act', 'pool')[n % 4])
                    r0 = (l * 16 + c) * 1024
                    cx.dma(W['tab_uv'][r0:r0 + 1024, z * 1024:(z + 1) * 1024].rearrange("(p r) d -> p r d", r=8),
                           o[:].rearrange("p (r d) -> p r d", r=8), [o], [W['tabbuf']])
                    n += 1


W_SHAPES = {
    'meta_tokens': [16, 1024], 'norm_mix': [2, 1024], 'norm_ffn': [2, 1024], 'norm_final': [1024],
    'rwkv_mix': [1, 6, 1024], 'rwkv_w_rkv': [1, 3, 1024, 1024], 'rwkv_w0': [1, 1024], 'rwkv_w1': [1, 1024, 64],
    'rwkv_w2': [1, 64, 1024], 'rwkv_a0': [1, 1024], 'rwkv_a1': [1, 1024, 64], 'rwkv_a2': [1, 64, 1024],
    'rwkv_g1': [1, 1024, 160], 'rwkv_g2': [1, 160, 1024], 'rwkv_k_k': [1, 1024], 'rwkv_k_a': [1, 1024],
    'rwkv_r_k': [1, 16, 64], 'rwkv_ln_w': [1, 1024], 'rwkv_ln_b': [1, 1024], 'rwkv_w_o': [1, 1024, 1024],
    'mamba_in_proj': [1, 1024, 5152], 'mamba_conv_w': [1, 4, 3072], 'mamba_conv_b': [1, 3072], 'mamba_dt_bias': [1, 32],
    'mamba_a_log': [1, 32], 'mamba_d': [1, 32], 'mamba_norm_w': [1, 2048], 'mamba_out_proj': [1, 2048, 1024],
    'peer_w_q': [2, 1024, 2048], 'peer_sub_keys': [2, 2, 8, 128, 128], 'peer_u': [2, 16384, 1024], 'peer_v': [2, 16384, 1024],
}
IN_SHAPES = {
    'xp': [2048, 1024], 'xs': [16, 4, 1024], 'sh0': [16, 1024], 'wkv0': [16, 16, 64, 64],
    'conv0': [16, 3, 3072], 'ssm0': [16, 32, 64, 128],
}
OUT_SHAPES = {
    'y_p': [2048, 1024], 'y_s': [16, 4, 1024], 'p_shift': [1, 1024], 'p_wkv': [16, 64, 64], 'p_conv': [3, 3072],
    'p_ssm': [32, 64, 128], 's_shift': [16, 1024], 's_wkv': [16, 16, 64, 64], 's_conv': [16, 3, 3072], 's_ssm': [16, 32, 64, 128],
}
NPT = 16


def build_program(npt=NPT, debug=False):
    nc = bass.Bass("TRN2", target_bir_lowering=False)
    W = {n: nc.dram_tensor(n, s, F32, kind="ExternalInput").ap() for n, s in W_SHAPES.items()}
    I = {n: nc.dram_tensor(n, s, F32, kind="ExternalInput").ap() for n, s in IN_SHAPES.items()}
    Od = {n: nc.dram_tensor(n, s, F32, kind="ExternalOutput").ap() for n, s in OUT_SHAPES.items()}
    hscr = nc.dram_tensor("hscr", [npt + 2, 128, 1024], F32, kind="Internal").ap()
    rw_scr = nc.dram_tensor("rw_scr", [7, 16, 8, 4, 128], F32, kind="Internal").ap()
    SD = {'dtx': nc.dram_tensor("sd_dtx", [64, 2048], F32, kind="Internal").ap(), 'da': nc.dram_tensor("sd_da", [64, 32], F32, kind="Internal").ap(),
          'b': nc.dram_tensor("sd_b", [64, 512], F32, kind="Internal").ap(), 'c': nc.dram_tensor("sd_c", [64, 512], F32, kind="Internal").ap(),
          'z': nc.dram_tensor("sd_z", [64, 2048], F32, kind="Internal").ap(), 'y': nc.dram_tensor("sd_y", [64, 2048], F32, kind="Internal").ap()}
    dbg = nc.dram_tensor("dbg_h", [3, npt + 2, 128, 1024], F32, kind="ExternalOutput").ap() if debug else None
    W['tab_uv'] = nc.dram_tensor("tab_uv", [2 * 16384, 2048], BF16, kind="Internal").ap()
    W['tabbuf'] = Buf("tabbuf")
    with ExitStack() as es:
        cx = Ctx(nc, es)
        cst = make_consts(cx)
        convert_tables(cx, W)
        hb = [Buf(f"h{i}") for i in range(npt + 2)]

        def snap(k):
            if debug:
                for i in range(npt + 2):
                    cx.dma(dbg[k, i], hscr[i], [hb[i]], [])

        Cs = [16] + [128] * npt
        first = [dict(C=16, src=W['meta_tokens'][:, :], dst=hscr[0, 0:16, :], hb=hb[0])]
        for i in range(npt):
            first.append(dict(C=128, src=I['xp'][i * 128:(i + 1) * 128, :], dst=hscr[i + 1, :, :], hb=hb[i + 1]))
        smp = dict(C=64, src=[(I['xs'][:, t, :], 16 * t, 16 * t + 16) for t in range(4)], dst=hscr[npt + 1, 0:64, :], hb=hb[npt + 1],
                   shift0=I['sh0'], wkv0=I['wkv0'], scr=rw_scr)
        rwkv_pass(cx, cst, W, first, smp, dict(p_shift=Od['p_shift'], s_shift=Od['s_shift'], p_wkv=Od['p_wkv'], s_wkv=Od['s_wkv']))
        inpl = [dict(C=Cs[i], src=hscr[i, 0:Cs[i], :], dst=hscr[i, 0:Cs[i], :], hb=hb[i]) for i in range(npt + 1)]
        inpl.append(dict(C=64, src=hscr[npt + 1, 0:64, :], dst=hscr[npt + 1, 0:64, :], hb=hb[npt + 1]))

        dbgp = nc.dram_tensor("dbg_p", [10, 128, 128], F32, kind="ExternalOutput").ap() if debug else None

        def peer(layer, gfin):
            with cx.scope():
                P = {'a': cx.ps("pa", [128, 512]), 'b': cx.ps("pb", [128, 512]), 'x0': cx.ps("px0", [128, 1024]),
                     'x1': cx.ps("px1", [128, 1024]), 'y': cx.ps("py", [128, 1024])}
                tl = [dict(t) for t in inpl]
                if gfin:
                    tl[0]['fin'] = []
                    for i in range(npt):
                        tl[i + 1]['fin'] = [(Od['y_p'][i * 128:(i + 1) * 128, :], 0, 128)]
                    tl[npt + 1]['fin'] = [(Od['y_s'][:, t, :], 16 * t, 16 * t + 16) for t in range(4)]
                peer_pass(cx, cst, W, layer, tl, P, gfin=(True if gfin else None), dbgp=(dbgp if layer == 0 else None))

        snap(0)
        peer(0, False)
        snap(1)
        smp2 = dict(C=64, src=hscr[npt + 1, 0:64, :], dst=hscr[npt + 1, 0:64, :], hb=hb[npt + 1],
                    conv0=I['conv0'], ssm0=I['ssm0'], scr=SD, conv_out=Od['s_conv'])
        mamba_pass(cx, cst, W, inpl[:npt + 1], smp2, dict(p_conv=Od['p_conv'], p_ssm=Od['p_ssm'], s_ssm=Od['s_ssm']))
        snap(2)
        peer(1, True)
        cx.S.finish()
        block = es.enter_context(nc.Block())
        cx.S.emit(block)
    return nc


_NC_CACHE = {}


def kernel(**inputs):
    inputs = {k: np.ascontiguousarray(np.asarray(v), dtype=np.float32) for k, v in inputs.items()}
    if 'nc' not in _NC_CACHE:
        _NC_CACHE['nc'] = build_program()
    nc = _NC_CACHE['nc']
    in_maps = []
    for c in range(NCORES):
        m = {n: inputs[n] for n in W_SHAPES}
        m['xp'] = inputs['x_prompt'][c]
        m['xs'] = inputs['x_sample'][16 * c:16 * c + 16]
        m['sh0'] = inputs['state_rwkv_shift'][0, 16 * c:16 * c + 16]
        m['wkv0'] = inputs['state_rwkv_wkv'][0, 16 * c:16 * c + 16]
        m['conv0'] = inputs['state_mamba_conv'][0, 16 * c:16 * c + 16]
        m['ssm0'] = inputs['state_mamba_ssm'][0, 16 * c:16 * c + 16]
        in_maps.append(m)
    res = run_bass_kernel_spmd(nc, in_maps, core_ids=list(range(NCORES)))
    R = res.results
    cat = lambda k: np.concatenate([R[c][k] for c in range(NCORES)], axis=0)
    stk = lambda k: np.stack([R[c][k] for c in range(NCORES)], axis=0)
    y_p = stk('y_p')
    y_s = cat('y_s')
    p_shift = cat('p_shift')[None]
    p_wkv = stk('p_wkv')[None]
    p_conv = stk('p_conv')[None]
    p_ssm = stk('p_ssm')[None]
    s_shift = cat('s_shift')[None]
    s_wkv = cat('s_wkv')[None]
    s_conv = cat('s_conv')[None]
    s_ssm = cat('s_ssm')[None]
    return (y_p, y_s, p_shift, p_wkv, p_conv, p_ssm, s_shift, s_wkv, s_conv, s_ssm)
```

```python
from contextlib import ExitStack
import os
import numpy as np
import concourse.bass as bass
import concourse.mybir as mybir
from concourse.bass_utils import run_bass_kernel_spmd

F32 = mybir.dt.float32
BF16 = mybir.dt.bfloat16
U32 = mybir.dt.uint32
I32 = mybir.dt.int32
AF = mybir.ActivationFunctionType
ALU = mybir.AluOpType
AX = mybir.AxisListType

D = 1024
NCORES = 8
NEG = -1.0e30

ENGS = ['pe', 'dve', 'act', 'pool', 'sp']
EPOCH = 16384
NEPOCH = {'pe': 7, 'dve': 7, 'act': 4, 'pool': 4, 'sp': 1}
NDSEM = 72
NDHW = 40
SELFSYNC = os.environ.get("K_SELFSYNC", "1") == "1"


class Buf:
    __slots__ = ('name', 'lw', 'rc', 'rd', 'excl')

    def __init__(self, name):
        self.name = name
        self.excl = False
        self.lw = None
        self.rc = {}
        self.rd = {}


class Sched:
    def __init__(self, nc, es):
        self.nc = nc
        self.q = {e: [] for e in ENGS}
        self.cnt = {e: 0 for e in ENGS}
        self.known = {e: {f: 0 for f in ENGS} for e in ENGS}
        self.sems = {e: [es.enter_context(nc.semaphore(f"s_{e}_{i}")) for i in range(NEPOCH[e])] for e in ENGS}
        self.dsems = [es.enter_context(nc.semaphore(f"sd_{i}")) for i in range(NDSEM)]
        self.dcount = [0] * NDSEM
        self.dknown = {e: [0] * NDSEM for e in ENGS}
        self.dnext = 0
        self.dnext_sw = 0
        self.n = 0
        self.limit = int(os.environ.get('K_LIMIT', '0'))
        self.phase = ''
        self.marks = []

    def mark(self, name):
        self.marks.append((self.n, name))

    def op(self, eng, fn, reads=(), writes=(), dma=False):
        self.n += 1
        if self.limit and self.n > self.limit:
            return None
        ex = [b for b in reads if b.excl]
        if ex:
            reads = [b for b in reads if not b.excl]
            writes = list(writes) + [b for b in ex if b not in writes]
        deps = []
        for b in list(reads) + list(writes):
            if b.lw is not None:
                deps.append(b.lw)
        for b in writes:
            for e2, seq in b.rc.items():
                deps.append(('c', e2, seq))
            for si, tgt in b.rd.items():
                deps.append(('d', si, tgt))
        waits = []
        for d in deps:
            if d[0] == 'c':
                _, e2, seq = d
                if e2 == eng and not dma and (eng == 'pe' or not SELFSYNC):
                    continue
                if self.known[eng][e2] >= seq:
                    continue
                self.known[eng][e2] = seq
                waits.append((self.sems[e2][(seq - 1) // EPOCH], (seq - 1) % EPOCH + 1))
            else:
                _, si, tgt = d
                if self.dknown[eng][si] >= tgt:
                    continue
                self.dknown[eng][si] = tgt
                waits.append((self.dsems[si], tgt))
        if dma:
            if eng == 'pool':
                si = NDHW + self.dnext_sw
                self.dnext_sw = (self.dnext_sw + 1) % (NDSEM - NDHW)
            else:
                si = self.dnext
                self.dnext = (self.dnext + 1) % NDHW
            prev = self.dcount[si]
            if prev > 0 and self.dknown[eng][si] < prev:
                self.dknown[eng][si] = prev
                waits.append((self.dsems[si], prev))
            self.dcount[si] = prev + 16
            tok = ('d', si, prev + 16)
            inc = (self.dsems[si], 16)
        else:
            self.cnt[eng] += 1
            seq = self.cnt[eng]
            assert seq <= EPOCH * NEPOCH[eng]
            tok = ('c', eng, seq)
            inc = (self.sems[eng][(seq - 1) // EPOCH], 1)
        for b in reads:
            if tok[0] == 'c':
                b.rc[tok[1]] = tok[2]
            else:
                b.rd[tok[1]] = tok[2]
        for b in writes:
            b.lw = tok
            b.rc = {}
            b.rd = {}
        self.q[eng].append((fn, waits, inc))
        return tok

    def barrier(self):
        for eng in ENGS:
            waits = []
            for si in range(NDSEM):
                if self.dcount[si] > self.dknown[eng][si]:
                    waits.append((self.dsems[si], self.dcount[si]))
                    self.dknown[eng][si] = self.dcount[si]
            for e in ENGS:
                seq = self.cnt[e]
                if seq > self.known[eng][e]:
                    waits.append((self.sems[e][(seq - 1) // EPOCH], (seq - 1) % EPOCH + 1))
                    self.known[eng][e] = seq
            if waits:
                self.q[eng].append((None, waits, None))

    def finish(self):
        waits = []
        for si in range(NDSEM):
            if self.dcount[si] > 0:
                waits.append((self.dsems[si], self.dcount[si]))
        for e in ENGS:
            if e != 'sp' and self.cnt[e] > 0:
                seq = self.cnt[e]
                waits.append((self.sems[e][(seq - 1) // EPOCH], (seq - 1) % EPOCH + 1))
        self.q['sp'].append((None, waits, None))

    def emit(self, block):
        def run(eng, name):
            for fn, waits, inc in self.q[name]:
                for sem, v in waits:
                    eng.wait_ge(sem, v)
                if fn is not None:
                    ins = fn(eng)
                    ins.then_inc(inc[0], inc[1])

        @block.tensor
        def _(e):
            run(e, 'pe')

        @block.vector
        def _(e):
            run(e, 'dve')

        @block.scalar
        def _(e):
            run(e, 'act')

        @block.gpsimd
        def _(e):
            run(e, 'pool')

        @block.sync
        def _(e):
            run(e, 'sp')


class _Scope:
    def __init__(self, cx):
        self.cx = cx

    def __enter__(self):
        self.old = self.cx.es
        self.es = ExitStack()
        self.cx.es = self.es
        return self

    def __exit__(self, *a):
        self.cx.S.barrier()
        self.es.close()
        self.cx.es = self.old
        return False


class T:
    def __init__(self, h, name):
        self.h = h
        self.b = Buf(name)

    def __getitem__(self, k):
        return self.h[k]


class Ctx:
    def __init__(self, nc, es):
        self.nc = nc
        self.es = es
        self.S = Sched(nc, es)
        self.dq = 0
        self.uid = 0

    def sb(self, name, shape, dt=F32):
        self.uid += 1
        name = f"{name}_{self.uid}"
        return T(self.es.enter_context(self.nc.sbuf_tensor(name, list(shape), dt)), name)

    def ps(self, name, shape, dt=F32):
        self.uid += 1
        name = f"{name}_{self.uid}"
        t = T(self.es.enter_context(self.nc.psum_tensor(name, list(shape), dt)), name)
        t.b.excl = True
        return t

    def scope(self):
        return _Scope(self)

    def dma(self, out, in_, reads=(), writes=(), eng=None, **kw):
        if eng is None:
            eng = ('sp', 'act')[self.dq % 2]
            self.dq += 1
        self.S.op(eng, lambda e: e.dma_start(out=out, in_=in_, **kw), reads=[r.b if isinstance(r, T) else r for r in reads],
                  writes=[w.b if isinstance(w, T) else w for w in writes], dma=True)

    def op(self, eng, fn, reads=(), writes=()):
        self.S.op(eng, fn, reads=[r.b if isinstance(r, T) else r for r in reads],
                  writes=[w.b if isinstance(w, T) else w for w in writes])

    def mm(self, out, lhsT, rhs, start, stop, reads, writes):
        self.op('pe', lambda e: e.matmul(out, lhsT, rhs, start=start, stop=stop), reads, writes)

    def tr(self, out, in_, ident, reads, writes):
        self.op('pe', lambda e: e.transpose(out, in_, ident), reads, writes)

    def act(self, out, in_, func, reads, writes, bias=None, scale=None, accum_out=None, eng='act'):
        kw = {}
        if bias is not None:
            kw['bias'] = bias
        if scale is not None:
            kw['scale'] = scale
        if accum_out is not None:
            kw['accum_out'] = accum_out
        self.op('act', lambda e: e.activation(out=out, in_=in_, func=func, **kw), reads, writes)

    def tt(self, out, in0, in1, op, reads, writes, eng='dve'):
        self.op(eng, lambda e: e.tensor_tensor(out=out, in0=in0, in1=in1, op=op), reads, writes)

    def ts(self, out, in0, s1, op0, reads, writes, s2=None, op1=None, eng='dve', accum_out=None):
        kw = {}
        if op1 is not None:
            kw['op1'] = op1
        if accum_out is not None:
            kw['accum_out'] = accum_out
        self.op(eng, lambda e: e.tensor_scalar(out=out, in0=in0, scalar1=s1, scalar2=s2, op0=op0, **kw), reads, writes)

    def stt(self, out, in0, scalar, in1, op0, op1, reads, writes, accum_out=None):
        kw = {}
        if accum_out is not None:
            kw['accum_out'] = accum_out
        self.op('dve', lambda e: e.scalar_tensor_tensor(out=out, in0=in0, scalar=scalar, in1=in1, op0=op0, op1=op1, **kw),
                reads, writes)

    def copy(self, out, in_, reads, writes, eng='dve'):
        if eng == 'act':
            self.op('act', lambda e: e.activation(out=out, in_=in_, func=AF.Copy), reads, writes)
        else:
            self.op(eng, lambda e: e.tensor_copy(out=out, in_=in_), reads, writes)

    def red(self, out, in_, op, reads, writes, axis=AX.X):
        self.op('dve', lambda e: e.tensor_reduce(out=out, in_=in_, axis=axis, op=op), reads, writes)

    def memset(self, ap, val, writes, eng='dve'):
        self.op(eng, lambda e: e.memset(ap, val), [], writes)


def make_consts(cx):
    nc = cx.nc
    c = {}
    it = cx.sb("c_iota", [128, 255], I32)
    cx.op('pool', lambda e: e.iota(it[:, 0:128], pattern=[[1, 128]], base=0, channel_multiplier=-1), [], [it])
    c['ident'] = cx.sb("c_ident", [128, 128], F32)
    c['identb'] = cx.sb("c_identb", [128, 128], BF16)
    c['triu'] = cx.sb("c_triu", [128, 128], F32)
    c['triu_s'] = cx.sb("c_trius", [128, 128], F32)
    c['tril_s'] = cx.sb("c_trils", [128, 128], F32)
    c['ones'] = cx.sb("c_ones", [128, 128], F32)
    cx.op('dve', lambda e: e.tensor_single_scalar(out=c['ident'][:], in_=it[:, 0:128], scalar=0, op=ALU.is_equal), [it], [c['ident']])
    cx.op('dve', lambda e: e.tensor_single_scalar(out=c['identb'][:], in_=it[:, 0:128], scalar=0, op=ALU.is_equal), [it], [c['identb']])
    cx.op('dve', lambda e: e.tensor_single_scalar(out=c['triu'][:], in_=it[:, 0:128], scalar=0, op=ALU.is_ge), [it], [c['triu']])
    cx.op('dve', lambda e: e.tensor_single_scalar(out=c['triu_s'][:], in_=it[:, 0:128], scalar=0, op=ALU.is_gt), [it], [c['triu_s']])
    cx.op('dve', lambda e: e.tensor_single_scalar(out=c['tril_s'][:], in_=it[:, 0:128], scalar=0, op=ALU.is_lt), [it], [c['tril_s']])
    cx.memset(c['ones'][:], 1.0, [c['ones']])
    it2 = cx.sb("c_iota2", [128, 255], I32)
    cx.op('pool', lambda e: e.iota(it2[:], pattern=[[1, 255]], base=-127, channel_multiplier=0), [], [it2])
    c['w1'] = cx.sb("c_w1", [128, 255], F32)
    cx.op('dve', lambda e: e.tensor_single_scalar(out=c['w1'][:], in_=it2[:], scalar=0, op=ALU.is_equal), [it2], [c['w1']])
    c['w1b'] = cx.sb("c_w1b", [128, 255], BF16)
    cx.op('dve', lambda e: e.tensor_single_scalar(out=c['w1b'][:], in_=it2[:], scalar=0, op=ALU.is_equal), [it2], [c['w1b']])
    c['iota16'] = cx.sb("c_iota16", [128, 16], F32)
    cx.op('dve', lambda e: e.tensor_single_scalar(out=c['iota16'][:], in_=it2[:, 127:143], scalar=0, op=ALU.add), [it2], [c['iota16']])
    c['eps'] = cx.sb("c_eps", [128, 1], F32)
    cx.memset(c['eps'][:], 1e-5, [c['eps']])
    return c


def rms_norm(cx, C, ht, gbc, xn, junk, ss, eps=1e-5, n=1024):
    cx.act(junk[0:C, 0:n], ht[0:C, 0:n], AF.Square, [ht], [junk, ss], accum_out=ss[0:C, 0:1])
    cx.ts(ss[0:C, 1:2], ss[0:C, 0:1], 1.0 / n, ALU.mult, [ss], [ss], s2=eps, op1=ALU.add)
    cx.act(ss[0:C, 1:2], ss[0:C, 1:2], AF.Sqrt, [ss], [ss])
    cx.op('dve', lambda e: e.reciprocal(out=ss[0:C, 1:2], in_=ss[0:C, 1:2]), [ss], [ss])
    cx.stt(xn[0:C, 0:n], ht[0:C, 0:n], ss[0:C, 1:2], gbc[0:C, 0:n], ALU.mult, ALU.mult, [ht, ss, gbc], [xn])


def peer_pass(cx, cst, W, layer, tiles, P, gfin=None, dbgp=None):
    nc = cx.nc
    l = layer
    wq = cx.sb(f"pw_wq{l}", [128, 8, 2048], F32)
    keysT = cx.sb(f"pw_keysT{l}", [128, 16, 128], F32)
    gbc = cx.sb(f"pw_g{l}", [128, 1024], F32)
    wqb = [Buf(f"wq{kc}") for kc in range(8)]
    for kc in range(8):
        cx.dma(wq[:, kc, :], W['peer_w_q'][l, kc * 128:(kc + 1) * 128, :], [], [wqb[kc]])
    cx.dma(gbc[:], W['norm_ffn'][l].partition_broadcast(128), [], [gbc])
    if gfin:
        gfb = cx.sb(f"pw_gf{l}", [128, 1024], F32)
        cx.dma(gfb[:], W['norm_final'].partition_broadcast(128), [], [gfb])
    pA, pB = P['a'], P['b']
    with cx.scope():
        kraw = cx.sb(f"pw_kraw{l}", [128, 16, 128], F32)
        cx.dma(kraw[:], W['peer_sub_keys'][l].rearrange("z h k d -> k (z h) d"), [], [kraw])
        for zh in range(16):
            z, h = zh // 8, zh % 8
            qb = h * 2 + z
            pp = (pA, pB)[zh % 2]
            cx.tr(pp[:, 0:128], kraw[:, zh, :], cst['ident'][:], [kraw, cst['ident']], [pp])
            cx.copy(keysT[:, qb, :], pp[:, 0:128], [pp], [keysT], eng=('dve', 'act')[zh % 2])

    ht2 = [cx.sb(f"pe_ht{l}_{i}", [128, 1024], F32) for i in range(2)]
    xnf = cx.sb(f"pe_xnf{l}", [128, 1024], F32)
    ssf = cx.sb(f"pe_ssf{l}", [128, 2], F32)
    xn = cx.sb(f"pe_xn{l}", [128, 1024], F32)
    xnb2 = [cx.sb(f"pe_xnb{l}_{i}", [128, 1024], BF16) for i in range(2)]
    junk = cx.sb(f"pe_junk{l}", [128, 1024], F32)
    ss = cx.sb(f"pe_ss{l}", [128, 2], F32)
    xnT = cx.sb(f"pe_xnT{l}", [128, 8, 128], F32)
    qT = cx.sb(f"pe_qT{l}", [128, 16, 128], F32)
    sc = cx.sb(f"pe_sc{l}", [128, 16, 128], F32)
    sc2 = cx.sb(f"pe_sc2{l}", [128, 16, 128], F32)
    cand = cx.sb(f"pe_cand{l}", [128, 8, 256], F32)
    oh = alias(sc, sc[:].rearrange("p a b -> p (a b)").rearrange("p (h x) -> p h x", h=8), f"pe_oh{l}")
    sv = cx.sb(f"pe_sv{l}", [128, 16, 16], F32)
    si = cx.sb(f"pe_si{l}", [128, 16, 16], U32)
    sif = cx.sb(f"pe_sif{l}", [128, 16, 16], F32)
    tv = cx.sb(f"pe_tv{l}", [128, 8, 16], F32)
    pos = cx.sb(f"pe_pos{l}", [128, 8, 16], U32)
    pij = cx.sb(f"pe_pij{l}", [128, 2, 128], U32)
    pijf = cx.sb(f"pe_pijf{l}", [128, 2, 128], F32)
    e01 = cx.sb(f"pe_e01{l}", [128, 2, 128], F32)
    eidx = cx.sb(f"pe_eidx{l}", [128, 128], F32)
    gate = cx.sb(f"pe_gate{l}", [128, 128], F32)
    zz = cx.sb(f"pe_zz{l}", [128, 16], F32)
    mx8 = cx.sb(f"pe_mx8{l}", [128, 8], F32)
    eidxT2 = [cx.sb(f"pe_eidxT{l}_{i}", [128, 128], U32) for i in range(2)]
    gateT2 = [cx.sb(f"pe_gateT{l}_{i}", [128, 128], F32) for i in range(2)]
    aT = cx.sb(f"pe_aT{l}", [128, 128], F32)
    coefT = cx.sb(f"pe_coefT{l}", [128, 128], F32)
    NBV, NBL = 7, 6
    uvb = [cx.sb(f"pe_uv{l}_{i}", [128, 2048], BF16) for i in range(NBV)]
    lb = [cx.sb(f"pe_lb{l}_{i}", [128, 128], BF16) for i in range(NBL)]
    acol = [cx.sb(f"pe_ac{l}_{i}", [128, 2], F32) for i in range(4)]
    junk2 = [cx.sb(f"pe_jk{l}_{i}", [128, 1024], BF16) for i in range(2)]
    xbc = [P['x0'], P['x1']]
    Y = P['y']
    ident, identb = cst['ident'], cst['identb']
    uvtab = W['tab_uv']
    eoff = l * 16384 * 2048

    def prep(ti, tl):
        C = tl['C']
        ht, xnb, eidxT, gateT = ht2[ti % 2], xnb2[ti % 2], eidxT2[ti % 2], gateT2[ti % 2]
        cx.dma(ht[0:C, :], tl['src'], [tl['hb']], [ht])
        rms_norm(cx, C, ht, gbc, xn, junk, ss)
        cx.copy(xnb[0:C, :], xn[0:C, :], [xn], [xnb], eng='pool')
        for kc in range(8):
            pp = (pA, pB)[kc % 2]
            cx.tr(pp[:, 0:C], xn[0:C, kc * 128:(kc + 1) * 128], ident[0:C, 0:C], [xn, ident], [pp])
            cx.copy(xnT[:, kc, 0:C], pp[:, 0:C], [pp], [xnT], eng=('dve', 'act')[kc % 2])
        for qb in range(16):
            pp = (pA, pB)[qb % 2]
            for kc in range(8):
                cx.mm(pp[:, 0:C], wq[:, kc, qb * 128:(qb + 1) * 128], xnT[:, kc, 0:C], kc == 0, kc == 7, [wqb[kc], xnT], [pp])
            cx.copy(qT[:, qb, 0:C], pp[:, 0:C], [pp], [qT], eng=('dve', 'act')[qb % 2])
            yield
        for qb in range(16):
            pp = (pA, pB)[qb % 2]
            cx.mm(pp[0:C, 0:128], qT[:, qb, 0:C], keysT[:, qb, :], True, True, [qT, keysT], [pp])
            cx.copy(sc[0:C, qb, :], pp[0:C, 0:128], [pp], [sc], eng='act')
            if qb % 4 == 3:
                yield
        for qb in range(16):
            cx.op('dve', lambda e, qb=qb: e.max(out=sv[0:C, qb, 0:8], in_=sc[0:C, qb, :]), [sc], [sv])
            cx.op('dve', lambda e, qb=qb: e.match_replace(out=sc2[0:C, qb, :], in_to_replace=sv[0:C, qb, 0:8],
                                                         in_values=sc[0:C, qb, :], imm_value=NEG), [sc, sv], [sc2])
            cx.op('dve', lambda e, qb=qb: e.max(out=sv[0:C, qb, 8:16], in_=sc2[0:C, qb, :]), [sc2], [sv])
            cx.op('dve', lambda e, qb=qb: e.max_index(out=si[0:C, qb, 0:8], in_max=sv[0:C, qb, 0:8], in_values=sc[0:C, qb, :]), [sc, sv], [si])
            cx.op('dve', lambda e, qb=qb: e.max_index(out=si[0:C, qb, 8:16], in_max=sv[0:C, qb, 8:16], in_values=sc2[0:C, qb, :]), [sc2, sv], [si])
            yield
        cx.copy(sif[0:C], si[0:C], [si], [sif])
        svv = sv[0:C].rearrange("p (h z) m -> p h z m", z=2)
        sifv = sif[0:C].rearrange("p (h z) m -> p h z m", z=2)
        candv = cand[0:C].rearrange("p h (i j) -> p h i j", j=16)
        cx.tt(candv, svv[:, :, 0, :].unsqueeze(3).to_broadcast([C, 8, 16, 16]),
              svv[:, :, 1, :].unsqueeze(2).to_broadcast([C, 8, 16, 16]), ALU.add, [sv], [cand])
        c2 = sc2[0:C].rearrange("p a b -> p (a b)").rearrange("p (h x) -> p h x", h=8)
        for h in range(8):
            cx.op('dve', lambda e, h=h: e.max(out=tv[0:C, h, 0:8], in_=cand[0:C, h, :]), [cand], [tv])
            cx.op('dve', lambda e, h=h: e.match_replace(out=c2[:, h, :], in_to_replace=tv[0:C, h, 0:8],
                                                       in_values=cand[0:C, h, :], imm_value=NEG), [cand, tv], [sc2])
            cx.op('dve', lambda e, h=h: e.max(out=tv[0:C, h, 8:16], in_=c2[:, h, :]), [sc2], [tv])
            cx.op('dve', lambda e, h=h: e.max_index(out=pos[0:C, h, 0:8], in_max=tv[0:C, h, 0:8], in_values=cand[0:C, h, :]), [cand, tv], [pos])
            cx.op('dve', lambda e, h=h: e.max_index(out=pos[0:C, h, 8:16], in_max=tv[0:C, h, 8:16], in_values=c2[:, h, :]), [sc2, tv], [pos])
            yield
        posf = pos[0:C].rearrange("p h m -> p (h m)")
        cx.op('dve', lambda e: e.tensor_single_scalar(out=pij[0:C, 0, :], in_=posf, scalar=4, op=ALU.logical_shift_right), [pos], [pij])
        cx.op('dve', lambda e: e.tensor_single_scalar(out=pij[0:C, 1, :], in_=posf, scalar=15, op=ALU.bitwise_and), [pos], [pij])
        cx.copy(pijf[0:C], pij[0:C], [pij], [pijf])
        ohv = oh[0:C].rearrange("p h (m i) -> p h m i", i=16)
        io = cst['iota16']
        for z in range(2):
            pv = pijf[0:C, z, :].rearrange("p (h m) -> p h m", h=8)
            cx.tt(ohv, pv.unsqueeze(3).to_broadcast([C, 8, 16, 16]),
                  io[0:C, :].unsqueeze(1).unsqueeze(1).to_broadcast([C, 8, 16, 16]), ALU.is_equal, [pijf, io], [oh])
            cx.tt(ohv, ohv, sifv[:, :, z, :].unsqueeze(2).to_broadcast([C, 8, 16, 16]), ALU.mult, [oh, sif], [oh])
            cx.red(e01[0:C, z, :], oh[0:C].rearrange("p h (m i) -> p (h m) i", i=16), ALU.add, [oh], [e01])
            yield
        cx.stt(eidx[0:C, :], e01[0:C, 0, :], 128.0, e01[0:C, 1, :], ALU.mult, ALU.add, [e01], [eidx])
        cx.copy(mx8[0:C, :], tv[0:C, :, 0], [tv], [mx8], eng='act')
        cx.tt(tv[0:C], tv[0:C], mx8[0:C, :].unsqueeze(2).to_broadcast([C, 8, 16]), ALU.subtract, [tv, mx8], [tv])
        gv = gate[0:C].rearrange("p (h m) -> p h m", h=8)
        cx.act(gv, tv[0:C], AF.Exp, [tv], [gate])
        cx.red(zz[0:C, 0:8], gv, ALU.add, [gate], [zz])
        cx.op('dve', lambda e: e.reciprocal(out=zz[0:C, 8:16], in_=zz[0:C, 0:8]), [zz], [zz])
        cx.tt(gv, gv, zz[0:C, 8:16].unsqueeze(2).to_broadcast([C, 8, 16]), ALU.mult, [gate, zz], [gate])
        cx.tr(pA[:, 0:C], eidx[0:C, :], ident[0:C, 0:C], [eidx, ident], [pA])
        cx.copy(eidxT[:, 0:C], pA[:, 0:C], [pA], [eidxT])
        cx.tr(pB[:, 0:C], gate[0:C, :], ident[0:C, 0:C], [gate, ident], [pB])
        cx.copy(gateT[:, 0:C], pB[:, 0:C], [pB], [gateT], eng='act')

    def gath(ti, tl):
        C = tl['C']
        ht, xnb, eidxT, gateT = ht2[ti % 2], xnb2[ti % 2], eidxT2[ti % 2], gateT2[ti % 2]
        w1 = cst['w1b']
        for step in range(C + 3):
            j = step
            if j < C:
                uv = uvb[j % NBV]
                xb = xbc[j % 2]
                jk = junk2[j % 2]
                cx.S.op('pool', lambda e, uv=uv, j=j: e.indirect_dma_start(
                    out=uv[:], out_offset=None, in_=uvtab,
                    in_offset=bass.IndirectOffsetOnAxis(ap=eidxT[:, j:j + 1], axis=0), element_offset=eoff),
                    reads=[eidxT.b], writes=[uv.b], dma=True)
                for hf in range(2):
                    cx.mm(xb[:, hf * 512:(hf + 1) * 512], identb[0:C, j:j + 1].to_broadcast([C, 128]),
                          xnb[0:C, hf * 512:(hf + 1) * 512], True, True, [identb, xnb], [xb])
                ac = acol[j % 4]
                cx.stt(jk[:], uv[:, 0:1024], 1.0, xb[:], ALU.mult, ALU.mult, [uv, xb], [jk, ac], accum_out=ac[:, 0:1])
                cx.act(ac[:, 1:2], ac[:, 0:1], AF.Gelu, [ac], [ac])
            j1 = step - 1
            if 0 <= j1 < C:
                L = lb[j1 % NBL]
                ac = acol[j1 % 4]
                cx.ts(L[:, 0:C], w1[:, 127 - j1:127 - j1 + C], ac[:, 1:2], ALU.mult, [w1, ac, gateT], [L], s2=gateT[:, j1:j1 + 1], op1=ALU.mult)
            j2 = step - 3
            if j2 >= 0:
                L = lb[j2 % NBL]
                vv = uvb[j2 % NBV]
                for hf in range(2):
                    cx.mm(Y[0:C, hf * 512:(hf + 1) * 512], L[:, 0:C], vv[:, 1024 + hf * 512:1024 + (hf + 1) * 512], j2 == 0, j2 == C - 1, [L, vv], [Y])
            yield
        cx.tt(ht[0:C, :], Y[0:C, :], ht[0:C, :], ALU.add, [Y, ht], [ht])
        cx.dma(tl['dst'], ht[0:C, :], [ht], [tl['hb']])
        if gfin:
            rms_norm(cx, C, ht, gfb, xnf, junk2[0], ssf)
            for (oap, r0, r1) in tl['fin']:
                cx.dma(oap, xnf[r0:r1, :], [xnf], [])

    for _ in prep(0, tiles[0]):
        pass
    for ti, tl in enumerate(tiles):
        nxt = prep(ti + 1, tiles[ti + 1]) if ti + 1 < len(tiles) else None
        k = 0
        for _ in gath(ti, tl):
            k += 1
            if nxt is not None and k % 2 == 0:
                try:
                    next(nxt)
                except StopIteration:
                    nxt = None
        if nxt is not None:
            for _ in nxt:
                pass


def load_cast(cx, dst, src_ap, stg, i, P, ncols):
    st = stg[i % len(stg)]
    cx.dma(st[0:P, 0:ncols], src_ap, [], [st])
    eng = ('dve', 'pool', 'act')[i % 3]
    return st, eng


def load_w_kc(cx, dstT, src2d, stg, cnt, ncols):
    K = src2d.shape[0]
    for kc in range((K + 127) // 128):
        P = min(128, K - kc * 128)
        st, eng = load_cast(cx, None, src2d[kc * 128:kc * 128 + P, :], stg, cnt[0], P, ncols)
        cx.copy(dstT[0:P, kc, 0:ncols], st[0:P, 0:ncols], [st], [dstT], eng=eng)
        cnt[0] += 1


def bcast_vec(cx, name, src1d, n=1024):
    t = cx.sb(name, [128, n], F32)
    cx.dma(t[:], src1d.partition_broadcast(128), [], [t])
    return t


def alias(parent, ap, name):
    t = T(ap, name)
    t.b = parent.b
    return t


def rms_norm_ps(cx, C, ht, gps, gT, xn, junk, ss, eps=1e-5, n=1024):
    cx.act(junk[0:C, 0:n], ht[0:C, 0:n], AF.Square, [ht], [junk, ss], accum_out=ss[0:C, 0:1])
    cx.ts(ss[0:C, 1:2], ss[0:C, 0:1], 1.0 / n, ALU.mult, [ss], [ss], s2=eps, op1=ALU.add)
    cx.act(ss[0:C, 1:2], ss[0:C, 1:2], AF.Sqrt, [ss], [ss])
    cx.op('dve', lambda e: e.reciprocal(out=ss[0:C, 1:2], in_=ss[0:C, 1:2]), [ss], [ss])
    cx.stt(xn[0:C, 0:n], ht[0:C, 0:n], ss[0:C, 1:2], gps, ALU.mult, ALU.mult, [ht, ss, gT], [xn])


def rwkv_pass(cx, cst, W, tiles, smp, O):
    ident, identb, ones = cst['ident'], cst['identb'], cst['ones']
    triu, triu_s, tril_s = cst['triu'], cst['triu_s'], cst['tril_s']
    NH = 16
    with cx.scope():
        PA = cx.ps("rw_PA", [128, 1024])
        PB = cx.ps("rw_PB", [128, 1024])
        PC = cx.ps("rw_PC", [128, 1024])
        PD = cx.ps("rw_PD", [128, 1024])
        Wo = cx.sb("rw_Wo", [128, 8, 1024], BF16)
        VT = [cx.sb(f"rw_VT{i}", [128, 1024]) for i in range(3)]
        vmap = {'g': (0, 0), 'kk': (0, 32), 'ka': (0, 64), 'rk': (1, 0), 'lnw': (1, 32), 'lnb': (1, 64), 'w0': (2, 0), 'a0': (2, 32)}
        for nm, src in (('g', W['norm_mix'][0:1, :]), ('kk', W['rwkv_k_k'][0:1, :]), ('ka', W['rwkv_k_a'][0:1, :]),
                        ('rk', W['rwkv_r_k'].rearrange("o h k -> o (h k)")), ('lnw', W['rwkv_ln_w'][0:1, :]), ('lnb', W['rwkv_ln_b'][0:1, :]),
                        ('w0', W['rwkv_w0'][0:1, :]), ('a0', W['rwkv_a0'][0:1, :])):
            ti_, p_ = vmap[nm]
            cx.dma(VT[ti_][p_:p_ + 1, :], src, [], [VT[ti_]])

        def vrow(nm, hf):
            ti_, p_ = vmap[nm]
            return ones[p_:p_ + 1, :], VT[ti_][p_:p_ + 1, hf * 512:(hf + 1) * 512], VT[ti_]

        def bcast(nm, C, PP):
            for hf in range(2):
                l, r, t = vrow(nm, hf)
                cx.mm(PP[0:C, hf * 512:(hf + 1) * 512], l[:, 0:C], r, True, True, [ones, t], [PP])

        ht = cx.sb("rw_ht", [128, 1024])
        xn = cx.sb("rw_xn", [128, 1024]); t1 = xn
        ss = cx.sb("rw_ss", [128, 2])
        r_ = cx.sb("rw_r", [128, 1024]); junk = r_
        k_ = cx.sb("rw_k", [128, 1024]); o_ = k_
        v_ = cx.sb("rw_v", [128, 1024])
        vb = cx.sb("rw_vb", [128, 1024], BF16)
        lw = cx.sb("rw_lw", [128, 1024])
        a_ = cx.sb("rw_a", [128, 1024])
        g_ = cx.sb("rw_g_", [128, 1024], BF16)
        kk = cx.sb("rw_kk_", [128, 1024])
        kp = cx.sb("rw_kp", [128, 1024])
        b_ = cx.sb("rw_b", [128, 1024])
        bon = cx.sb("rw_bon", [128, 1024])
        st16 = cx.sb("rw_st16", [128, 64])
        ogb = cx.sb("rw_ogb", [128, 1024], BF16)
        ogT = cx.sb("rw_ogT", [128, 8, 128], BF16)
        prevT = cx.sb("rw_prevT", [128, 8, 16])
        cx.memset(prevT[:], 0.0, [prevT])

        def back(tl, osrc, oT):
            C = tl['C']
            o3 = osrc[0:C, :].rearrange("p (h k) -> p h k", k=64)
            cx.red(st16[0:C, 0:16], o3, ALU.add, [oT], [st16])
            cx.act(t1[0:C, :], osrc[0:C, :], AF.Square, [oT], [t1])
            cx.red(st16[0:C, 16:32], t1[0:C, :].rearrange("p (h k) -> p h k", k=64), ALU.add, [t1], [st16])
            cx.ts(st16[0:C, 0:32], st16[0:C, 0:32], 1.0 / 64, ALU.mult, [st16], [st16])
            cx.tt(st16[0:C, 48:64], st16[0:C, 0:16], st16[0:C, 0:16], ALU.mult, [st16], [st16])
            cx.tt(st16[0:C, 16:32], st16[0:C, 16:32], st16[0:C, 48:64], ALU.subtract, [st16], [st16])
            cx.ts(st16[0:C, 16:32], st16[0:C, 16:32], 64e-5, ALU.add, [st16], [st16])
            cx.act(st16[0:C, 16:32], st16[0:C, 16:32], AF.Sqrt, [st16], [st16])
            cx.op('dve', lambda e: e.reciprocal(out=st16[0:C, 16:32], in_=st16[0:C, 16:32]), [st16], [st16])
            t3 = t1[0:C, :].rearrange("p (h k) -> p h k", k=64)
            cx.tt(t3, o3, st16[0:C, 0:16].unsqueeze(2).to_broadcast([C, 16, 64]), ALU.subtract, [oT, st16], [t1])
            cx.tt(t3, t3, st16[0:C, 16:32].unsqueeze(2).to_broadcast([C, 16, 64]), ALU.mult, [t1, st16], [t1])
            bcast('lnw', C, PC)
            cx.tt(t1[0:C, :], t1[0:C, :], PC[0:C, :], ALU.mult, [t1, PC], [t1])
            bcast('lnb', C, PD)
            cx.tt(t1[0:C, :], t1[0:C, :], PD[0:C, :], ALU.add, [t1, PD], [t1])
            cx.tt(t1[0:C, :], t1[0:C, :], bon[0:C, :], ALU.add, [t1, bon], [t1])
            cx.tt(ogb[0:C, :], t1[0:C, :], g_[0:C, :], ALU.mult, [t1, g_], [ogb])
            PBb = PB[:].bitcast(BF16)
            for kc in range(8):
                cx.tr(PBb[:, kc * 128:kc * 128 + C], ogb[0:C, kc * 128:(kc + 1) * 128], identb[0:C, 0:C], [ogb, identb], [PB])
            cx.copy(ogT[:, :, 0:C], PBb[:, 0:1024].rearrange("p (a b) -> p a b", b=128)[:, :, 0:C], [PB], [ogT])
            for hf in range(2):
                for kc in range(8):
                    cx.mm(PA[0:C, hf * 512:(hf + 1) * 512], ogT[:, kc, 0:C], Wo[:, kc, hf * 512:(hf + 1) * 512], kc == 0, kc == 7, [ogT, Wo], [PA])
            cx.tt(ht[0:C, :], PA[0:C, :], ht[0:C, :], ALU.add, [PA, ht], [ht])
            cx.dma(tl['dst'], ht[0:C, :], [ht], [tl['hb']])

        with cx.scope():
            Wr = cx.sb("rw_Wr", [128, 8, 1024], BF16)
            Wk = cx.sb("rw_Wk", [128, 8, 1024], BF16)
            Wv = cx.sb("rw_Wv", [128, 8, 1024], BF16)
            Wl = cx.sb("rw_Wl", [128, 8, 288], BF16)
            W2 = cx.sb("rw_W2", [128, 4, 1024], BF16)
            mixv = cx.sb("rw_mixv", [128, 8, 6], F32)
            with cx.scope():
                stg = [cx.sb(f"rw_stg{i}", [128, 1024], F32) for i in range(2)]
                cnt = [0]
                load_w_kc(cx, Wr, W['rwkv_w_rkv'][0, 0], stg, cnt, 1024)
                load_w_kc(cx, Wk, W['rwkv_w_rkv'][0, 1], stg, cnt, 1024)
                load_w_kc(cx, Wv, W['rwkv_w_rkv'][0, 2], stg, cnt, 1024)
                load_w_kc(cx, Wo, W['rwkv_w_o'][0], stg, cnt, 1024)
                for (nm, c0, n) in (('rwkv_w1', 0, 64), ('rwkv_a1', 64, 64), ('rwkv_g1', 128, 160)):
                    for kc in range(8):
                        st = stg[cnt[0] % 2]
                        cx.dma(st[:, 0:n], W[nm][0, kc * 128:(kc + 1) * 128, :], [], [st])
                        cx.copy(Wl[:, kc, c0:c0 + n], st[:, 0:n], [st], [Wl], eng=('dve', 'pool')[cnt[0] % 2])
                        cnt[0] += 1
                for (nm, slot, r0, nr) in (('rwkv_w2', 0, 0, 64), ('rwkv_a2', 1, 0, 64), ('rwkv_g2', 2, 0, 128), ('rwkv_g2', 3, 128, 32)):
                    st = stg[cnt[0] % 2]
                    cx.dma(st[0:nr, :], W[nm][0, r0:r0 + nr, :], [], [st])
                    cx.copy(W2[0:nr, slot, :], st[0:nr, :], [st], [W2], eng=('dve', 'pool')[cnt[0] % 2])
                    cnt[0] += 1
                cx.memset(stg[0][0:32, :], 0.0, [stg[0]])
                cx.dma(stg[0][0:6, :], W['rwkv_mix'][0], [], [stg[0]])
                for kc in range(8):
                    cx.mm(PA[:, kc * 32:(kc + 1) * 32], stg[0][0:32, kc * 128:(kc + 1) * 128], ident[0:32, 0:32], True, True, [stg[0], ident], [PA])
                cx.copy(mixv[:], PA[:, 0:256].rearrange("p (a b) -> p a b", b=32)[:, :, 0:6], [PA], [mixv])

            xnT = cx.sb("rw_xnT", [128, 8, 128])
            dxT = cx.sb("rw_dxT", [128, 8, 128])
            tmpT = cx.sb("rw_tmpT", [128, 8, 128])
            mT = [alias(ogT, ogT[:], "rw_mT0"), cx.sb("rw_mT1", [128, 8, 128], BF16)]
            hT = cx.sb("rw_hT", [128, 2, 128], BF16)

            def front(tl, sh, after_norm=None):
                C = tl['C']
                if isinstance(tl['src'], list):
                    for (ap, r0, r1) in tl['src']:
                        cx.dma(ht[r0:r1, :], ap, [tl['hb']], [ht])
                else:
                    cx.dma(ht[0:C, :], tl['src'], [tl['hb']], [ht])
                bcast('g', C, PD)
                rms_norm_ps(cx, C, ht, PD[0:C, :], PD, xn, junk, ss)
                if after_norm is not None:
                    after_norm()
                for kc in range(8):
                    pp = (PA, PB)[kc % 2]
                    cx.tr(pp[:, 0:C], xn[0:C, kc * 128:(kc + 1) * 128], ident[0:C, 0:C], [xn, ident], [pp])
                    cx.copy(xnT[:, kc, 0:C], pp[:, 0:C], [pp], [xnT], eng=('dve', 'act')[kc % 2])
                cx.tt(dxT[:, :, 0:sh], prevT[:, :, 0:sh], xnT[:, :, 0:sh], ALU.subtract, [prevT, xnT], [dxT])
                if C > sh:
                    cx.tt(dxT[:, :, sh:C], xnT[:, :, 0:C - sh], xnT[:, :, sh:C], ALU.subtract, [xnT], [dxT])
                cx.copy(prevT[:, :, 0:sh], xnT[:, :, C - sh:C], [xnT], [prevT], eng='pool')

                def mix(j, dst):
                    cx.tt(tmpT[:, :, 0:C], dxT[:, :, 0:C], mixv[:, :, j:j + 1].to_broadcast([128, 8, C]), ALU.mult, [dxT, mixv], [tmpT])
                    cx.tt(dst[:, :, 0:C], tmpT[:, :, 0:C], xnT[:, :, 0:C], ALU.add, [tmpT, xnT], [dst])

                def proj(src, Wm, PP):
                    for hf in range(2):
                        for kc in range(8):
                            cx.mm(PP[0:C, hf * 512:(hf + 1) * 512], src[:, kc, 0:C], Wm[:, kc, hf * 512:(hf + 1) * 512], kc == 0, kc == 7, [src, Wm], [PP])

                mix(0, mT[0]); proj(mT[0], Wr, PA)
                cx.copy(r_[0:C, :], PA[0:C, :], [PA], [r_], eng='act')
                mix(2, mT[1]); proj(mT[1], Wk, PB)
                cx.copy(k_[0:C, :], PB[0:C, :], [PB], [k_], eng='act')
                mix(3, mT[0]); proj(mT[0], Wv, PA)
                cx.copy(v_[0:C, :], PA[0:C, :], [PA], [v_], eng='act')
                cx.copy(vb[0:C, :], PA[0:C, :], [PA], [vb], eng='dve')
                mix(1, mT[1])
                for kc in range(8):
                    cx.mm(PC[0:64, 0:C], Wl[:, kc, 0:64], mT[1][:, kc, 0:C], kc == 0, kc == 7, [Wl, mT[1]], [PC])
                cx.act(hT[0:64, 0, 0:C], PC[0:64, 0:C], AF.Tanh, [PC], [hT])
                for hf in range(2):
                    l, r, t = vrow('w0', hf)
                    cx.mm(PB[0:C, hf * 512:(hf + 1) * 512], l[:, 0:C], r, True, False, [ones, t], [PB])
                    cx.mm(PB[0:C, hf * 512:(hf + 1) * 512], hT[0:64, 0, 0:C], W2[0:64, 0, hf * 512:(hf + 1) * 512], False, True, [hT, W2], [PB])
                cx.act(lw[0:C, :], PB[0:C, :], AF.Sigmoid, [PB], [lw])
                cx.ts(lw[0:C, :], lw[0:C, :], -0.6065306597126334, ALU.mult, [lw], [lw], eng='pool')
                mix(4, mT[0])
                for kc in range(8):
                    cx.mm(PC[0:64, 0:C], Wl[:, kc, 64:128], mT[0][:, kc, 0:C], kc == 0, kc == 7, [Wl, mT[0]], [PC])
                cx.copy(hT[0:64, 0, 0:C], PC[0:64, 0:C], [PC], [hT], eng='act')
                for hf in range(2):
                    l, r, t = vrow('a0', hf)
                    cx.mm(PA[0:C, hf * 512:(hf + 1) * 512], l[:, 0:C], r, True, False, [ones, t], [PA])
                    cx.mm(PA[0:C, hf * 512:(hf + 1) * 512], hT[0:64, 0, 0:C], W2[0:64, 1, hf * 512:(hf + 1) * 512], False, True, [hT, W2], [PA])
                cx.act(a_[0:C, :], PA[0:C, :], AF.Sigmoid, [PA], [a_])
                mix(5, mT[1])
                for (c0, n, slot) in ((128, 128, 0), (256, 32, 1)):
                    for kc in range(8):
                        cx.mm(PC[0:n, 0:C], Wl[:, kc, c0:c0 + n], mT[1][:, kc, 0:C], kc == 0, kc == 7, [Wl, mT[1]], [PC])
                    cx.act(hT[0:n, slot, 0:C], PC[0:n, 0:C], AF.Sigmoid, [PC], [hT])
                for hf in range(2):
                    cx.mm(PB[0:C, hf * 512:(hf + 1) * 512], hT[0:128, 0, 0:C], W2[0:128, 2, hf * 512:(hf + 1) * 512], True, False, [hT, W2], [PB])
                    cx.mm(PB[0:C, hf * 512:(hf + 1) * 512], hT[0:32, 1, 0:C], W2[0:32, 3, hf * 512:(hf + 1) * 512], False, True, [hT, W2], [PB])
                cx.copy(g_[0:C, :], PB[0:C, :], [PB], [g_], eng='act')
                bcast('kk', C, PC)
                cx.tt(kk[0:C, :], k_[0:C, :], PC[0:C, :], ALU.mult, [k_, PC], [kk])
                cx.tt(t1[0:C, :], kk[0:C, :], kk[0:C, :], ALU.mult, [kk], [t1], eng='pool')
                cx.red(st16[0:C, 0:16], t1[0:C, :].rearrange("p (h k) -> p h k", k=64), ALU.add, [t1], [st16])
                cx.act(st16[0:C, 0:16], st16[0:C, 0:16], AF.Sqrt, [st16], [st16])
                cx.ts(st16[0:C, 0:16], st16[0:C, 0:16], 1e-12, ALU.max, [st16], [st16])
                cx.op('dve', lambda e: e.reciprocal(out=st16[0:C, 16:32], in_=st16[0:C, 0:16]), [st16], [st16])
                kk3 = kk[0:C, :].rearrange("p (h k) -> p h k", k=64)
                cx.tt(kk3, kk3, st16[0:C, 16:32].unsqueeze(2).to_broadcast([C, 16, 64]), ALU.mult, [kk, st16], [kk])
                bcast('ka', C, PD)
                cx.stt(t1[0:C, :], a_[0:C, :], -1.0, PD[0:C, :], ALU.add, ALU.mult, [a_, PD], [t1])
                cx.stt(kp[0:C, :], t1[0:C, :], 1.0, k_[0:C, :], ALU.add, ALU.mult, [t1, k_], [kp])
                cx.tt(b_[0:C, :], kk[0:C, :], a_[0:C, :], ALU.mult, [kk, a_], [b_], eng='pool')
                bcast('rk', C, PC)
                cx.tt(t1[0:C, :], r_[0:C, :], kp[0:C, :], ALU.mult, [r_, kp], [t1])
                cx.tt(t1[0:C, :], t1[0:C, :], PC[0:C, :], ALU.mult, [t1, PC], [t1])
                cx.red(st16[0:C, 32:48], t1[0:C, :].rearrange("p (h k) -> p h k", k=64), ALU.add, [t1], [st16])
                cx.tt(bon[0:C, :].rearrange("p (h k) -> p h k", k=64), v_[0:C, :].rearrange("p (h k) -> p h k", k=64),
                      st16[0:C, 32:48].unsqueeze(2).to_broadcast([C, 16, 64]), ALU.mult, [v_, st16], [bon])

            with cx.scope():
                Hs = cx.sb("rw_H", [64, 16, 64])
                Hb = cx.sb("rw_Hb", [64, 16, 64], BF16)
                G = cx.sb("rw_G", [64, 32])
                FT_B = cx.sb("rw_FTB", [64, 16, 128], BF16)
                FT_K = cx.sb("rw_FTK", [64, 16, 128], BF16)
                FT_AR = cx.sb("rw_FTAR", [64, 16, 256], BF16)
                tokb = [alias(ogb, ogb[:], "rw_tokb0"), cx.sb("rw_tokb1", [128, 1024], BF16)]
                xnTb = xnT[:].rearrange("p a b -> p (a b)").bitcast(BF16)
                Kh = alias(xnT, xnTb[:, 0:1024], "rw_Kh")
                Bh = alias(xnT, xnTb[:, 1024:2048], "rw_Bh")
                e1 = alias(tmpT, tmpT[:].rearrange("p a b -> p (a b)"), "rw_e1")
                sets = []
                for i in range(1):
                    sets.append(dict(PTb=cx.sb(f"rw_PTb{i}", [128, 4, 128], BF16), ArbT=cx.sb(f"rw_ArbT{i}", [128, 4, 128], BF16),
                                     AakT=cx.sb(f"rw_AakT{i}", [128, 4, 128], BF16), ArkT=cx.sb(f"rw_ArkT{i}", [128, 4, 128], BF16)))
                Mx = [cx.sb(f"rw_M{i}", [128, 4, 128]) for i in range(2)]
                MTx = [cx.sb(f"rw_MT{i}", [128, 4, 128]) for i in range(2)]
                PTf = cx.sb("rw_PTf", [128, 4, 128])
                Xs = cx.sb("rw_Xs", [128, 256], BF16)
                Us = cx.sb("rw_Us", [128, 256], BF16)
                HT_o = alias(a_, a_[:, 0:512].rearrange("p (a b) -> p a b", b=64), "rw_HTo")
                cx.S.mark('weights_loaded')
                cx.memset(Hs[:], 0.0, [Hs])
                cx.memset(Hb[:], 0.0, [Hb], eng='pool')
                for ti, tl in enumerate(tiles):
                    C = tl['C']
                    cx.S.mark(f'tile{ti}_start')
                    if ti == len(tiles) - 1:
                        front(tl, 1, lambda C=C: cx.dma(O['p_shift'], xn[C - 1:C, :], [xn], []))
                    else:
                        front(tl, 1)
                    for hf in range(2):
                        cx.mm(PA[0:C, hf * 512:(hf + 1) * 512], triu[0:C, 0:C], lw[0:C, hf * 512:(hf + 1) * 512], True, True, [triu, lw], [PA])
                        cx.mm(PB[0:C, hf * 512:(hf + 1) * 512], ones[0:C, 0:C], lw[0:C, hf * 512:(hf + 1) * 512], True, True, [ones, lw], [PB])
                    for h in range(16):
                        cx.mm(PC[0:64, h * 2:h * 2 + 2], lw[0:C, h * 64:(h + 1) * 64], ones[0:C, 0:2], True, True, [lw, ones], [PC])
                    cx.act(G[:, 0:32], PC[0:64, 0:32], AF.Exp, [PC], [G])
                    PCb = PC[:].bitcast(BF16)
                    PDb = PD[:].bitcast(BF16)

                    def to_fm(src, dst, off, pp, ppT):
                        for h in range(16):
                            cx.tr(pp[0:64, h * 128:h * 128 + C], src[0:C, h * 64:(h + 1) * 64], identb[0:C, 0:C], [src, identb], [ppT])
                        cx.copy(dst[:, :, off:off + C], pp[0:64, 0:2048].rearrange("p (a b) -> p a b", b=128)[:, :, 0:C], [ppT], [dst], eng='act')

                    cx.act(e1[0:C, :], PA[0:C, :], AF.Exp, [PA], [e1])
                    cx.tt(tokb[0][0:C, :], r_[0:C, :], e1[0:C, :], ALU.mult, [r_, e1], [tokb[0]])
                    to_fm(tokb[0], FT_AR, C, PCb, PC)
                    cx.act(e1[0:C, :], PA[0:C, :], AF.Exp, [PA], [e1], scale=-1.0)
                    cx.tt(tokb[1][0:C, :], kp[0:C, :], e1[0:C, :], ALU.mult, [kp, e1], [tokb[1]])
                    to_fm(tokb[1], FT_K, 0, PDb, PD)
                    cx.tt(tokb[0][0:C, :], b_[0:C, :], e1[0:C, :], ALU.mult, [b_, e1], [tokb[0]])
                    to_fm(tokb[0], FT_B, 0, PCb, PC)
                    cx.tt(t1[0:C, :], PA[0:C, :], lw[0:C, :], ALU.subtract, [PA, lw], [t1])
                    cx.act(e1[0:C, :], t1[0:C, :], AF.Exp, [t1], [e1])
                    cx.stt(tokb[1][0:C, :], kk[0:C, :], -1.0, e1[0:C, :], ALU.mult, ALU.mult, [kk, e1], [tokb[1]])
                    to_fm(tokb[1], FT_AR, 0, PDb, PD)
                    cx.copy(t1[0:C, :], PA[0:C, :], [PA], [t1], eng='act')
                    cx.tt(t1[0:C, :], PB[0:C, :], t1[0:C, :], ALU.subtract, [PB, t1], [t1])
                    cx.act(e1[0:C, :], t1[0:C, :], AF.Exp, [t1], [e1])
                    cx.tt(Kh[0:C, :], kp[0:C, :], e1[0:C, :], ALU.mult, [kp, e1], [Kh])
                    cx.tt(Bh[0:C, :], b_[0:C, :], e1[0:C, :], ALU.mult, [b_, e1], [Bh], eng='pool')
                    cx.S.mark(f'tile{ti}_prep_done')
                    nl = max(1, int(np.ceil(np.log2(C))))
                    for hg in range(4):
                        st_ = sets[0]
                        PTb, ArbT, AakT, ArkT = st_['PTb'], st_['ArbT'], st_['AakT'], st_['ArkT']
                        for (FT_l, outs) in ((FT_B, ('MT', ArbT)), (FT_K, (AakT, ArkT))):
                            for q in range(4):
                                h = hg * 4 + q
                                pp = (PC, PD)[q // 2]
                                c0 = (q % 2) * 2 * C
                                cx.mm(pp[0:C, c0:c0 + 2 * C], FT_l[0:64, h, 0:C], FT_AR[0:64, h, 0:2 * C], True, True, [FT_l, FT_AR], [pp])
                            for q2 in range(2):
                                pp = (PC, PD)[q2]
                                v4 = pp[0:C, 0:4 * C].rearrange("p (q z c) -> p q z c", q=2, z=2)
                                o0 = MTx[0] if outs[0] == 'MT' else outs[0]
                                cx.tt(o0[0:C, 2 * q2:2 * q2 + 2, 0:C], v4[:, :, 0, :], triu_s[0:C, 0:C].unsqueeze(1).to_broadcast([C, 2, C]), ALU.mult, [pp, triu_s], [o0])
                                cx.tt(outs[1][0:C, 2 * q2:2 * q2 + 2, 0:C], v4[:, :, 1, :], triu[0:C, 0:C].unsqueeze(1).to_broadcast([C, 2, C]), ALU.mult, [pp, triu], [outs[1]])
                        for q in range(4):
                            h = hg * 4 + q
                            cx.mm(PC[0:C, q * C:(q + 1) * C], FT_AR[0:64, h, 0:C], FT_B[0:64, h, 0:C], True, True, [FT_AR, FT_B], [PC])
                        cx.tt(Mx[0][0:C, :, 0:C], PC[0:C, 0:4 * C].rearrange("p (q c) -> p q c", q=4), tril_s[0:C, 0:C].unsqueeze(1).to_broadcast([C, 4, C]), ALU.mult, [PC, tril_s], [Mx[0]])
                        cx.tt(PTf[0:C, :, 0:C], MTx[0][0:C, :, 0:C], ident[0:C, 0:C].unsqueeze(1).to_broadcast([C, 4, C]), ALU.add, [MTx[0], ident], [PTf])
                        cur = 0
                        lo, hi = Buf("pc_lo"), Buf("pc_hi")
                        for bb_ in (lo, hi):
                            bb_.excl = True
                            bb_.lw, bb_.rc, bb_.rd = PC.b.lw, dict(PC.b.rc), dict(PC.b.rd)
                        for lv in range(1, nl):
                            nx = 1 - cur
                            last = (lv == nl - 1)
                            for q in range(4):
                                cx.mm(PC[0:C, q * C:(q + 1) * C], MTx[cur][0:C, q, 0:C], Mx[cur][0:C, q, 0:C], True, True, [MTx[cur], Mx[cur]], [lo])
                            cx.copy(Mx[nx][0:C, :, 0:C], PC[0:C, 0:4 * C].rearrange("p (q c) -> p q c", q=4), [lo], [Mx[nx]], eng='act')
                            if not last:
                                for q in range(4):
                                    cx.mm(PD[0:C, q * C:(q + 1) * C], Mx[cur][0:C, q, 0:C], MTx[cur][0:C, q, 0:C], True, True, [MTx[cur], Mx[cur]], [PD])
                                cx.copy(MTx[nx][0:C, :, 0:C], PD[0:C, 0:4 * C].rearrange("p (q c) -> p q c", q=4), [PD], [MTx[nx]], eng='dve')
                            for q in range(4):
                                cx.mm(PC[0:C, 512 + q * C:512 + (q + 1) * C], Mx[nx][0:C, q, 0:C], PTf[0:C, q, 0:C], True, True, [Mx[nx], PTf], [hi])
                            cx.tt(PTf[0:C, :, 0:C], PTf[0:C, :, 0:C], PC[0:C, 512:512 + 4 * C].rearrange("p (q c) -> p q c", q=4), ALU.add, [PTf, hi], [PTf])
                            cur = nx
                        mrc = {}
                        for bb_ in (lo, hi):
                            for e_, q_ in bb_.rc.items():
                                mrc[e_] = max(mrc.get(e_, 0), q_)
                            if bb_.lw is not None and bb_.lw[0] == 'c':
                                mrc[bb_.lw[1]] = max(mrc.get(bb_.lw[1], 0), bb_.lw[2])
                        PC.b.lw = lo.lw
                        PC.b.rc = mrc
                        PC.b.rd = {**lo.rd, **hi.rd}
                        cx.copy(PTb[0:C, :, 0:C], PTf[0:C, :, 0:C], [PTf], [PTb], eng='pool')
                        cx.S.mark(f'tile{ti}_hg{hg}_inv_done')
                        c4 = hg * 256
                        pX = PA if hg % 2 == 0 else PB
                        for q in range(4):
                            h = hg * 4 + q
                            cx.mm(pX[0:C, q * 64:(q + 1) * 64], FT_AR[0:64, h, 0:C], Hb[0:64, h, :], True, False, [FT_AR, Hb], [pX])
                            cx.mm(pX[0:C, q * 64:(q + 1) * 64], AakT[0:C, q, 0:C], vb[0:C, h * 64:(h + 1) * 64], False, True, [AakT, vb], [pX])
                        cx.copy(Xs[0:C, :], pX[0:C, 0:256], [pX], [Xs], eng='act')
                        for q in range(4):
                            cx.mm(pX[0:C, 256 + q * 64:256 + (q + 1) * 64], PTb[0:C, q, 0:C], Xs[0:C, q * 64:(q + 1) * 64], True, True, [PTb, Xs], [pX])
                        cx.copy(Us[0:C, :], pX[0:C, 256:512], [pX], [Us], eng='act')
                        for q in range(4):
                            h = hg * 4 + q
                            oc = 512 + q * 64
                            cx.mm(pX[0:C, oc:oc + 64], FT_AR[0:64, h, C:2 * C], Hb[0:64, h, :], True, False, [FT_AR, Hb], [pX])
                            cx.mm(pX[0:C, oc:oc + 64], ArkT[0:C, q, 0:C], vb[0:C, h * 64:(h + 1) * 64], False, False, [ArkT, vb], [pX])
                            cx.mm(pX[0:C, oc:oc + 64], ArbT[0:C, q, 0:C], Us[0:C, q * 64:(q + 1) * 64], False, True, [ArbT, Us], [pX])
                        cx.copy(o_[0:C, c4:c4 + 256], pX[0:C, 512:768], [pX], [o_], eng='act')
                        for q in range(4):
                            h = hg * 4 + q
                            oc = 768 + q * 64
                            cx.mm(pX[0:64, oc:oc + 64], Kh[0:C, h * 64:(h + 1) * 64], vb[0:C, h * 64:(h + 1) * 64], True, False, [Kh, vb], [pX])
                            cx.mm(pX[0:64, oc:oc + 64], Bh[0:C, h * 64:(h + 1) * 64], Us[0:C, q * 64:(q + 1) * 64], False, True, [Bh, Us], [pX])
                        H4 = Hs[0:64, hg * 4:hg * 4 + 4, :]
                        cx.tt(H4, H4, G[:, 0:32].rearrange("p (a b) -> p a b", b=2)[:, hg * 4:hg * 4 + 4, 0:1].to_broadcast([64, 4, 64]), ALU.mult, [Hs, G], [Hs])
                        cx.tt(H4, H4, pX[0:64, 768:1024].rearrange("p (a b) -> p a b", b=64), ALU.add, [Hs, pX], [Hs])
                    cx.copy(Hb[:], Hs[:], [Hs], [Hb], eng='pool')
                    cx.S.mark(f'tile{ti}_chunk_done')
                    back(tl, o_, o_)
                cx.S.mark('prompt_tiles_done')
                for fb in range(8):
                    pp = (PA, PB)[fb % 2]
                    cx.tr(pp[:, 0:64], Hs[0:64, 2 * fb:2 * fb + 2, :].rearrange("p a b -> p (a b)"), ident[0:64, 0:64], [Hs, ident], [pp])
                    cx.copy(HT_o[:, fb, :], pp[:, 0:64], [pp], [HT_o], eng=('dve', 'act')[fb % 2])
                cx.dma(O['p_wkv'].rearrange("(fb hp) v k -> (hp v) fb k", hp=2), HT_o[:, :, :], [HT_o], [])

            C = 64
            cx.S.mark('prompt_done')
            sh0 = alias(a_, a_[0:16, :], "rw_sh0")
            cx.dma(sh0[:, :], smp['shift0'], [], [sh0])
            for kc in range(8):
                pp = (PA, PB)[kc % 2]
                cx.tr(pp[:, 0:16], sh0[:, kc * 128:(kc + 1) * 128], ident[0:16, 0:16], [sh0, ident], [pp])
                cx.copy(prevT[:, kc, 0:16], pp[:, 0:16], [pp], [prevT])
            front(smp, 16, lambda: cx.dma(O['s_shift'], xn[48:64, :], [xn], []))
            cx.act(lw[0:C, :], lw[0:C, :], AF.Exp, [lw], [lw])
            SD = smp['scr']
            sdb = Buf("rw_sd")
            for qi, src in enumerate((r_, lw, kp, v_, kk, b_)):
                for t in range(4):
                    cx.dma(SD[qi, :, :, t, :], src[16 * t:16 * t + 16, :].rearrange("p (hh f) -> p hh f", f=128), [src], [sdb])

        with cx.scope():
            cx.S.mark('sample_front_done')
            Sst = cx.sb("rw_S", [128, 8192])
            tmpS = cx.sb("rw_tmpS", [128, 4096])
            ops_ = cx.sb("rw_ops", [128, 6, 4, 128])
            skk = cx.sb("rw_skk", [128, 128])
            osm = cx.sb("rw_osm", [128, 4, 128])
            cx.dma(Sst[:], smp['wkv0'].rearrange("b (hh hl) v k -> (b hh) (hl v k)", hl=2), [], [Sst])
            for qi in range(6):
                cx.dma(ops_[:, qi, :, :], SD[qi].rearrange("b hh t f -> (b hh) t f"), [sdb], [ops_])
            T3 = tmpS[:].rearrange("p (v k) -> p v k", v=64)

            def bk(qi, t, hl):
                return ops_[:, qi, t, hl * 64:(hl + 1) * 64].unsqueeze(1).to_broadcast([128, 64, 64])

            def bv(ap2):
                return ap2.unsqueeze(2).to_broadcast([128, 64, 64])

            for t in range(4):
                for hl in range(2):
                    S3 = Sst[:, hl * 4096:(hl + 1) * 4096].rearrange("p (v k) -> p v k", v=64)
                    hs = slice(hl * 64, hl * 64 + 64)
                    cx.tt(T3, S3, bk(4, t, hl), ALU.mult, [Sst, ops_], [tmpS])
                    cx.red(skk[:, hs], T3, ALU.add, [tmpS], [skk])
                    cx.tt(S3, S3, bk(1, t, hl), ALU.mult, [Sst, ops_], [Sst])
                    cx.tt(T3, bv(skk[:, hs]), bk(5, t, hl), ALU.mult, [skk, ops_], [tmpS])
                    cx.tt(S3, S3, T3, ALU.subtract, [Sst, tmpS], [Sst])
                    cx.tt(T3, bv(ops_[:, 3, t, hs]), bk(2, t, hl), ALU.mult, [ops_], [tmpS])
                    cx.tt(S3, S3, T3, ALU.add, [Sst, tmpS], [Sst])
                    cx.tt(T3, S3, bk(0, t, hl), ALU.mult, [Sst, ops_], [tmpS])
                    cx.red(osm[:, t, hs], T3, ALU.add, [tmpS], [osm])
            cx.S.mark('sample_rec_done')
            cx.dma(O['s_wkv'].rearrange("b (hh hl) v k -> (b hh) (hl v k)", hl=2), Sst[:], [Sst], [])
            cx.dma(SD[6].rearrange("b hh t f -> (b hh) t f"), osm[:], [osm], [sdb])
            for t in range(4):
                cx.dma(o_[16 * t:16 * t + 16, :].rearrange("p (hh f) -> p hh f", f=128), SD[6, :, :, t, :], [sdb], [o_])
            back(smp, o_, o_)


def mamba_pass(cx, cst, W, tiles, smp, O):
    ident, identb, ones, triu = cst['ident'], cst['identb'], cst['ones'], cst['triu']
    ZO, XO, BO, CO, DO = 0, 2048, 4096, 4608, 5120
    with cx.scope():
        PA = cx.ps("mb_PA", [128, 1024])
        PB = cx.ps("mb_PB", [128, 1024])
        PC = cx.ps("mb_PC", [128, 1024])
        PD = cx.ps("mb_PD", [128, 1024])
        Wout = cx.sb("mb_Wout", [128, 16, 1024], BF16)
        VT = cx.sb("mb_VT", [128, 1024])
        VS = cx.sb("mb_VS", [1, 128])
        vec = cx.sb("mb_vec", [128, 128])
        cx.dma(VT[0:1, :], W['norm_mix'][1:2, :], [], [VT])
        cx.dma(VT[32:33, :], W['mamba_norm_w'][0:1, 0:1024], [], [VT])
        cx.dma(VT[64:65, :], W['mamba_norm_w'][0:1, 1024:2048], [], [VT])
        cx.memset(VS[:], 0.0, [VS])
        cx.dma(VS[0:1, 0:32], W['mamba_dt_bias'][0:1, :], [], [VS])
        cx.dma(VS[0:1, 32:64], W['mamba_a_log'][0:1, :], [], [VS])
        cx.dma(VS[0:1, 64:96], W['mamba_d'][0:1, :], [], [VS])
        cx.mm(PA[:, 0:128], ones[0:1, :], VS[0:1, :], True, True, [ones, VS], [PA])
        cx.copy(vec[:], PA[:, 0:128], [PA], [vec])
        cx.act(vec[:, 32:64], vec[:, 32:64], AF.Exp, [vec], [vec])
        cx.ts(vec[:, 32:64], vec[:, 32:64], -1.0, ALU.mult, [vec], [vec])

        def bcast(p_, C, PP, n=1024):
            for hf in range(n // 512):
                cx.mm(PP[0:C, hf * 512:(hf + 1) * 512], ones[p_:p_ + 1, 0:C], VT[p_:p_ + 1, hf * 512:(hf + 1) * 512], True, True, [ones, VT], [PP])

        ht = cx.sb("mb_ht", [128, 1024])
        ss = cx.sb("mb_ss", [128, 8])
        y_ = cx.sb("mb_y", [128, 2048])
        xn = alias(y_, y_[:, 1024:2048], "mb_xn")
        yjunk = alias(y_, y_[:, 0:1024], "mb_yjunk")
        xs = cx.sb("mb_xs", [128, 2048], BF16)
        tmp = cx.sb("mb_tmp", [128, 512])
        ygb = alias(xs, xs[:, :], "mb_ygb")
        ygT = cx.sb("mb_ygT", [128, 16, 128], BF16)
        xnb = cx.sb("mb_xnb", [128, 1024], BF16)
        xnT = cx.sb("mb_xnT", [128, 8, 128], BF16)

        def norm_in(tl):
            C = tl['C']
            if isinstance(tl['src'], list):
                for (ap, r0, r1) in tl['src']:
                    cx.dma(ht[r0:r1, :], ap, [tl['hb']], [ht])
            else:
                cx.dma(ht[0:C, :], tl['src'], [tl['hb']], [ht])
            bcast(0, C, PD)
            rms_norm_ps(cx, C, ht, PD[0:C, :], PD, xn, yjunk, ss)
            cx.copy(xnb[0:C, :], xn[0:C, :], [xn], [xnb], eng='pool')
            PAb = PA[:].bitcast(BF16)
            for kc in range(8):
                cx.tr(PAb[:, kc * 128:kc * 128 + C], xnb[0:C, kc * 128:(kc + 1) * 128], identb[0:C, 0:C], [xnb, identb], [PA])
            cx.copy(xnT[:, :, 0:C], PAb[:, 0:1024].rearrange("p (a b) -> p a b", b=128)[:, :, 0:C], [PA], [xnT])

        def zgate(C, Win, j):
            pp = (PA, PB)[j % 2]
            for kc in range(8):
                cx.mm(pp[0:C, 0:512], xnT[:, kc, 0:C], Win[:, kc, ZO + j * 512:ZO + (j + 1) * 512], kc == 0, kc == 7, [xnT, Win], [pp])
            cx.act(tmp[0:C, :], pp[0:C, 0:512], AF.Silu, [pp], [tmp])

        def back(tl, Win, zsrc=None, zb=None):
            C = tl['C']
            for j in range(4):
                if zsrc is None:
                    zgate(C, Win, j)
                else:
                    cx.dma(tmp[0:C, :], zsrc[:, j * 512:(j + 1) * 512], [zb], [tmp])
                cx.tt(y_[0:C, j * 512:(j + 1) * 512], y_[0:C, j * 512:(j + 1) * 512], tmp[0:C, :], ALU.mult, [y_, tmp], [y_])
                cx.act(tmp[0:C, :], y_[0:C, j * 512:(j + 1) * 512], AF.Square, [y_], [tmp, ss], accum_out=ss[0:C, j:j + 1])
            cx.ts(ss[0:C, 4:8], ss[0:C, 0:4], 1.0 / 512, ALU.mult, [ss], [ss], s2=1e-5, op1=ALU.add)
            cx.act(ss[0:C, 4:8], ss[0:C, 4:8], AF.Sqrt, [ss], [ss])
            cx.op('dve', lambda e: e.reciprocal(out=ss[0:C, 4:8], in_=ss[0:C, 4:8]), [ss], [ss])
            for hf in range(2):
                bcast(32 + 32 * hf, C, PC)
                for j2 in range(2):
                    j = hf * 2 + j2
                    cx.stt(ygb[0:C, j * 512:(j + 1) * 512], y_[0:C, j * 512:(j + 1) * 512], ss[0:C, 4 + j:5 + j], PC[0:C, j2 * 512:(j2 + 1) * 512],
                           ALU.mult, ALU.mult, [y_, ss, PC], [ygb])
            PBb = PB[:].bitcast(BF16)
            for kc in range(16):
                cx.tr(PBb[:, kc * 128:kc * 128 + C], ygb[0:C, kc * 128:(kc + 1) * 128], identb[0:C, 0:C], [ygb, identb], [PB])
            cx.copy(ygT[:, :, 0:C], PBb[:, 0:2048].rearrange("p (a b) -> p a b", b=128)[:, :, 0:C], [PB], [ygT])
            for hf in range(2):
                for kc in range(16):
                    cx.mm(PA[0:C, hf * 512:(hf + 1) * 512], ygT[:, kc, 0:C], Wout[:, kc, hf * 512:(hf + 1) * 512], kc == 0, kc == 15, [ygT, Wout], [PA])
            cx.tt(ht[0:C, :], PA[0:C, :], ht[0:C, :], ALU.add, [PA, ht], [ht])
            cx.dma(tl['dst'], ht[0:C, :], [ht], [tl['hb']])

        with cx.scope():
            Win = cx.sb("mb_Win", [128, 8, 5152], BF16)
            cwv = cx.sb("mb_cwv", [128, 24, 8])
            with cx.scope():
                stg = [cx.sb(f"mb_stg{i}", [128, 1288], F32) for i in range(2)]
                cnt = 0
                for kc in range(8):
                    for cc in range(4):
                        st = stg[cnt % 2]
                        cx.dma(st[:, :], W['mamba_in_proj'][0, kc * 128:(kc + 1) * 128, cc * 1288:(cc + 1) * 1288], [], [st])
                        cx.copy(Win[:, kc, cc * 1288:(cc + 1) * 1288], st[:, :], [st], [Win], eng=('dve', 'pool', 'act')[cnt % 3])
                        cnt += 1
                for kc in range(16):
                    st = stg[cnt % 2]
                    cx.dma(st[:, 0:1024], W['mamba_out_proj'][0, kc * 128:(kc + 1) * 128, :], [], [st])
                    cx.copy(Wout[:, kc, :], st[:, 0:1024], [st], [Wout], eng=('dve', 'pool', 'act')[cnt % 3])
                    cnt += 1
                for c3 in range(3):
                    st = stg[cnt % 2]
                    cnt += 1
                    cx.memset(st[0:32, 0:1024], 0.0, [st])
                    cx.dma(st[0:4, 0:1024], W['mamba_conv_w'][0, :, c3 * 1024:(c3 + 1) * 1024], [], [st])
                    cx.dma(st[4:5, 0:1024], W['mamba_conv_b'][0:1, c3 * 1024:(c3 + 1) * 1024], [], [st])
                    for c8 in range(8):
                        cx.mm(PA[:, c8 * 32:(c8 + 1) * 32], st[0:32, c8 * 128:(c8 + 1) * 128], ident[0:32, 0:32], True, True, [st, ident], [PA])
                    cx.copy(cwv[:, c3 * 8:(c3 + 1) * 8, :], PA[:, 0:256].rearrange("p (a b) -> p a b", b=32)[:, :, 0:8], [PA], [cwv])

            cst3 = cx.sb("mb_cst3", [128, 24, 3])
            full = [cx.sb(f"mb_full{i}", [128, 132]) for i in range(2)]
            acc = [cx.sb(f"mb_acc{i}", [128, 128]) for i in range(2)]
            xbcT = cx.sb("mb_xbcT", [128, 24, 128], BF16)
            dtt = cx.sb("mb_dtt", [128, 160])
            ncv = cx.sb("mb_ncv", [64, 1024])

            def conv_part(tl, sh, Cst):
                C = tl['C']
                for ct in range(24):
                    pp = (PA, PB)[ct % 2]
                    fu = full[ct % 2]
                    ac = acc[ct % 2]
                    for kc in range(8):
                        cx.mm(pp[:, 0:C], Win[:, kc, XO + ct * 128:XO + (ct + 1) * 128], xnT[:, kc, 0:C], kc == 0, kc == 7, [Win, xnT], [pp])
                    if sh == 1:
                        cx.copy(fu[:, 0:3], cst3[:, ct, :], [cst3], [fu], eng='pool')
                        cx.copy(fu[:, 3:3 + C], pp[:, 0:C], [pp], [fu], eng='act')
                        cx.copy(cst3[:, ct, :], fu[:, C:C + 3], [fu], [cst3], eng='pool')
                        for j in range(4):
                            if j == 0:
                                cx.ts(ac[:, 0:C], fu[:, 0:C], cwv[:, ct, 0:1], ALU.mult, [fu, cwv], [ac], s2=cwv[:, ct, 4:5], op1=ALU.add)
                            else:
                                cx.stt(ac[:, 0:C], fu[:, j:j + C], cwv[:, ct, j:j + 1], ac[:, 0:C], ALU.mult, ALU.add, [fu, cwv, ac], [ac])
                    else:
                        cx.copy(fu[:, 0:48], Cst[:, ct, :, :].rearrange("p a b -> p (a b)"), [Cst], [fu], eng='pool')
                        cx.copy(fu[:, 48:48 + C], pp[:, 0:C], [pp], [fu], eng='act')
                        for j in range(4):
                            if j == 0:
                                cx.ts(ac[:, 0:C], fu[:, 0:C], cwv[:, ct, 0:1], ALU.mult, [fu, cwv], [ac], s2=cwv[:, ct, 4:5], op1=ALU.add)
                            else:
                                cx.stt(ac[:, 0:C], fu[:, 16 * j:16 * j + C], cwv[:, ct, j:j + 1], ac[:, 0:C], ALU.mult, ALU.add, [fu, cwv, ac], [ac])
                    cx.act(xbcT[:, ct, 0:C], ac[:, 0:C], AF.Silu, [ac], [xbcT])

            def dt_part(tl):
                C = tl['C']
                for kc in range(8):
                    cx.mm(PC[0:C, 0:32], xnT[:, kc, 0:C], Win[:, kc, DO:DO + 32], kc == 0, kc == 7, [xnT, Win], [PC])
                cx.tt(dtt[0:C, 0:32], PC[0:C, 0:32], vec[0:C, 0:32], ALU.add, [PC, vec], [dtt])
                cx.act(dtt[0:C, 0:32], dtt[0:C, 0:32], AF.Exp, [dtt], [dtt])
                cx.act(dtt[0:C, 0:32], dtt[0:C, 0:32], AF.Ln, [dtt], [dtt], bias=ones[0:C, 0:1])
                cx.tt(dtt[0:C, 32:64], dtt[0:C, 0:32], vec[0:C, 32:64], ALU.mult, [dtt, vec], [dtt])

            def x_tok(tl):
                C = tl['C']
                PBb = PB[:].bitcast(BF16)
                for ct in range(16):
                    cx.tr(PBb[0:C, ct * 128:(ct + 1) * 128], xbcT[:, ct, 0:C], identb[:, :], [xbcT, identb], [PB])
                cx.copy(xs[0:C, :], PBb[0:C, 0:2048], [PB], [xs])

            def newconv_rows(tl, r0, nrows, dsts):
                for c3 in range(3):
                    for c2 in range(2):
                        c6 = c3 * 2 + c2
                        pp = (PC, PD)[c6 % 2]
                        for kc in range(8):
                            cx.mm(pp[0:nrows, 0:512], xnT[:, kc, r0:r0 + nrows], Win[:, kc, XO + c6 * 512:XO + (c6 + 1) * 512], kc == 0, kc == 7, [xnT, Win], [pp])
                        cx.copy(ncv[0:nrows, c2 * 512:(c2 + 1) * 512], pp[0:nrows, 0:512], [pp], [ncv], eng=('act', 'dve')[c6 % 2])
                    for (ap, a, b) in dsts:
                        cx.dma(ap[:, c3 * 1024:(c3 + 1) * 1024], ncv[a:b, :], [ncv], [])

            with cx.scope():
                Hs = cx.sb("mb_H", [128, 2048])
                Hb = cx.sb("mb_Hb", [128, 2048], BF16)
                xdt = cx.sb("mb_xdt", [128, 2048], BF16)
                xdtd = cx.sb("mb_xdtd", [128, 2048], BF16)
                Btok = cx.sb("mb_Btok", [128, 512], BF16)
                cbm = cx.sb("mb_cbm", [128, 4, 128], BF16)
                acsT = cx.sb("mb_acsT", [32, 128])
                LT = cx.sb("mb_LT", [128, 512])
                MT = cx.sb("mb_MT", [128, 4, 128], BF16)
                cdb = cx.sb("mb_cdb", [128, 32])
                cx.memset(Hs[:], 0.0, [Hs])
                cx.memset(Hb[:], 0.0, [Hb], eng='pool')
                cx.memset(cst3[:], 0.0, [cst3])
                for ti, tl in enumerate(tiles):
                    C = tl['C']
                    cx.S.mark(f'mb_tile{ti}')
                    norm_in(tl)
                    conv_part(tl, 1, None)
                    if ti == len(tiles) - 1:
                        newconv_rows(tl, C - 4, 4, [(O['p_conv'], 1, 4)])
                    dt_part(tl)
                    x_tok(tl)
                    PBb = PB[:].bitcast(BF16)
                    for g in range(4):
                        cx.tr(PBb[0:C, g * 128:(g + 1) * 128], xbcT[:, 16 + g, 0:C], identb[:, :], [xbcT, identb], [PB])
                    cx.copy(Btok[0:C, :], PBb[0:C, 0:512], [PB], [Btok])
                    cx.mm(PC[0:C, 0:32], triu[0:C, 0:C], dtt[0:C, 32:64], True, True, [triu, dtt], [PC])
                    cx.copy(dtt[0:C, 64:96], PC[0:C, 0:32], [PC], [dtt])
                    cx.mm(PC[:, 32:64], ones[0:C, :], dtt[0:C, 32:64], True, True, [ones, dtt], [PC])
                    cx.act(cdb[:, :], PC[:, 32:64], AF.Exp, [PC], [cdb])
                    cx.tt(dtt[0:C, 128:160], PC[0:C, 32:64], dtt[0:C, 64:96], ALU.subtract, [PC, dtt], [dtt])
                    cx.act(dtt[0:C, 128:160], dtt[0:C, 128:160], AF.Exp, [dtt], [dtt])
                    cx.act(dtt[0:C, 96:128], dtt[0:C, 64:96], AF.Exp, [dtt], [dtt])
                    x3 = xs[0:C, :].rearrange("p (h q) -> p h q", q=64)
                    cx.tt(xdt[0:C, :].rearrange("p (h q) -> p h q", q=64), x3, dtt[0:C, 0:32].unsqueeze(2).to_broadcast([C, 32, 64]), ALU.mult, [xs, dtt], [xdt])
                    cx.tt(xdtd[0:C, :].rearrange("p (h q) -> p h q", q=64), xdt[0:C, :].rearrange("p (h q) -> p h q", q=64),
                          dtt[0:C, 128:160].unsqueeze(2).to_broadcast([C, 32, 64]), ALU.mult, [xdt, dtt], [xdtd], eng='pool')
                    cx.mm(PC[0:32, 64:64 + C], dtt[0:C, 64:96], ident[0:C, 0:C], True, True, [dtt, ident], [PC])
                    cx.copy(acsT[:, 0:C], PC[0:32, 64:64 + C], [PC], [acsT])
                    for g in range(4):
                        cx.mm(PD[0:C, g * 128:g * 128 + C], xbcT[:, 16 + g, 0:C], xbcT[:, 20 + g, 0:C], True, True, [xbcT], [PD])
                    cx.tt(cbm[0:C, :, 0:C], PD[0:C, 0:512].rearrange("p (g c) -> p g c", g=4)[:, :, 0:C], triu[0:C, 0:C].unsqueeze(1).to_broadcast([C, 4, C]), ALU.mult, [PD, triu], [cbm])
                    for g in range(4):
                        pY = (PA, PB)[g % 2]
                        for h4 in range(2):
                            h0 = g * 8 + h4 * 4
                            for q in range(4):
                                h = h0 + q
                                cx.mm(PC[0:C, 512 + q * 128:512 + q * 128 + C], ident[0:32, h:h + 1].to_broadcast([32, C]), acsT[:, 0:C], True, True, [ident, acsT], [PC])
                            L3 = LT[0:C, :].rearrange("p (q c) -> p q c", q=4)[:, :, 0:C]
                            cx.tt(L3, PC[0:C, 512:1024].rearrange("p (q c) -> p q c", q=4)[:, :, 0:C],
                                  dtt[0:C, 64 + h0:64 + h0 + 4].unsqueeze(2).to_broadcast([C, 4, C]), ALU.subtract, [PC, dtt], [LT])
                            cx.ts(L3, L3, 0.0, ALU.min, [LT], [LT])
                            cx.act(L3, L3, AF.Exp, [LT], [LT])
                            cx.tt(MT[0:C, :, 0:C], L3, cbm[0:C, g:g + 1, 0:C].to_broadcast([C, 4, C]), ALU.mult, [LT, cbm], [MT])
                            for q in range(4):
                                h = h0 + q
                                cx.mm(pY[0:C, (h4 * 4 + q) * 64:(h4 * 4 + q + 1) * 64], MT[0:C, q, 0:C], xdt[0:C, h * 64:(h + 1) * 64], True, True, [MT, xdt], [pY])
                        cx.mm(pY[0:C, 512:1024], xbcT[:, 20 + g, 0:C], Hb[:, g * 512:(g + 1) * 512], True, True, [xbcT, Hb], [pY])
                        cx.tt(tmp[0:C, :].rearrange("p (h q) -> p h q", q=64), pY[0:C, 512:1024].rearrange("p (h q) -> p h q", q=64),
                              dtt[0:C, 96 + g * 8:96 + g * 8 + 8].unsqueeze(2).to_broadcast([C, 8, 64]), ALU.mult, [pY, dtt], [tmp])
                        cx.tt(y_[0:C, g * 512:(g + 1) * 512], pY[0:C, 0:512], tmp[0:C, :], ALU.add, [pY, tmp], [y_])
                    for g in range(4):
                        pp = (PC, PD)[g % 2]
                        cx.mm(pp[:, 0:512], Btok[0:C, g * 128:(g + 1) * 128], xdtd[0:C, g * 512:(g + 1) * 512], True, True, [Btok, xdtd], [pp])
                        H3 = Hs[:, g * 512:(g + 1) * 512].rearrange("p (h q) -> p h q", q=64)
                        cx.tt(H3, H3, cdb[:, g * 8:(g + 1) * 8].unsqueeze(2).to_broadcast([128, 8, 64]), ALU.mult, [Hs, cdb], [Hs])
                        cx.tt(Hs[:, g * 512:(g + 1) * 512], Hs[:, g * 512:(g + 1) * 512], pp[:, 0:512], ALU.add, [Hs, pp], [Hs])
                    cx.copy(Hb[:], Hs[:], [Hs], [Hb], eng='pool')
                    for g in range(4):
                        cx.tt(tmp[0:C, :].rearrange("p (h q) -> p h q", q=64), xs[0:C, g * 512:(g + 1) * 512].rearrange("p (h q) -> p h q", q=64),
                              vec[0:C, 64 + g * 8:64 + g * 8 + 8].unsqueeze(2).to_broadcast([C, 8, 64]), ALU.mult, [xs, vec], [tmp])
                        cx.tt(y_[0:C, g * 512:(g + 1) * 512], y_[0:C, g * 512:(g + 1) * 512], tmp[0:C, :], ALU.add, [y_, tmp], [y_])
                    back(tl, Win)
                HTo = alias(y_, y_[:, :], "mb_HTo")
                for rnd in range(8):
                    for q in range(2):
                        c16 = rnd * 2 + q
                        pp = (PA, PB)[q]
                        cx.tr(pp[:, 0:128], Hs[:, c16 * 128:(c16 + 1) * 128], ident[:, :], [Hs, ident], [pp])
                        cx.copy(HTo[:, c16 * 128:(c16 + 1) * 128], pp[:, 0:128], [pp], [HTo], eng=('dve', 'act')[q])
                cx.dma(O['p_ssm'].rearrange("(c hl) p n -> (hl p) c n", hl=2), HTo[:, :].rearrange("q (c n) -> q c n", n=128), [HTo], [])

            cx.S.mark('mb_sample_front')
            C = 64
            Cst = cx.sb("mb_Cst", [128, 24, 3, 16])
            for c3 in range(3):
                for j in range(3):
                    cx.dma(ncv[16 * j:16 * j + 16, :], smp['conv0'][:, j, c3 * 1024:(c3 + 1) * 1024], [], [ncv])
                for c8 in range(8):
                    ct = c3 * 8 + c8
                    pp = (PA, PB)[ct % 2]
                    cx.tr(pp[:, 0:48], ncv[0:48, c8 * 128:(c8 + 1) * 128], ident[0:48, 0:48], [ncv, ident], [pp])
                    cx.copy(Cst[:, ct, :, :].rearrange("p a b -> p (a b)"), pp[:, 0:48], [pp], [Cst], eng=('dve', 'act')[ct % 2])
            norm_in(smp)
            conv_part(smp, 16, Cst)
            newconv_rows(smp, 0, 64, [(smp['conv_out'][:, t - 1, :], 16 * t, 16 * t + 16) for t in range(1, 4)])
            dt_part(smp)
            x_tok(smp)
            SD = smp['scr']
            sdb = Buf("mb_sd")
            bct = alias(ygT, ygT[:].rearrange("p a b -> p (a b)")[:, 0:1024], "mb_bct")
            PBb = PB[:].bitcast(BF16)
            for g in range(8):
                cx.tr(PBb[0:C, g * 128:(g + 1) * 128], xbcT[:, 16 + g, 0:C], identb[:, :], [xbcT, identb], [PB])
            cx.copy(bct[0:C, 0:1024], PBb[0:C, 0:1024], [PB], [bct])
            cx.act(dtt[0:C, 64:96], dtt[0:C, 32:64], AF.Exp, [dtt], [dtt])
            cx.tt(y_[0:C, :].rearrange("p (h q) -> p h q", q=64), xs[0:C, :].rearrange("p (h q) -> p h q", q=64),
                  dtt[0:C, 0:32].unsqueeze(2).to_broadcast([C, 32, 64]), ALU.mult, [xs, dtt], [y_])
            cx.dma(SD['dtx'], y_[0:C, :], [y_], [sdb])
            cx.dma(SD['da'], dtt[0:C, 64:96], [dtt], [sdb])
            for z, nm in enumerate(('b', 'c')):
                cx.copy(tmp[0:C, :], bct[0:C, z * 512:(z + 1) * 512], [bct], [tmp])
                cx.dma(SD[nm], tmp[0:C, :], [tmp], [sdb])
            for j in range(4):
                zgate(C, Win, j)
                cx.dma(SD['z'][:, j * 512:(j + 1) * 512], tmp[0:C, :], [tmp], [sdb])

        cx.S.mark('mb_sample_rec')
        with cx.scope():
            Sst = cx.sb("mb_S", [128, 8192])
            tmpS = cx.sb("mb_tmpS", [128, 8192])
            dtx = cx.sb("mb_dtx", [128, 4, 64])
            dA = cx.sb("mb_dA", [128, 4])
            bcs = cx.sb("mb_bcs", [16, 4, 256])
            cx.memset(bcs[:], 0.0, [bcs])
            BC = cx.sb("mb_BC", [128, 4, 256])
            ysm = cx.sb("mb_ysm", [128, 4, 64])
            E = cx.sb("mb_E", [16, 128])
            ie = cx.sb("mb_ie", [16, 128], I32)
            e2 = cx.sb("mb_e2", [16, 128])
            cx.op('pool', lambda e: e.iota(ie[:], pattern=[[1, 128]], base=0, channel_multiplier=-8), [], [ie])
            cx.op('dve', lambda e: e.tensor_single_scalar(out=E[:], in_=ie[:], scalar=0, op=ALU.is_ge), [ie], [E])
            cx.op('dve', lambda e: e.tensor_single_scalar(out=e2[:], in_=ie[:], scalar=8, op=ALU.is_lt), [ie], [e2])
            cx.tt(E[:], E[:], e2[:], ALU.mult, [E, e2], [E])
            S3 = Sst[:].rearrange("p (q n) -> p q n", n=128)
            T3 = tmpS[:].rearrange("p (q n) -> p q n", n=128)
            for qb in range(4):
                b0 = qb * 4
                cx.dma(Sst[:], smp['ssm0'][b0:b0 + 4].rearrange("b h p n -> (b h) (p n)"), [], [Sst])
                cx.dma(dtx[:], SD['dtx'].rearrange("(t b) (h p) -> b h t p", b=16, p=64)[b0:b0 + 4].rearrange("b h t p -> (b h) t p"), [sdb], [dtx])
                cx.dma(dA[:], SD['da'].rearrange("(t b) h -> b h t", b=16)[b0:b0 + 4].rearrange("b h t -> (b h) t"), [sdb], [dA], allow_slow_non_contiguous=True)
                for z, nm in enumerate(('b', 'c')):
                    cx.dma(bcs[:, :, z * 128:(z + 1) * 128],
                           SD[nm].rearrange("(t b) (g n) -> b g t n", b=16, g=4)[b0:b0 + 4].rearrange("b g t n -> (b g) t n"), [sdb], [bcs])
                for t2 in range(2):
                    cx.mm(PA[:, t2 * 512:(t2 + 1) * 512], E[:, :], bcs[:, 2 * t2:2 * t2 + 2, :].rearrange("p a b -> p (a b)"), True, True, [E, bcs], [PA])
                cx.copy(BC[:].rearrange("p a b -> p (a b)"), PA[:, :], [PA], [BC])
                for t in range(4):
                    cx.ts(Sst[:], Sst[:], dA[:, t:t + 1], ALU.mult, [Sst, dA], [Sst])
                    cx.tt(T3, dtx[:, t, :].unsqueeze(2).to_broadcast([128, 64, 128]), BC[:, t, 0:128].unsqueeze(1).to_broadcast([128, 64, 128]), ALU.mult, [dtx, BC], [tmpS])
                    cx.tt(Sst[:], Sst[:], tmpS[:], ALU.add, [Sst, tmpS], [Sst])
                    cx.tt(T3, S3, BC[:, t, 128:256].unsqueeze(1).to_broadcast([128, 64, 128]), ALU.mult, [Sst, BC], [tmpS])
                    cx.red(ysm[:, t, :], T3, ALU.add, [tmpS], [ysm])
                cx.dma(O['s_ssm'][b0:b0 + 4].rearrange("b h p n -> (b h) (p n)"), Sst[:], [Sst], [])
                cx.dma(SD['y'].rearrange("(t b) (h p) -> b h t p", b=16, p=64)[b0:b0 + 4].rearrange("b h t p -> (b h) t p"), ysm[:], [ysm], [sdb])
            cx.dma(y_[0:64, :], SD['y'], [sdb], [y_])
            C = 64
            for g in range(4):
                cx.tt(tmp[0:C, :].rearrange("p (h q) -> p h q", q=64), xs[0:C, g * 512:(g + 1) * 512].rearrange("p (h q) -> p h q", q=64),
                      vec[0:C, 64 + g * 8:64 + g * 8 + 8].unsqueeze(2).to_broadcast([C, 8, 64]), ALU.mult, [xs, vec], [tmp])
                cx.tt(y_[0:C, g * 512:(g + 1) * 512], y_[0:C, g * 512:(g + 1) * 512], tmp[0:C, :], ALU.add, [y_, tmp], [y_])
            back(smp, None, SD['z'], sdb)


def convert_tables(cx, W):
    with cx.scope():
        stg = [cx.sb(f"cv_stg{i}", [128, 8192], F32) for i in range(4)]
        ob = [cx.sb(f"cv_ob{i}", [128, 8192], BF16) for i in range(4)]
        n = 0
        for l in range(2):
            for z, nm in enumerate(('peer_u', 'peer_v')):
                for c in range(16):
                    st, o = stg[n % 4], ob[n % 4]
                    cx.dma(st[:], W[nm][l, c * 1024:(c + 1) * 1024, :].rearrange("(p r) d -> p (r d)", r=8), [], [st])
                    cx.copy(o[:], st[:], [st], [o], eng=('act', 'dve', 'act', 'pool')[n % 4])
                    r0 = (l * 16 + c) * 1024
                    cx.dma(W['tab_uv'][r0:r0 + 1024, z * 1024:(z + 1) * 1024].rearrange("(p r) d -> p r d", r=8),
                           o[:].rearrange("p (r d) -> p r d", r=8), [o], [W['tabbuf']])
                    n += 1


W_SHAPES = {
    'meta_tokens': [16, 1024], 'norm_mix': [2, 1024], 'norm_ffn': [2, 1024], 'norm_final': [1024],
    'rwkv_mix': [1, 6, 1024], 'rwkv_w_rkv': [1, 3, 1024, 1024], 'rwkv_w0': [1, 1024], 'rwkv_w1': [1, 1024, 64],
    'rwkv_w2': [1, 64, 1024], 'rwkv_a0': [1, 1024], 'rwkv_a1': [1, 1024, 64], 'rwkv_a2': [1, 64, 1024],
    'rwkv_g1': [1, 1024, 160], 'rwkv_g2': [1, 160, 1024], 'rwkv_k_k': [1, 1024], 'rwkv_k_a': [1, 1024],
    'rwkv_r_k': [1, 16, 64], 'rwkv_ln_w': [1, 1024], 'rwkv_ln_b': [1, 1024], 'rwkv_w_o': [1, 1024, 1024],
    'mamba_in_proj': [1, 1024, 5152], 'mamba_conv_w': [1, 4, 3072], 'mamba_conv_b': [1, 3072], 'mamba_dt_bias': [1, 32],
    'mamba_a_log': [1, 32], 'mamba_d': [1, 32], 'mamba_norm_w': [1, 2048], 'mamba_out_proj': [1, 2048, 1024],
    'peer_w_q': [2, 1024, 2048], 'peer_sub_keys': [2, 2, 8, 128, 128], 'peer_u': [2, 16384, 1024], 'peer_v': [2, 16384, 1024],
}
IN_SHAPES = {
    'xp': [2048, 1024], 'xs': [16, 4, 1024], 'sh0': [16, 1024], 'wkv0': [16, 16, 64, 64],
    'conv0': [16, 3, 3072], 'ssm0': [16, 32, 64, 128],
}
OUT_SHAPES = {
    'y_p': [2048, 1024], 'y_s': [16, 4, 1024], 'p_shift': [1, 1024], 'p_wkv': [16, 64, 64], 'p_conv': [3, 3072],
    'p_ssm': [32, 64, 128], 's_shift': [16, 1024], 's_wkv': [16, 16, 64, 64], 's_conv': [16, 3, 3072], 's_ssm': [16, 32, 64, 128],
}
NPT = 16


def build_program(npt=NPT, debug=False):
    nc = bass.Bass("TRN2", target_bir_lowering=False)
    W = {n: nc.dram_tensor(n, s, F32, kind="ExternalInput").ap() for n, s in W_SHAPES.items()}
    I = {n: nc.dram_tensor(n, s, F32, kind="ExternalInput").ap() for n, s in IN_SHAPES.items()}
    Od = {n: nc.dram_tensor(n, s, F32, kind="ExternalOutput").ap() for n, s in OUT_SHAPES.items()}
    hscr = nc.dram_tensor("hscr", [npt + 2, 128, 1024], F32, kind="Internal").ap()
    rw_scr = nc.dram_tensor("rw_scr", [7, 16, 8, 4, 128], F32, kind="Internal").ap()
    SD = {'dtx': nc.dram_tensor("sd_dtx", [64, 2048], F32, kind="Internal").ap(), 'da': nc.dram_tensor("sd_da", [64, 32], F32, kind="Internal").ap(),
          'b': nc.dram_tensor("sd_b", [64, 512], F32, kind="Internal").ap(), 'c': nc.dram_tensor("sd_c", [64, 512], F32, kind="Internal").ap(),
          'z': nc.dram_tensor("sd_z", [64, 2048], F32, kind="Internal").ap(), 'y': nc.dram_tensor("sd_y", [64, 2048], F32, kind="Internal").ap()}
    dbg = nc.dram_tensor("dbg_h", [3, npt + 2, 128, 1024], F32, kind="ExternalOutput").ap() if debug else None
    W['tab_uv'] = nc.dram_tensor("tab_uv", [2 * 16384, 2048], BF16, kind="Internal").ap()
    W['tabbuf'] = Buf("tabbuf")
    with ExitStack() as es:
        cx = Ctx(nc, es)
        cst = make_consts(cx)
        convert_tables(cx, W)
        hb = [Buf(f"h{i}") for i in range(npt + 2)]

        def snap(k):
            if debug:
                for i in range(npt + 2):
                    cx.dma(dbg[k, i], hscr[i], [hb[i]], [])

        Cs = [16] + [128] * npt
        first = [dict(C=16, src=W['meta_tokens'][:, :], dst=hscr[0, 0:16, :], hb=hb[0])]
        for i in range(npt):
            first.append(dict(C=128, src=I['xp'][i * 128:(i + 1) * 128, :], dst=hscr[i + 1, :, :], hb=hb[i + 1]))
        smp = dict(C=64, src=[(I['xs'][:, t, :], 16 * t, 16 * t + 16) for t in range(4)], dst=hscr[npt + 1, 0:64, :], hb=hb[npt + 1],
                   shift0=I['sh0'], wkv0=I['wkv0'], scr=rw_scr)
        rwkv_pass(cx, cst, W, first, smp, dict(p_shift=Od['p_shift'], s_shift=Od['s_shift'], p_wkv=Od['p_wkv'], s_wkv=Od['s_wkv']))
        inpl = [dict(C=Cs[i], src=hscr[i, 0:Cs[i], :], dst=hscr[i, 0:Cs[i], :], hb=hb[i]) for i in range(npt + 1)]
        inpl.append(dict(C=64, src=hscr[npt + 1, 0:64, :], dst=hscr[npt + 1, 0:64, :], hb=hb[npt + 1]))

        dbgp = nc.dram_tensor("dbg_p", [10, 128, 128], F32, kind="ExternalOutput").ap() if debug else None

        def peer(layer, gfin):
            with cx.scope():
                P = {'a': cx.ps("pa", [128, 512]), 'b': cx.ps("pb", [128, 512]), 'x0': cx.ps("px0", [128, 1024]),
                     'x1': cx.ps("px1", [128, 1024]), 'y': cx.ps("py", [128, 1024])}
                tl = [dict(t) for t in inpl]
                if gfin:
                    tl[0]['fin'] = []
                    for i in range(npt):
                        tl[i + 1]['fin'] = [(Od['y_p'][i * 128:(i + 1) * 128, :], 0, 128)]
                    tl[npt + 1]['fin'] = [(Od['y_s'][:, t, :], 16 * t, 16 * t + 16) for t in range(4)]
                peer_pass(cx, cst, W, layer, tl, P, gfin=(True if gfin else None), dbgp=(dbgp if layer == 0 else None))

        snap(0)
        peer(0, False)
        snap(1)
        smp2 = dict(C=64, src=hscr[npt + 1, 0:64, :], dst=hscr[npt + 1, 0:64, :], hb=hb[npt + 1],
                    conv0=I['conv0'], ssm0=I['ssm0'], scr=SD, conv_out=Od['s_conv'])
        mamba_pass(cx, cst, W, inpl[:npt + 1], smp2, dict(p_conv=Od['p_conv'], p_ssm=Od['p_ssm'], s_ssm=Od['s_ssm']))
        snap(2)
        peer(1, True)
        cx.S.finish()
        block = es.enter_context(nc.Block())
        cx.S.emit(block)
    return nc


_NC_CACHE = {}


def kernel(**inputs):
    inputs = {k: np.ascontiguousarray(np.asarray(v), dtype=np.float32) for k, v in inputs.items()}
    if 'nc' not in _NC_CACHE:
        _NC_CACHE['nc'] = build_program()
    nc = _NC_CACHE['nc']
    in_maps = []
    for c in range(NCORES):
        m = {n: inputs[n] for n in W_SHAPES}
        m['xp'] = inputs['x_prompt'][c]
        m['xs'] = inputs['x_sample'][16 * c:16 * c + 16]
        m['sh0'] = inputs['state_rwkv_shift'][0, 16 * c:16 * c + 16]
        m['wkv0'] = inputs['state_rwkv_wkv'][0, 16 * c:16 * c + 16]
        m['conv0'] = inputs['state_mamba_conv'][0, 16 * c:16 * c + 16]
        m['ssm0'] = inputs['state_mamba_ssm'][0, 16 * c:16 * c + 16]
        in_maps.append(m)
    res = run_bass_kernel_spmd(nc, in_maps, core_ids=list(range(NCORES)))
    R = res.results
    cat = lambda k: np.concatenate([R[c][k] for c in range(NCORES)], axis=0)
    stk = lambda k: np.stack([R[c][k] for c in range(NCORES)], axis=0)
    y_p = stk('y_p')
    y_s = cat('y_s')
    p_shift = cat('p_shift')[None]
    p_wkv = stk('p_wkv')[None]
    p_conv = stk('p_conv')[None]
    p_ssm = stk('p_ssm')[None]
    s_shift = cat('s_shift')[None]
    s_wkv = cat('s_wkv')[None]
    s_conv = cat('s_conv')[None]
    s_ssm = cat('s_ssm')[None]
    return (y_p, y_s, p_shift, p_wkv, p_conv, p_ssm, s_shift, s_wkv, s_conv, s_ssm)
```
